# Optimizing a Trainium2 kernel written in Bass

```python
import jax, jax.numpy as jnp
from jax import lax
import numpy as np

D_MODEL = 4096
BATCH = 1
SEQ = 16384
DEPTH = 1

CHUNK = 64
N_META = 16
Q_BLOCK = 128
MAX_TOPK = 256
NORM_EPS = 1e-6

A_HEAD_DIM = 64
A_WIDTH = D_MODEL // 2
A_HEADS = A_WIDTH // A_HEAD_DIM
A_DECAY_LORA = 96
A_ICLR_LORA = 96
A_GATE_LORA = 256
A_GN_EPS = 64e-5
A_SIZES = (A_WIDTH, A_WIDTH, A_WIDTH, A_DECAY_LORA, A_ICLR_LORA, A_GATE_LORA)
A_COLS = sum(A_SIZES)

B_HEAD_DIM = 128
B_WIDTH = D_MODEL // 2
B_HEADS = B_WIDTH // B_HEAD_DIM
B_KV_RANK = 512
IDX_HEADS = 32
IDX_DIM = 64
IDX_EPS = 1e-6
B_SIZES = (B_WIDTH, B_KV_RANK, IDX_HEADS * IDX_DIM, IDX_DIM, IDX_HEADS)
B_COLS = sum(B_SIZES)
IN_COLS = A_COLS + B_COLS

D_FF = 11008
CONV_W = 3

kernel_name = "hybrid_rwkv7_dsa_convffn_block"


def rmsnorm(x, w, eps=NORM_EPS):
    xf = x.astype(jnp.float32)
    y = xf * lax.rsqrt(jnp.mean(xf * xf, axis=-1, keepdims=True) + eps)
    return (y * w.astype(jnp.float32)).astype(x.dtype)


def layernorm(x, w, b, eps):
    xf = x.astype(jnp.float32)
    mu = jnp.mean(xf, axis=-1, keepdims=True)
    var = jnp.mean(jnp.square(xf - mu), axis=-1, keepdims=True)
    y = (xf - mu) * lax.rsqrt(var + eps)
    return (y * w.astype(jnp.float32) + b.astype(jnp.float32)).astype(x.dtype)


def split_cols(z, sizes):
    return jnp.split(z, [int(c) for c in np.cumsum(sizes)[:-1]], axis=-1)


def chunk_ids(n):
    p = jnp.arange(n)
    return jnp.where(p < N_META, 0, 1 + (p - N_META) // CHUNK)


def causal_dwconv(x, w, b):
    k_w = w.shape[0]
    L = x.shape[1]
    xp = jnp.pad(x, ((0, 0), (k_w - 1, 0), (0, 0)))
    return sum(xp[:, i:i + L] * w[i] for i in range(k_w)) + b


def rwkv7_mix(za, mu, w0, w2, a0, a2, g2, k_k, k_a, r_k, ln_w, ln_b):
    f32 = jnp.float32
    B, L, _ = za.shape
    prev = jnp.pad(za, ((0, 0), (1, 0), (0, 0)))[:, :-1]
    zs = za + (prev - za) * mu
    r, k, v, w_lo, a_lo, g_lo = split_cols(zs, A_SIZES)
    w = -jax.nn.softplus(-(w0 + jnp.tanh(w_lo) @ w2)) - 0.5
    decay = jnp.exp(-jnp.exp(w.astype(f32)))
    a = jax.nn.sigmoid(a0 + a_lo @ a2)
    g = jax.nn.sigmoid(g_lo) @ g2

    def heads(t):
        return t.reshape(B, L, A_HEADS, A_HEAD_DIM)

    kk = heads(k * k_k).astype(f32)
    kk = kk * lax.rsqrt(jnp.sum(kk * kk, axis=-1, keepdims=True) + 1e-12)
    k = k * (1.0 + (a - 1.0) * k_a)
    rh, kh, vh, ah = heads(r).astype(f32), heads(k).astype(f32), heads(v).astype(f32), heads(a).astype(f32)
    bb = kk * ah

    def step(S, inp):
        r_t, d_t, k_t, v_t, kk_t, b_t = inp
        S = (S * d_t[:, :, None, :]
             - jnp.einsum('bhvk,bhk->bhv', S, kk_t)[..., None] * b_t[:, :, None, :]
             + v_t[..., None] * k_t[:, :, None, :])
        return S, jnp.einsum('bhvk,bhk->bhv', S, r_t)

    xs = tuple(jnp.moveaxis(t, 1, 0) for t in (rh, heads(decay), kh, vh, kk, bb))
    S0 = jnp.zeros((B, A_HEADS, A_HEAD_DIM, A_HEAD_DIM), f32)
    _, y = lax.scan(step, S0, xs)
    y = jnp.moveaxis(y, 0, 1)
    mu_y = jnp.mean(y, axis=-1, keepdims=True)
    var_y = jnp.mean(jnp.square(y - mu_y), axis=-1, keepdims=True)
    y = ((y - mu_y) * lax.rsqrt(var_y + A_GN_EPS)).reshape(B, L, A_WIDTH)
    y = y * ln_w.astype(f32) + ln_b.astype(f32)
    bonus = jnp.sum(rh * kh * r_k.astype(f32), axis=-1, keepdims=True) * vh
    y = (y + bonus.reshape(B, L, A_WIDTH)) * g.astype(f32)
    return y.astype(za.dtype)


def dsa_mix(zb, kv_norm_w, w_uk, w_uv, idx_ln_w, idx_ln_b, topk):
    f32 = jnp.float32
    B, L, _ = zb.shape
    q, c_kv, q_idx, k_idx, w_idx = split_cols(zb, B_SIZES)
    c_kv = rmsnorm(c_kv, kv_norm_w)
    k_idx = layernorm(k_idx, idx_ln_w, idx_ln_b, IDX_EPS)
    w_idx = w_idx * (IDX_HEADS ** -0.5 * IDX_DIM ** -0.5)
    n_blk = -(-L // Q_BLOCK)
    Lp = n_blk * Q_BLOCK

    def blocks(t):
        pad = [(0, 0), (0, Lp - L)] + [(0, 0)] * (t.ndim - 2)
        t = jnp.pad(t, pad)
        return jnp.moveaxis(t.reshape((B, n_blk, Q_BLOCK) + t.shape[2:]), 1, 0)

    q_blk = blocks(q.reshape(B, L, B_HEADS, B_HEAD_DIM))
    qi_blk = blocks(q_idx.reshape(B, L, IDX_HEADS, IDX_DIM))
    wi_blk = blocks(w_idx)
    cid = chunk_ids(Lp)
    cid_q = cid.reshape(n_blk, Q_BLOCK)
    cid_k = cid[:L]
    scale = B_HEAD_DIM ** -0.5
    k_idx_f = k_idx.astype(f32)

    def one_block(args):
        qb, qib, wib, cq = args
        rel = jax.nn.relu(jnp.einsum('bqhd,bsd->bqhs', qib.astype(f32), k_idx_f))
        score = jnp.einsum('bqhs,bqh->bqs', rel, wib.astype(f32))
        admissible = cid_k[None, None, :] <= cq[None, :, None]
        score = jnp.where(admissible, score, -jnp.inf)
        sel_score, sel_idx = lax.top_k(score, topk)
        valid = sel_score > -jnp.inf
        c_sel = jax.vmap(lambda c, i: c[i])(c_kv, sel_idx)
        q_lat = jnp.einsum('bqhd,hrd->bqhr', qb, w_uk)
        logits = jnp.einsum('bqhr,bqkr->bqhk', q_lat, c_sel).astype(f32) * scale
        logits = jnp.where(valid[:, :, None, :], logits, -jnp.inf)
        p = jax.nn.softmax(logits, axis=-1).astype(c_sel.dtype)
        o_lat = jnp.einsum('bqhk,bqkr->bqhr', p, c_sel)
        return jnp.einsum('bqhr,hrd->bqhd', o_lat, w_uv)

    o = lax.map(one_block, (q_blk, qi_blk, wi_blk, cid_q))
    o = jnp.moveaxis(o, 0, 1).reshape(B, Lp, B_WIDTH)[:, :L]
    return o


def setup_inputs(seed: int = 0) -> dict:
    key = jax.random.key(seed)
    ks = iter(jax.random.split(key, 40))

    def nrm(shape, scale=1.0):
        return jax.random.normal(next(ks), shape, jnp.float32) * scale

    def gain(shape):
        return 1.0 + nrm(shape, 0.05)

    ramp = (jnp.arange(A_WIDTH, dtype=jnp.float32) / (A_WIDTH - 1)) ** 1.5
    conv_base = jnp.array([0.2, 0.3, 1.0], jnp.float32)[None, :, None]
    return {
        "x": nrm((BATCH, SEQ, D_MODEL)),
        "meta_tokens": nrm((N_META, D_MODEL)),
        "norm_mix_w": gain((DEPTH, D_MODEL)),
        "w_in": nrm((DEPTH, D_MODEL, IN_COLS), D_MODEL ** -0.5),
        "mu_shift": jax.random.uniform(next(ks), (DEPTH, A_COLS), jnp.float32),
        "rwkv_w0": -6.0 + 5.0 * ramp[None, :] + nrm((DEPTH, A_WIDTH), 0.1),
        "rwkv_w2": nrm((DEPTH, A_DECAY_LORA, A_WIDTH), 0.1 * A_DECAY_LORA ** -0.5),
        "rwkv_a0": nrm((DEPTH, A_WIDTH), 0.1),
        "rwkv_a2": nrm((DEPTH, A_ICLR_LORA, A_WIDTH), 0.1 * A_ICLR_LORA ** -0.5),
        "rwkv_g2": nrm((DEPTH, A_GATE_LORA, A_WIDTH), A_GATE_LORA ** -0.5),
        "rwkv_k_k": 0.85 + nrm((DEPTH, A_WIDTH), 0.05),
        "rwkv_k_a": 1.0 + nrm((DEPTH, A_WIDTH), 0.05),
        "rwkv_r_k": nrm((DEPTH, A_HEADS, A_HEAD_DIM), 0.1),
        "rwkv_ln_w": gain((DEPTH, A_WIDTH)),
        "rwkv_ln_b": nrm((DEPTH, A_WIDTH), 0.02),
        "kv_norm_w": gain((DEPTH, B_KV_RANK)),
        "w_uk": nrm((DEPTH, B_HEADS, B_KV_RANK, B_HEAD_DIM), B_KV_RANK ** -0.5),
        "w_uv": nrm((DEPTH, B_HEADS, B_KV_RANK, B_HEAD_DIM), B_KV_RANK ** -0.5),
        "idx_ln_w": gain((DEPTH, IDX_DIM)),
        "idx_ln_b": nrm((DEPTH, IDX_DIM), 0.02),
        "w_proj_a": nrm((DEPTH, A_WIDTH, D_MODEL), A_WIDTH ** -0.5),
        "w_proj_b": nrm((DEPTH, B_WIDTH, D_MODEL), B_WIDTH ** -0.5),
        "w_gate": nrm((DEPTH, D_MODEL, 2 * D_MODEL), D_MODEL ** -0.5),
        "w_out": nrm((DEPTH, D_MODEL, D_MODEL), D_MODEL ** -0.5),
        "norm_ffn_w": gain((DEPTH, D_MODEL)),
        "w_ffn_in": nrm((DEPTH, D_MODEL, 2 * D_FF), D_MODEL ** -0.5),
        "ffn_conv_w": conv_base + nrm((DEPTH, CONV_W, 2 * D_FF), 0.1),
        "ffn_conv_b": nrm((DEPTH, 2 * D_FF), 0.02),
        "w_ffn_out": nrm((DEPTH, D_FF, D_MODEL), D_FF ** -0.5),
        "norm_final_w": gain((D_MODEL,)),
    }


def reference(x, meta_tokens, norm_mix_w, w_in, mu_shift, rwkv_w0, rwkv_w2, rwkv_a0, rwkv_a2,
              rwkv_g2, rwkv_k_k, rwkv_k_a, rwkv_r_k, rwkv_ln_w, rwkv_ln_b, kv_norm_w, w_uk, w_uv,
              idx_ln_w, idx_ln_b, w_proj_a, w_proj_b, w_gate, w_out, norm_ffn_w, w_ffn_in,
              ffn_conv_w, ffn_conv_b, w_ffn_out, norm_final_w):
    B = x.shape[0]
    topk = min(MAX_TOPK, SEQ // 4)
    meta = jnp.broadcast_to(meta_tokens[None].astype(x.dtype), (B, N_META, D_MODEL))
    h = jnp.concatenate([meta, x], axis=1)
    for l in range(DEPTH):
        u = rmsnorm(h, norm_mix_w[l])
        z = u @ w_in[l]
        za, zb = z[..., :A_COLS], z[..., A_COLS:]
        ya = rwkv7_mix(za, mu_shift[l], rwkv_w0[l], rwkv_w2[l], rwkv_a0[l], rwkv_a2[l], rwkv_g2[l],
                       rwkv_k_k[l], rwkv_k_a[l], rwkv_r_k[l], rwkv_ln_w[l], rwkv_ln_b[l])
        yb = dsa_mix(zb, kv_norm_w[l], w_uk[l], w_uv[l], idx_ln_w[l], idx_ln_b[l], topk)
        gates = jax.nn.sigmoid(u @ w_gate[l])
        g_a, g_b = gates[..., :D_MODEL], gates[..., D_MODEL:]
        merged = g_a * (ya @ w_proj_a[l]) + g_b * (yb @ w_proj_b[l])
        h = h + merged @ w_out[l]
        u = rmsnorm(h, norm_ffn_w[l])
        zf = causal_dwconv(u @ w_ffn_in[l], ffn_conv_w[l], ffn_conv_b[l])
        zg, zu = zf[..., :D_FF], zf[..., D_FF:]
        h = h + (jax.nn.silu(zg) * zu) @ w_ffn_out[l]
    y = rmsnorm(h, norm_final_w)
    return y[:, N_META:]
```

```python
from contextlib import ExitStack

import numpy as np
import concourse.bass as bass
import concourse.mybir as mybir
from concourse.bass_utils import run_bass_kernel_spmd

F32 = mybir.dt.float32
BF16 = mybir.dt.bfloat16
AF = mybir.ActivationFunctionType
ALU = mybir.AluOpType

NCORES = 8
N_META = 16
EPS = 1e-6
TB = 256
OWN = TB - 2


class Buf:
    __slots__ = ("name", "w", "rd", "dsem", "dcnt")

    def __init__(self, name):
        self.name = name
        self.w = None
        self.rd = {}
        self.dsem = None
        self.dcnt = 0


class KB:
    ENG = ("pe", "act", "dve", "pool", "sp")

    def __init__(self):
        self.nc = bass.Bass("TRN2", target_bir_lowering=False)
        self.st = ExitStack()
        self.ops = {e: [] for e in self.ENG}
        self.cnt = {e: 0 for e in self.ENG}
        self.semnames = []
        self.csem = {e: self._newsem(e) for e in ("pe", "act", "dve", "pool")}
        self.abufs = []
        self.cfinal = {}

    def _newsem(self, name):
        self.semnames.append(name)
        return len(self.semnames) - 1

    def sb(self, name, shape, dt):
        return self.st.enter_context(self.nc.sbuf_tensor(name, list(shape), dt))

    def ps(self, name, shape, dt=F32):
        return self.st.enter_context(self.nc.psum_tensor(name, list(shape), dt))

    def dram(self, name, shape, dt, kind="Internal"):
        if kind == "Internal":
            return self.nc.dram_tensor(name, list(shape), dt)
        return self.nc.dram_tensor(name, list(shape), dt, kind=kind)

    def _deps(self, reads, writes, own):
        waits = []
        for b in list(reads) + list(writes):
            if b.w is not None and b.w[0] != own:
                waits.append(b.w)
        for b in writes:
            for s, v in b.rd.items():
                if s != own:
                    waits.append((s, v))
        return waits

    def _commit(self, reads, writes, ev):
        for b in reads:
            if b.rd.get(ev[0], 0) < ev[1]:
                b.rd[ev[0]] = ev[1]
        for b in writes:
            b.w = ev
            b.rd = {}

    SEM_EPOCH = 48000

    def op(self, eng, fn, reads=(), writes=()):
        if self.cnt[eng] >= self.SEM_EPOCH:
            self.csem[eng] = self._newsem(f"{eng}_e{len(self.semnames)}")
            self.cnt[eng] = 0
        own = self.csem[eng]
        waits = self._deps(reads, writes, own if eng == "pe" else -1)
        self.cnt[eng] += 1
        ev = (own, self.cnt[eng])
        self.cfinal[own] = self.cnt[eng]
        self._commit(reads, writes, ev)
        self.ops[eng].append((fn, waits, (own, 1)))

    def dma(self, q, fn, reads=(), writes=(), via=None, inc=16):
        if via.dsem is None:
            via.dsem = self._newsem("d_" + via.name)
            self.abufs.append(via)
        waits = self._deps(reads, writes, -1)
        via.dcnt += inc
        ev = (via.dsem, via.dcnt)
        self._commit(reads, writes, ev)
        self.ops[q].append((fn, waits, (via.dsem, inc)))

    def emit(self):
        nc = self.nc
        with ExitStack() as st:
            sems = [st.enter_context(nc.semaphore(n)) for n in self.semnames]
            block = st.enter_context(nc.Block())
            final = [(b.dsem, b.dcnt) for b in self.abufs]

            def mk(name):
                def body(e):
                    waited = {}
                    for fn, waits, inc in self.ops[name]:
                        for s, v in waits:
                            if waited.get(s, 0) < v:
                                e.wait_ge(sems[s], v)
                                waited[s] = v
                        ins = fn(e)
                        if inc[1] == 1:
                            ins.then_inc(sems[inc[0]])
                        else:
                            ins.then_inc(sems[inc[0]], inc[1])
                    if name == "sp":
                        for s, v in final:
                            if waited.get(s, 0) < v:
                                e.wait_ge(sems[s], v)
                        for s, v in self.cfinal.items():
                            e.wait_ge(sems[s], v)
                return body

            block.tensor(mk("pe"))
            block.scalar(mk("act"))
            block.vector(mk("dve"))
            block.gpsimd(mk("pool"))
            block.sync(mk("sp"))
        self.st.close()
        return nc


def build_program(D, DFF, NG, use_cc=True):
    kb = KB()
    nc = kb.nc
    KC = D // 128
    FC = DFF // 128
    KR = D // NCORES
    NBI = 2
    CBI = 2 * DFF // NBI
    NRW = D // NCORES
    NBO = 2
    RBO = DFF // NBO
    assert KR % 128 == 0 and CBI % 128 == 0 and NRW == 512 or True

    xt = kb.dram("xt", [NG * TB, D], F32, kind="ExternalInput")
    NR = 1 if use_cc else NCORES
    wi_sh = kb.dram("wi_sh", [NR * KR, 2 * DFF], F32, kind="ExternalInput")
    wo_sh = kb.dram("wo_sh", [NR * DFF, NRW], F32, kind="ExternalInput")
    nfw_col = kb.dram("nfw_col", [128, KC], F32, kind="ExternalInput")
    nfin_b = kb.dram("nfin_b", [128, D], F32, kind="ExternalInput")
    cw_col = kb.dram("cw_col", [128, 2 * FC * 4], F32, kind="ExternalInput")
    ident_in = kb.dram("ident_in", [128, 128], F32, kind="ExternalInput")
    out = kb.dram("out", [NG * OWN, D], F32, kind="ExternalOutput")

    KK = KR // 128
    assert FC % NBI == 0
    FCB = FC // NBI
    bi = [kb.dram(f"bi{q}", [FCB * 128, KK * 256], BF16) for q in range(NBI)]
    gi = [kb.dram(f"gi{q}", [NCORES * FCB * 128, KK * 256], BF16) for q in range(NBI)]
    bo = [kb.dram(f"bo{q}", [RBO, NRW], BF16) for q in range(NBO)]
    go = [kb.dram(f"go{q}", [NCORES * RBO, NRW], BF16) for q in range(NBO)]
    b_bi = [Buf(f"bi{q}") for q in range(NBI)]
    b_gi = [Buf(f"gi{q}") for q in range(NBI)]
    b_bo = [Buf(f"bo{q}") for q in range(NBO)]
    b_go = [Buf(f"go{q}") for q in range(NBO)]

    ident_f = kb.sb("ident_f", [128, 128], F32)
    ident = kb.sb("ident", [128, 128], BF16)
    nfw = kb.sb("nfw", [128, KC], F32)
    nfin = kb.sb("nfin", [128, D], F32)
    cw = kb.sb("cw", [128, 2 * FC * 4], F32)
    b_const = Buf("const")
    kb.dma("sp", lambda e: e.dma_start(out=ident_f[:, :], in_=ident_in[:, :]), writes=[b_const], via=b_const)
    kb.dma("sp", lambda e: e.dma_start(out=nfw[:, :], in_=nfw_col[:, :]), writes=[b_const], via=b_const)
    kb.dma("sp", lambda e: e.dma_start(out=nfin[:, :], in_=nfin_b[:, :]), writes=[b_const], via=b_const)
    kb.dma("sp", lambda e: e.dma_start(out=cw[:, :], in_=cw_col[:, :]), writes=[b_const], via=b_const)
    b_ident = Buf("ident")
    kb.op("dve", lambda e: e.tensor_copy(ident[:, :], ident_f[:, :]), reads=[b_const], writes=[b_ident])

    CW = 1024
    stg_f = [kb.sb(f"stg_f{i}", [128, CW], F32) for i in range(2)]
    stg_b = [kb.sb(f"stg_b{i}", [128, CW], BF16) for i in range(2)]
    b_sf = [Buf(f"stg_f{i}") for i in range(2)]
    b_sb = [Buf(f"stg_b{i}") for i in range(2)]
    it = [0]

    def cast_piece(src_ap, dst_ap, dst_buf, rows, cols, nf=None):
        i = it[0] % 2
        it[0] += 1
        q = "sp" if (it[0] % 2) else "act"
        kb.dma(q, lambda e: e.dma_start(out=stg_f[i][:rows, :cols], in_=src_ap), writes=[b_sf[i]], via=b_sf[i])
        eng = "dve" if (it[0] % 2) else "pool"
        kb.op(eng, lambda e: e.tensor_copy(stg_b[i][:rows, :cols], stg_f[i][:rows, :cols]),
              reads=[b_sf[i]], writes=[b_sb[i]])
        sview = stg_b[i][:rows, :cols]
        if nf is not None:
            sview = sview.rearrange("p (f n) -> p f n", f=nf)
        kb.dma("sp", lambda e: e.dma_start(out=dst_ap, in_=sview),
               reads=[b_sb[i]], writes=[dst_buf], via=b_sb[i])

    for q in range(NBI):
        for r in range(NR):
            if use_cc:
                bv = bi[q].ap().rearrange("(f p) (k h n) -> p f k h n", p=128, k=KK, h=2)
                dbuf = b_bi[q]
            else:
                bv = gi[q].ap().rearrange("(r f p) (k h n) -> r p f k h n", r=NCORES, p=128, k=KK, h=2)[r]
                dbuf = b_gi[q]
            for kk in range(KK):
                for half in range(2):
                    for f0 in range(0, FCB, CW // 128):
                        nf = min(CW // 128, FCB - f0)
                        c0 = half * DFF + (q * FCB + f0) * 128
                        cast_piece(wi_sh[r * KR + kk * 128:r * KR + (kk + 1) * 128, c0:c0 + nf * 128],
                                   bv[:, f0:f0 + nf, kk, half, :], dbuf, 128, nf * 128, nf)
        if use_cc:
            kb.dma("pool", (lambda q: lambda e: e.collective_compute(
                "AllGather", ALU.bypass, replica_groups=[list(range(NCORES))],
                ins=[bi[q].ap().opt()], outs=[gi[q].ap().opt()]))(q),
                reads=[b_bi[q]], writes=[b_gi[q]], via=b_gi[q], inc=1)
    for q in range(NBO):
        for r in range(NR):
            for r0 in range(0, RBO, 128):
                rr = min(128, RBO - r0)
                if use_cc:
                    dst, dbuf = bo[q][r0:r0 + rr, :], b_bo[q]
                else:
                    dst, dbuf = go[q][r * RBO + r0:r * RBO + r0 + rr, :], b_go[q]
                cast_piece(wo_sh[r * DFF + q * RBO + r0:r * DFF + q * RBO + r0 + rr, :], dst, dbuf, rr, NRW)
        if use_cc:
            kb.dma("pool", (lambda q: lambda e: e.collective_compute(
                "AllGather", ALU.bypass, replica_groups=[list(range(NCORES))],
                ins=[bo[q].ap().opt()], outs=[go[q].ap().opt()]))(q),
                reads=[b_bo[q]], writes=[b_go[q]], via=b_go[q], inc=1)

    NT = TB // 128
    h_t = [kb.sb(f"h{t}", [128, D], F32) for t in range(NT)]
    b_h = [Buf(f"h{t}") for t in range(NT)]
    ub = kb.sb("ub", [128, D], BF16)
    b_ub = Buf("ub")
    ss = kb.sb("ss", [128, 2], F32)
    b_ss = Buf("ss")
    uT = kb.sb("uT", [128, KC, TB], BF16)
    b_uT = Buf("uT")
    hid = kb.sb("hid", [128, FC, TB], BF16)
    b_hid = Buf("hid")
    wi_t = [kb.sb(f"wi_t{i}", [128, KC, 256], BF16) for i in range(2)]
    b_wi = [Buf(f"wi_t{i}") for i in range(2)]
    wo_t = [kb.sb(f"wo_t{i}", [128, 2048], BF16) for i in range(3)]
    b_wo = [Buf(f"wo_t{i}") for i in range(3)]
    yf = [kb.sb(f"yf{i}", [128, 2, TB], F32) for i in range(2)]
    b_yf = [Buf(f"yf{i}") for i in range(2)]
    zf = [kb.sb(f"zf{i}", [128, 2, TB], F32) for i in range(2)]
    b_zf = [Buf(f"zf{i}") for i in range(2)]
    p_tr = [kb.ps(f"p_tr{i}", [128, 1024], BF16) for i in range(2)]
    b_ptr = [Buf(f"p_tr{i}") for i in range(2)]
    p_y = [kb.ps(f"p_y{i}", [128, 2, TB], F32) for i in range(2)]
    b_py = [Buf(f"p_y{i}") for i in range(2)]
    p_o = kb.ps("p_o", [128, 4, 512], F32)
    b_po = [Buf(f"p_o{i}") for i in range(4)]

    def rms_rstd(src_t, sbuf, col):
        kb.op("dve", lambda e, src_t=src_t: e.tensor_tensor(ub[:, :], src_t[:, :], src_t[:, :], ALU.mult),
              reads=[sbuf], writes=[b_ub])
        kb.op("dve", lambda e, col=col: e.reduce_sum(ss[:, col:col + 1], ub[:, :], mybir.AxisListType.X),
              reads=[b_ub], writes=[b_ss])
        kb.op("dve", lambda e, col=col: e.tensor_scalar(ss[:, col:col + 1], ss[:, col:col + 1], 1.0 / D, EPS,
                                                        ALU.mult, ALU.add), reads=[b_ss], writes=[b_ss])
        kb.op("act", lambda e, col=col: e.activation(ss[:, col:col + 1], ss[:, col:col + 1], AF.Sqrt),
              reads=[b_ss], writes=[b_ss])
        kb.op("dve", lambda e, col=col: e.reciprocal(ss[:, col:col + 1], ss[:, col:col + 1]),
              reads=[b_ss], writes=[b_ss])

    wcount = [0]
    for g in range(NG):
        for t in range(NT):
            kb.dma("pool", lambda e, dst=h_t[t][:, :], s=xt[g * TB + t * 128:g * TB + (t + 1) * 128, :]:
                   e.dma_start(out=dst, in_=s), writes=[b_h[t]], via=b_h[t])
        for t in range(NT):
            rms_rstd(h_t[t], b_h[t], 0)
            kb.op("dve", lambda e, t=t: e.tensor_scalar(ub[:, :], h_t[t][:, :], ss[:, 0:1], None, ALU.mult),
                  reads=[b_h[t], b_ss], writes=[b_ub])
            for k8 in range(0, KC, 8):
                pi = (k8 // 8) % 2
                nk = min(8, KC - k8)
                for k in range(k8, k8 + nk):
                    kb.op("pe", lambda e, k=k, pi=pi: e.transpose(
                        p_tr[pi][:, (k % 8) * 128:(k % 8 + 1) * 128], ub[:, k * 128:(k + 1) * 128], ident[:, :]),
                        reads=[b_ub, b_ident], writes=[b_ptr[pi]])
                for k in range(k8, k8 + nk):
                    kb.op("dve", lambda e, k=k, pi=pi, t=t: e.tensor_scalar(
                        uT[:, k, t * 128:(t + 1) * 128], p_tr[pi][:, (k % 8) * 128:(k % 8 + 1) * 128],
                        nfw[:, k:k + 1], None, ALU.mult),
                        reads=[b_ptr[pi], b_const], writes=[b_uT])
        for fc in range(FC):
            wi = wcount[0] % 2
            wcount[0] += 1
            qi, fl = divmod(fc, FCB)
            gvi = gi[qi].ap().rearrange("(r f p) (k c) -> f p r k c", r=NCORES, p=128, k=KK)
            kb.dma("sp" if fc % 2 == 0 else "act",
                   lambda e, s=gvi[fl], dst=wi_t[wi][:, :, :].rearrange("p (r k) c -> p r k c", r=NCORES):
                   e.dma_start(out=dst, in_=s),
                   reads=[b_gi[qi]], writes=[b_wi[wi]], via=b_wi[wi])
            pi = fc % 2
            for half in range(2):
                for k in range(KC):
                    kb.op("pe", lambda e, k=k, half=half, wi=wi, pi=pi: e.matmul(
                        p_y[pi][:, half, :], wi_t[wi][:, k, half * 128:(half + 1) * 128], uT[:, k, :],
                        start=(k == 0), stop=(k == KC - 1)),
                        reads=[b_wi[wi], b_uT], writes=[b_py[pi]])
            yi = fc % 2
            kb.op("act", lambda e, pi=pi, yi=yi: e.activation(yf[yi][:, :, :], p_y[pi][:, :, :], AF.Copy),
                  reads=[b_py[pi]], writes=[b_yf[yi]])
            for half in range(2):
                c4 = (half * FC + fc) * 4
                kb.op("dve", lambda e, half=half, yi=yi, c4=c4: e.tensor_scalar(
                    zf[yi][:, half, :], yf[yi][:, half, :], cw[:, c4 + 2:c4 + 3], cw[:, c4 + 3:c4 + 4],
                    ALU.mult, ALU.add), reads=[b_yf[yi], b_const], writes=[b_zf[yi]])
                kb.op("dve", lambda e, half=half, yi=yi, c4=c4: e.scalar_tensor_tensor(
                    zf[yi][:, half, 1:TB], yf[yi][:, half, 0:TB - 1], cw[:, c4 + 1:c4 + 2], zf[yi][:, half, 1:TB],
                    ALU.mult, ALU.add), reads=[b_yf[yi], b_const, b_zf[yi]], writes=[b_zf[yi]])
                kb.op("dve", lambda e, half=half, yi=yi, c4=c4: e.scalar_tensor_tensor(
                    zf[yi][:, half, 2:TB], yf[yi][:, half, 0:TB - 2], cw[:, c4:c4 + 1], zf[yi][:, half, 2:TB],
                    ALU.mult, ALU.add), reads=[b_yf[yi], b_const, b_zf[yi]], writes=[b_zf[yi]])
            kb.op("act", lambda e, yi=yi: e.activation(yf[yi][:, 0, :], zf[yi][:, 0, :], AF.Silu),
                  reads=[b_zf[yi]], writes=[b_yf[yi]])
            kb.op("pool", lambda e, yi=yi, fc=fc: e.tensor_tensor(
                hid[:, fc, :], yf[yi][:, 0, :], zf[yi][:, 1, :], ALU.mult),
                reads=[b_yf[yi], b_zf[yi]], writes=[b_hid])
        for t in range(NT):
            for nh in range(-(-D // 2048)):
                ncols = min(2048, D - nh * 2048)
                nb = ncols // 512
                for fc in range(FC):
                    wo = wcount[0] % 3
                    wcount[0] += 1
                    qo, fr = divmod(fc * 128, RBO)
                    gv = go[qo].ap().rearrange("(r f) n -> f r n", r=NCORES)
                    src_ap = gv[fr:fr + 128, nh * 4:nh * 4 + nb, :]
                    dst_ap = wo_t[wo][:, :nb * 512].rearrange("p (r n) -> p r n", n=512)
                    kb.dma("sp" if fc % 2 == 0 else "act",
                           lambda e, s=src_ap, dst=dst_ap: e.dma_start(out=dst, in_=s),
                           reads=[b_go[qo]], writes=[b_wo[wo]], via=b_wo[wo])
                    for j in range(nb):
                        kb.op("pe", lambda e, j=j, fc=fc, wo=wo, t=t: e.matmul(
                            p_o[:, j, :], hid[:, fc, t * 128:(t + 1) * 128], wo_t[wo][:, j * 512:(j + 1) * 512],
                            start=(fc == 0), stop=(fc == FC - 1)),
                            reads=[b_hid, b_wo[wo]], writes=[b_po[j]])
                for j in range(nb):
                    c0 = nh * 2048 + j * 512
                    kb.op("dve", lambda e, j=j, c0=c0, t=t: e.tensor_tensor(
                        h_t[t][:, c0:c0 + 512], p_o[:, j, :], h_t[t][:, c0:c0 + 512], ALU.add),
                        reads=[b_po[j], b_h[t]], writes=[b_h[t]])
            rms_rstd(h_t[t], b_h[t], 1)
            kb.op("dve", lambda e, t=t: e.scalar_tensor_tensor(
                h_t[t][:, :], h_t[t][:, :], ss[:, 1:2], nfin[:, :], ALU.mult, ALU.mult),
                reads=[b_h[t], b_ss, b_const], writes=[b_h[t]])
            if t == 0:
                kb.dma("pool", lambda e, dst=out[g * OWN:g * OWN + 126, :], s=h_t[0][2:128, :]:
                       e.dma_start(out=dst, in_=s), reads=[b_h[0]], via=b_h[0])
            else:
                r0 = g * OWN + 126 + (t - 1) * 128
                kb.dma("pool", lambda e, dst=out[r0:r0 + 128, :], s=h_t[t][:, :]:
                       e.dma_start(out=dst, in_=s), reads=[b_h[t]], via=b_h[t])
    return kb.emit()


class Shared:
    def __init__(self, kb, D, ident_in, wt_kc):
        self.kb = kb
        self.D = D
        self.ident_f = kb.sb("ident_f", [128, 128], F32)
        self.ident = kb.sb("ident", [128, 128], BF16)
        self.b_ident = Buf("ident")
        b0 = Buf("ident_f")
        kb.dma("sp", lambda e: e.dma_start(out=self.ident_f[:, :], in_=ident_in[:, :]), writes=[b0], via=b0)
        kb.op("dve", lambda e: e.tensor_copy(self.ident[:, :], self.ident_f[:, :]), reads=[b0], writes=[self.b_ident])
        self.ub = kb.sb("ub", [128, D], BF16)
        self.b_ub = Buf("ub")
        self.ss = kb.sb("ss", [128, 4], F32)
        self.b_ss = Buf("ss")
        self.p_tr = [kb.ps(f"p_tr{i}", [128, 1024], BF16) for i in range(2)]
        self.b_ptr = [Buf(f"p_tr{i}") for i in range(2)]
        self.p_o = kb.ps("p_o", [128, 4, 512], F32)
        self.b_po = [Buf(f"p_o{i}") for i in range(4)]
        self.po_i = 0
        self.wt = [kb.sb(f"wt{i}", [128, wt_kc, 512], BF16) for i in range(2)]
        self.b_wt = [Buf(f"wt{i}") for i in range(2)]
        self.wt_i = 0
        CW = 1024
        self.CW = CW
        self.stg_f = [kb.sb(f"stg_f{i}", [128, CW], F32) for i in range(2)]
        self.stg_b = [kb.sb(f"stg_b{i}", [128, CW], BF16) for i in range(2)]
        self.b_sf = [Buf(f"stg_f{i}") for i in range(2)]
        self.b_sb = [Buf(f"stg_b{i}") for i in range(2)]
        self.stg_i = 0


def emit_cast_tiled(S, src, K, N, name):
    kb = S.kb
    KC, NB = K // 128, N // 512
    assert N % 512 == 0 and K % 128 == 0
    scr = kb.dram(name, [NB * 128, KC * 512], BF16)
    b_scr = Buf(name)
    sv = scr.ap().rearrange("(nb p) (k n) -> p nb k n", p=128, k=KC)
    for k in range(KC):
        for nb0 in range(0, NB, 2):
            nn = min(2, NB - nb0)
            i = S.stg_i % 2
            S.stg_i += 1
            cols = nn * 512
            kb.dma("sp" if S.stg_i % 2 else "act",
                   lambda e, i=i, cols=cols, s=src[k * 128:(k + 1) * 128, nb0 * 512:nb0 * 512 + cols]:
                   e.dma_start(out=S.stg_f[i][:, :cols], in_=s), writes=[S.b_sf[i]], via=S.b_sf[i])
            kb.op("dve" if S.stg_i % 2 else "pool",
                  lambda e, i=i, cols=cols: e.tensor_copy(S.stg_b[i][:, :cols], S.stg_f[i][:, :cols]),
                  reads=[S.b_sf[i]], writes=[S.b_sb[i]])
            kb.dma("sp", lambda e, i=i, cols=cols, nn=nn, d=sv[:, nb0:nb0 + nn, k, :]:
                   e.dma_start(out=d, in_=S.stg_b[i][:, :cols].rearrange("p (a n) -> p a n", a=nn)),
                   reads=[S.b_sb[i]], writes=[b_scr], via=S.b_sb[i])
    return scr, b_scr


def emit_rstd(S, src_t, b_src, col, width):
    kb = S.kb
    kb.op("dve", lambda e: e.tensor_tensor(S.ub[:, :width], src_t[:, :width], src_t[:, :width], ALU.mult),
          reads=[b_src], writes=[S.b_ub])
    kb.op("dve", lambda e: e.reduce_sum(S.ss[:, col:col + 1], S.ub[:, :width], mybir.AxisListType.X),
          reads=[S.b_ub], writes=[S.b_ss])
    kb.op("dve", lambda e: e.tensor_scalar(S.ss[:, col:col + 1], S.ss[:, col:col + 1], 1.0 / width, EPS,
                                           ALU.mult, ALU.add), reads=[S.b_ss], writes=[S.b_ss])
    kb.op("act", lambda e: e.activation(S.ss[:, col:col + 1], S.ss[:, col:col + 1], AF.Sqrt),
          reads=[S.b_ss], writes=[S.b_ss])
    kb.op("dve", lambda e: e.reciprocal(S.ss[:, col:col + 1], S.ss[:, col:col + 1]),
          reads=[S.b_ss], writes=[S.b_ss])


def emit_transposeT(S, width, dstT, b_dst, t, scale_cols=None, b_scale=None):
    kb = S.kb
    KC = width // 128
    for k8 in range(0, KC, 8):
        pi = (k8 // 8) % 2
        nk = min(8, KC - k8)
        for k in range(k8, k8 + nk):
            kb.op("pe", lambda e, k=k, pi=pi: e.transpose(
                S.p_tr[pi][:, (k % 8) * 128:(k % 8 + 1) * 128], S.ub[:, k * 128:(k + 1) * 128], S.ident[:, :]),
                reads=[S.b_ub, S.b_ident], writes=[S.b_ptr[pi]])
        for k in range(k8, k8 + nk):
            if scale_cols is not None:
                kb.op("dve", lambda e, k=k, pi=pi: e.tensor_scalar(
                    dstT[:, k, t * 128:(t + 1) * 128], S.p_tr[pi][:, (k % 8) * 128:(k % 8 + 1) * 128],
                    scale_cols[:, k:k + 1], None, ALU.mult),
                    reads=[S.b_ptr[pi], b_scale], writes=[b_dst])
            else:
                kb.op("dve", lambda e, k=k, pi=pi: e.tensor_copy(
                    dstT[:, k, t * 128:(t + 1) * 128], S.p_tr[pi][:, (k % 8) * 128:(k % 8 + 1) * 128]),
                    reads=[S.b_ptr[pi]], writes=[b_dst])


def emit_gemm_tok(S, lhsT, b_lhs, KC, scr, b_scr, NB, NT, evac):
    kb = S.kb
    for nb in range(NB):
        wi = S.wt_i % 2
        S.wt_i += 1
        kb.dma("sp" if nb % 2 == 0 else "act",
               lambda e, wi=wi, s=scr[nb * 128:(nb + 1) * 128, :]:
               e.dma_start(out=S.wt[wi][:, :KC, :], in_=s.rearrange("p (k n) -> p k n", k=KC)),
               reads=[b_scr], writes=[S.b_wt[wi]], via=S.b_wt[wi])
        for t in range(NT):
            j = S.po_i % 4
            S.po_i += 1
            for k in range(KC):
                kb.op("pe", lambda e, k=k, wi=wi, j=j, t=t: e.matmul(
                    S.p_o[:, j, :], lhsT[:, k, t * 128:(t + 1) * 128], S.wt[wi][:, k, :],
                    start=(k == 0), stop=(k == KC - 1)),
                    reads=[b_lhs, S.b_wt[wi]], writes=[S.b_po[j]])
            evac(t, nb, S.p_o[:, j, :], S.b_po[j])


def build_mixin(D, N1, N2, NG):
    kb = KB()
    KC = D // 128
    NT = TB // 128
    xt = kb.dram("xt", [NG * TB, D], F32, kind="ExternalInput")
    w1 = kb.dram("w1", [D, N1], F32, kind="ExternalInput")
    w2 = kb.dram("w2", [D, N2], F32, kind="ExternalInput")
    nw_col = kb.dram("nw_col", [128, KC], F32, kind="ExternalInput")
    ident_in = kb.dram("ident_in", [128, 128], F32, kind="ExternalInput")
    z = kb.dram("z", [NG * TB, N1], F32, kind="ExternalOutput")
    gt = kb.dram("gt", [NG * TB, N2], F32, kind="ExternalOutput")
    S = Shared(kb, D, ident_in, KC)
    nw = kb.sb("nw", [128, KC], F32)
    b_nw = Buf("nw")
    kb.dma("sp", lambda e: e.dma_start(out=nw[:, :], in_=nw_col[:, :]), writes=[b_nw], via=b_nw)
    s1, b_s1 = emit_cast_tiled(S, w1, D, N1, "s_w1")
    s2, b_s2 = emit_cast_tiled(S, w2, D, N2, "s_w2")
    h_t = [kb.sb(f"h{t}", [128, D], F32) for t in range(NT)]
    b_h = [Buf(f"h{t}") for t in range(NT)]
    uT = kb.sb("uT", [128, KC, TB], BF16)
    b_uT = Buf("uT")
    og = [kb.sb(f"og{i}", [128, 512], F32) for i in range(4)]
    b_og = [Buf(f"og{i}") for i in range(4)]
    oc = [0]
    for g in range(NG):
        for t in range(NT):
            kb.dma("pool", lambda e, dst=h_t[t][:, :], s=xt[g * TB + t * 128:g * TB + (t + 1) * 128, :]:
                   e.dma_start(out=dst, in_=s), writes=[b_h[t]], via=b_h[t])
        for t in range(NT):
            emit_rstd(S, h_t[t], b_h[t], 0, D)
            kb.op("dve", lambda e, t=t: e.tensor_scalar(S.ub[:, :], h_t[t][:, :], S.ss[:, 0:1], None, ALU.mult),
                  reads=[b_h[t], S.b_ss], writes=[S.b_ub])
            emit_transposeT(S, D, uT, b_uT, t, nw, b_nw)

        def mk_evac(dst, func):
            def evac(t, nb, ps, b_ps):
                i = oc[0] % 4
                oc[0] += 1
                kb.op("act", lambda e: e.activation(og[i][:, :], ps, func), reads=[b_ps], writes=[b_og[i]])
                r0 = g * TB + t * 128
                kb.dma("pool", lambda e: e.dma_start(out=dst[r0:r0 + 128, nb * 512:(nb + 1) * 512], in_=og[i][:, :]),
                       reads=[b_og[i]], via=b_og[i])
            return evac
        emit_gemm_tok(S, uT, b_uT, KC, s1, b_s1, N1 // 512, NT, mk_evac(z, AF.Copy))
        emit_gemm_tok(S, uT, b_uT, KC, s2, b_s2, N2 // 512, NT, mk_evac(gt, AF.Sigmoid))
    return kb.emit()


def _group_rows(hfull, NG, halo):
    own = TB - halo
    L, D = hfull.shape
    outs = []
    for c in range(NCORES):
        rows = np.zeros((NG * TB, D), hfull.dtype)
        for j in range(NG):
            p0 = (c * NG + j) * own - halo
            a, b = max(p0, 0), min(p0 + TB, L)
            if b > a:
                rows[j * TB + (a - p0):j * TB + (b - p0)] = hfull[a:b]
        outs.append(rows)
    return outs


def _ungroup_rows(per_core, NG, halo, L):
    own = TB - halo
    chunks = []
    for c in range(NCORES):
        a = per_core[c].reshape(NG, TB, -1)[:, halo:, :]
        chunks.append(a.reshape(NG * own, -1))
    return np.concatenate(chunks, axis=0)[:L]


def run_mixin(hfull, norm_mix_w, w_in, w_gate):
    L, D = hfull.shape
    N1 = -(-w_in.shape[1] // 512) * 512
    N2 = w_gate.shape[1]
    NG = -(-(-(-L // TB)) // NCORES)
    nc = build_mixin(D, N1, N2, NG)
    w1 = np.zeros((D, N1), np.float32)
    w1[:, :w_in.shape[1]] = w_in
    rows = _group_rows(hfull, NG, 0)
    nw_col = np.ascontiguousarray(norm_mix_w.reshape(D // 128, 128).T)
    ident = np.eye(128, dtype=np.float32)
    maps = [{"xt": rows[c], "w1": w1, "w2": np.ascontiguousarray(w_gate), "nw_col": nw_col, "ident_in": ident}
            for c in range(NCORES)]
    res = run_bass_kernel_spmd(nc, maps, core_ids=list(range(NCORES)))
    z = _ungroup_rows([res.results[c]["z"] for c in range(NCORES)], NG, 0, L)[:, :w_in.shape[1]]
    gt = _ungroup_rows([res.results[c]["gt"] for c in range(NCORES)], NG, 0, L)
    return z, gt


AW = 2048
LW, LA, LG = 96, 96, 256


def build_rwkv_prep(NT128):
    kb = KB()
    ACOLS = 3 * AW + LW + LA + LG
    za = kb.dram("za", [NT128 * 128, ACOLS], F32, kind="ExternalInput")
    zp = kb.dram("zp", [NT128 * 128, ACOLS], F32, kind="ExternalInput")
    mu_b = kb.dram("mu_b", [128, ACOLS], F32, kind="ExternalInput")
    cst = kb.dram("cst", [4, 128, AW], F32, kind="ExternalInput")
    w2i = kb.dram("w2i", [LW, AW], F32, kind="ExternalInput")
    a2i = kb.dram("a2i", [LA, AW], F32, kind="ExternalInput")
    g2i = kb.dram("g2i", [LG, AW], F32, kind="ExternalInput")
    ident_in = kb.dram("ident_in", [128, 128], F32, kind="ExternalInput")
    outs = {n: kb.dram(n, [NT128 * 128, AW], F32, kind="ExternalOutput")
            for n in ("o_r", "o_kp", "o_v", "o_kap", "o_b", "o_dec", "o_g")}
    ident = kb.sb("ident", [128, 128], F32)
    mu = kb.sb("mu", [128, ACOLS], F32)
    cs = kb.sb("cs", [128, 4, AW], F32)
    w2 = kb.sb("w2", [128, AW], F32)
    a2 = kb.sb("a2", [128, AW], F32)
    g2 = kb.sb("g2", [128, 2, AW], F32)
    b_c = Buf("c")
    kb.dma("sp", lambda e: e.dma_start(out=ident[:, :], in_=ident_in[:, :]), writes=[b_c], via=b_c)
    kb.dma("sp", lambda e: e.dma_start(out=mu[:, :], in_=mu_b[:, :]), writes=[b_c], via=b_c)
    for i in range(4):
        kb.dma("act", lambda e, i=i: e.dma_start(out=cs[:, i, :], in_=cst[i, :, :]), writes=[b_c], via=b_c)
    kb.dma("sp", lambda e: e.dma_start(out=w2[0:LW, :], in_=w2i[:, :]), writes=[b_c], via=b_c)
    kb.dma("sp", lambda e: e.dma_start(out=a2[0:LA, :], in_=a2i[:, :]), writes=[b_c], via=b_c)
    kb.dma("sp", lambda e: e.dma_start(out=g2[:, :, :], in_=g2i.ap().rearrange("(k p) n -> p k n", p=128)),
           writes=[b_c], via=b_c)
    zt = kb.sb("zt", [128, ACOLS], F32)
    pt = kb.sb("pt", [128, ACOLS], F32)
    b_zt, b_pt = Buf("zt"), Buf("pt")
    W = [kb.sb(f"W{i}", [128, AW], F32) for i in range(5)]
    b_W = [Buf(f"W{i}") for i in range(5)]
    sm = kb.sb("sm", [128, 512], F32)
    b_sm = Buf("sm")
    lT = kb.sb("lT", [128, 4, 128], F32)
    b_lT = Buf("lT")
    hs = kb.sb("hs", [128, 64], F32)
    b_hs = Buf("hs")
    ptr = kb.ps("ptr", [128, 512], F32)
    b_ptr = Buf("ptr")
    pm = [kb.ps(f"pm{i}", [128, 512], F32) for i in range(2)]
    b_pm = [Buf(f"pm{i}") for i in range(2)]
    pc = [0]
    X = mybir.AxisListType.X
    R0, K0, V0 = 0, AW, 2 * AW
    WL0, AL0, GL0 = 3 * AW, 3 * AW + LW, 3 * AW + LW + LA

    def store(name, tile_ap, buf, rows):
        kb.dma("pool", lambda e, d=outs[name][rows, :]: e.dma_start(out=d, in_=tile_ap), reads=[buf], via=buf)

    def lora_mm(dst, b_dst, slots, kparts, rhs_fn, bias_idx):
        for nb in range(AW // 512):
            i = pc[0] % 2
            pc[0] += 1
            for si, slot in enumerate(slots):
                kb.op("pe", lambda e, i=i, nb=nb, si=si, slot=slot: e.matmul(
                    pm[i][:, :], lT[0:kparts, slot, :], rhs_fn(si)[0:kparts, nb * 512:(nb + 1) * 512],
                    start=(si == 0), stop=(si == len(slots) - 1)), reads=[b_lT, b_c], writes=[b_pm[i]])
            if bias_idx is None:
                kb.op("act", lambda e, i=i, nb=nb: e.activation(dst[:, nb * 512:(nb + 1) * 512], pm[i][:, :], AF.Copy),
                      reads=[b_pm[i]], writes=[b_dst])
            else:
                kb.op("dve", lambda e, i=i, nb=nb: e.tensor_tensor(
                    dst[:, nb * 512:(nb + 1) * 512], pm[i][:, :], cs[:, bias_idx, nb * 512:(nb + 1) * 512], ALU.add),
                    reads=[b_pm[i], b_c], writes=[b_dst])

    for j in range(NT128):
        rows = slice(j * 128, (j + 1) * 128)
        kb.dma("sp", lambda e, s=za[rows, :]: e.dma_start(out=zt[:, :], in_=s), writes=[b_zt], via=b_zt)
        kb.dma("act", lambda e, s=zp[rows, :]: e.dma_start(out=pt[:, :], in_=s), writes=[b_pt], via=b_pt)
        kb.op("dve", lambda e: e.tensor_tensor(pt[:, :], pt[:, :], zt[:, :], ALU.subtract), reads=[b_zt, b_pt], writes=[b_pt])
        kb.op("pool", lambda e: e.tensor_tensor(pt[:, :], pt[:, :], mu[:, :], ALU.mult), reads=[b_pt, b_c], writes=[b_pt])
        kb.op("dve", lambda e: e.tensor_tensor(zt[:, :], zt[:, :], pt[:, :], ALU.add), reads=[b_zt, b_pt], writes=[b_zt])
        store("o_r", zt[:, R0:R0 + AW], b_zt, rows)
        store("o_v", zt[:, V0:V0 + AW], b_zt, rows)
        kb.op("act", lambda e: e.activation(sm[:, 0:LW], zt[:, WL0:WL0 + LW], AF.Tanh), reads=[b_zt], writes=[b_sm])
        kb.op("act", lambda e: e.activation(sm[:, 256:512], zt[:, GL0:GL0 + LG], AF.Sigmoid), reads=[b_zt], writes=[b_sm])
        kb.op("dve", lambda e: e.tensor_copy(sm[:, 128:128 + LA], zt[:, AL0:AL0 + LA]), reads=[b_zt], writes=[b_sm])
        for slot, (c0, wdt) in enumerate(((0, LW), (128, LA), (256, 128), (384, 128))):
            kb.op("pe", lambda e, slot=slot, c0=c0, wdt=wdt: e.transpose(
                ptr[0:wdt, slot * 128:(slot + 1) * 128], sm[:, c0:c0 + wdt], ident[:, :]),
                reads=[b_sm, b_c], writes=[b_ptr])
            kb.op("dve", lambda e, slot=slot, wdt=wdt: e.tensor_copy(lT[0:wdt, slot, :], ptr[0:wdt, slot * 128:(slot + 1) * 128]),
                  reads=[b_ptr], writes=[b_lT])
        lora_mm(W[0], b_W[0], [0], LW, lambda si: w2, 0)
        kb.op("act", lambda e: e.activation(W[0][:, :], W[0][:, :], AF.Exp, scale=-1.0), reads=[b_W[0]], writes=[b_W[0]])
        kb.op("act", lambda e: e.activation(W[0][:, :], W[0][:, :], AF.Ln, bias=1.0), reads=[b_W[0]], writes=[b_W[0]])
        kb.op("act", lambda e: e.activation(W[0][:, :], W[0][:, :], AF.Exp, scale=-1.0, bias=-0.5),
              reads=[b_W[0]], writes=[b_W[0]])
        kb.op("act", lambda e: e.activation(W[0][:, :], W[0][:, :], AF.Exp, scale=-1.0), reads=[b_W[0]], writes=[b_W[0]])
        store("o_dec", W[0][:, :], b_W[0], rows)
        lora_mm(W[1], b_W[1], [1], LA, lambda si: a2, 1)
        kb.op("act", lambda e: e.activation(W[1][:, :], W[1][:, :], AF.Sigmoid), reads=[b_W[1]], writes=[b_W[1]])
        lora_mm(W[2], b_W[2], [2, 3], 128, lambda si: g2[:, si, :], None)
        store("o_g", W[2][:, :], b_W[2], rows)
        kb.op("dve", lambda e: e.tensor_tensor(W[3][:, :], zt[:, K0:K0 + AW], cs[:, 2, :], ALU.mult),
              reads=[b_zt, b_c], writes=[b_W[3]])
        kb.op("pool", lambda e: e.tensor_tensor(W[4][:, :], W[3][:, :], W[3][:, :], ALU.mult), reads=[b_W[3]], writes=[b_W[4]])
        kb.op("dve", lambda e: e.reduce_sum(hs[:, 0:32], W[4][:, :].rearrange("p (h k) -> p h k", k=64), X),
              reads=[b_W[4]], writes=[b_hs])
        kb.op("dve", lambda e: e.tensor_scalar(hs[:, 0:32], hs[:, 0:32], 1e-12, None, ALU.add), reads=[b_hs], writes=[b_hs])
        kb.op("act", lambda e: e.activation(hs[:, 0:32], hs[:, 0:32], AF.Sqrt), reads=[b_hs], writes=[b_hs])
        kb.op("dve", lambda e: e.reciprocal(hs[:, 0:32], hs[:, 0:32]), reads=[b_hs], writes=[b_hs])
        kb.op("dve", lambda e: e.tensor_tensor(
            W[3][:, :].rearrange("p (h k) -> p h k", k=64), W[3][:, :].rearrange("p (h k) -> p h k", k=64),
            hs[:, 0:32].unsqueeze(2).to_broadcast([128, 32, 64]), ALU.mult), reads=[b_W[3], b_hs], writes=[b_W[3]])
        store("o_kap", W[3][:, :], b_W[3], rows)
        kb.op("pool", lambda e: e.tensor_tensor(W[4][:, :], W[3][:, :], W[1][:, :], ALU.mult),
              reads=[b_W[3], b_W[1]], writes=[b_W[4]])
        store("o_b", W[4][:, :], b_W[4], rows)
        kb.op("dve", lambda e: e.scalar_tensor_tensor(W[1][:, :], W[1][:, :], 1.0, cs[:, 3, :], ALU.subtract, ALU.mult),
              reads=[b_W[1], b_c], writes=[b_W[1]])
        kb.op("dve", lambda e: e.tensor_scalar(W[1][:, :], W[1][:, :], 1.0, None, ALU.add), reads=[b_W[1]], writes=[b_W[1]])
        kb.op("pool", lambda e: e.tensor_tensor(W[1][:, :], W[1][:, :], zt[:, K0:K0 + AW], ALU.mult),
              reads=[b_W[1], b_zt], writes=[b_W[1]])
        store("o_kp", W[1][:, :], b_W[1], rows)
    return kb.emit()


def run_rwkv_prep(za, mu_shift, w0, w2, a0, a2, g2, k_k, k_a):
    L = za.shape[0]
    NT128 = -(-(-(-L // 128)) // NCORES)
    tot = NT128 * NCORES * 128
    zap = np.zeros((tot, za.shape[1]), np.float32)
    zap[:L] = za
    zpp = np.zeros_like(zap)
    zpp[1:L] = za[:L - 1]
    bc = lambda v: np.ascontiguousarray(np.broadcast_to(v[None, :], (128, v.shape[0]))).astype(np.float32)
    common = {"mu_b": bc(mu_shift), "cst": np.stack([bc(w0), bc(a0), bc(k_k), bc(k_a)]),
              "w2i": np.ascontiguousarray(w2), "a2i": np.ascontiguousarray(a2), "g2i": np.ascontiguousarray(g2),
              "ident_in": np.eye(128, dtype=np.float32)}
    nc = build_rwkv_prep(NT128)
    maps = []
    for c in range(NCORES):
        r = slice(c * NT128 * 128, (c + 1) * NT128 * 128)
        maps.append(dict(common, za=zap[r], zp=zpp[r]))
    res = run_bass_kernel_spmd(nc, maps, core_ids=list(range(NCORES)))
    return {n: np.concatenate([res.results[c][n] for c in range(NCORES)], axis=0)[:L]
            for n in ("o_r", "o_kp", "o_v", "o_kap", "o_b", "o_dec", "o_g")}


SBLK = 32


def build_rwkv_scan(Tp):
    kb = KB()
    NB = Tp // SBLK
    L1 = kb.dram("L1", [128, Tp * 8], F32, kind="ExternalInput")
    LB = kb.dram("LB", [4, Tp * 128], F32, kind="ExternalInput")
    LK = kb.dram("LK", [4, Tp * 128], F32, kind="ExternalInput")
    VM = kb.dram("VM", [4, Tp * 128], F32, kind="ExternalInput")
    DC = kb.dram("DC", [128, Tp * 2], F32, kind="ExternalInput")
    M8 = kb.dram("M8", [8, 128], F32, kind="ExternalInput")
    yo = kb.dram("yo", [4, Tp * 128], F32, kind="ExternalOutput")
    ST = kb.sb("ST", [128, 128], F32)
    b_ST = Buf("ST")
    m8 = kb.sb("m8", [8, 128], F32)
    b_m8 = Buf("m8")
    kb.dma("sp", lambda e: e.dma_start(out=m8[:, :], in_=M8[:, :]), writes=[b_m8], via=b_m8)
    kb.op("pool", lambda e: e.memset(ST[:, :], 0.0), writes=[b_ST])
    l1 = [kb.sb(f"l1_{i}", [128, SBLK, 8], F32) for i in range(2)]
    lb = [kb.sb(f"lb_{i}", [4, SBLK, 128], F32) for i in range(2)]
    lk = [kb.sb(f"lk_{i}", [4, SBLK, 128], F32) for i in range(2)]
    vm = [kb.sb(f"vm_{i}", [4, SBLK, 128], F32) for i in range(2)]
    dc = [kb.sb(f"dc_{i}", [128, SBLK, 2], F32) for i in range(2)]
    r2 = [kb.sb(f"r2_{i}", [8, SBLK, 128], F32) for i in range(2)]
    b_in = [Buf(f"in{i}") for i in range(2)]
    b_r2 = [Buf(f"r2_{i}") for i in range(2)]
    p1 = kb.ps("p1", [8, 128], F32)
    pU = kb.ps("pU", [128, 128], F32)
    b_p1, b_pU = Buf("p1"), Buf("pU")
    for blk in range(NB):
        i = blk % 2
        t0 = blk * SBLK
        for (dst, srcd, w, q) in ((l1[i], L1, 8, "sp"), (lb[i], LB, 128, "act"), (lk[i], LK, 128, "sp"),
                                  (vm[i], VM, 128, "act"), (dc[i], DC, 2, "sp")):
            kb.dma(q, lambda e, dst=dst, w=w, s=srcd[:, t0 * w:(t0 + SBLK) * w]: e.dma_start(
                out=dst[:, :, :].rearrange("p a b -> p (a b)"), in_=s), writes=[b_in[i]], via=b_in[i])
        for s in range(SBLK):
            kb.op("pe", lambda e, i=i, s=s: e.matmul(p1[:, :], l1[i][:, s, :], ST[:, :], start=True, stop=True),
                  reads=[b_in[i], b_ST], writes=[b_p1])
            kb.op("dve", lambda e, i=i, s=s: e.tensor_tensor(r2[i][:, s, :], p1[:, :], m8[:, :], ALU.mult),
                  reads=[b_p1, b_m8], writes=[b_r2[i]])
            kb.op("pe", lambda e, i=i, s=s: e.matmul(pU[:, :], lk[i][:, s, :], vm[i][:, s, :], start=True, stop=False),
                  reads=[b_in[i]], writes=[b_pU])
            kb.op("pe", lambda e, i=i, s=s: e.matmul(pU[:, :], lb[i][:, s, :], r2[i][0:4, s, :], start=False, stop=True),
                  reads=[b_in[i], b_r2[i]], writes=[b_pU])
            for g in range(2):
                kb.op("dve", lambda e, i=i, s=s, g=g: e.scalar_tensor_tensor(
                    ST[:, g * 64:(g + 1) * 64], ST[:, g * 64:(g + 1) * 64], dc[i][:, s, g:g + 1],
                    pU[:, g * 64:(g + 1) * 64], ALU.mult, ALU.add),
                    reads=[b_ST, b_in[i], b_pU], writes=[b_ST])
        kb.dma("pool", lambda e, i=i, d=yo[:, t0 * 128:(t0 + SBLK) * 128]: e.dma_start(
            out=d, in_=r2[i][4:8, :, :].rearrange("p a b -> p (a b)")), reads=[b_r2[i]], via=b_r2[i])
    return kb.emit()


def run_rwkv_scan(P):
    L = P["o_r"].shape[0]
    Tp = -(-(L + 1) // SBLK) * SBLK
    H = lambda a: a.reshape(L, 32, 64)
    r, kp, v, kap, b, dec = (H(P[n]) for n in ("o_r", "o_kp", "o_v", "o_kap", "o_b", "o_dec"))
    m8 = np.zeros((8, 128), np.float32)
    for j in range(4):
        g = j // 2
        m8[j, g * 64:(g + 1) * 64] = -1.0
        m8[4 + j, g * 64:(g + 1) * 64] = 1.0
    nc = build_rwkv_scan(Tp)
    maps = []
    for c in range(NCORES):
        L1 = np.zeros((128, Tp, 8), np.float32)
        LB = np.zeros((4, Tp, 128), np.float32)
        LK = np.zeros((4, Tp, 128), np.float32)
        VM = np.zeros((4, Tp, 128), np.float32)
        DC = np.ones((128, Tp, 2), np.float32)
        for j in range(4):
            g, h2 = j // 2, j % 2
            hd = 4 * c + j
            ps = slice(h2 * 64, (h2 + 1) * 64)
            L1[ps, 0:L, j] = kap[:, hd, :].T
            L1[ps, 1:L + 1, 4 + j] = r[:, hd, :].T
            LB[j, 0:L, ps] = b[:, hd, :]
            LK[j, 0:L, ps] = kp[:, hd, :]
            VM[j, 0:L, g * 64:(g + 1) * 64] = v[:, hd, :]
            DC[ps, 0:L, g] = dec[:, hd, :].T
        maps.append({"L1": L1.reshape(128, -1), "LB": LB.reshape(4, -1), "LK": LK.reshape(4, -1),
                     "VM": VM.reshape(4, -1), "DC": DC.reshape(128, -1), "M8": m8})
    res = run_bass_kernel_spmd(nc, maps, core_ids=list(range(NCORES)))
    y = np.zeros((L, 32, 64), np.float32)
    for c in range(NCORES):
        yo = res.results[c]["yo"].reshape(4, Tp, 128)
        for j in range(4):
            g = j // 2
            y[:, 4 * c + j, :] = yo[j, 1:L + 1, g * 64:(g + 1) * 64]
    return y.reshape(L, 2048)


A_GN_EPS = 64e-5


def build_rwkv_post(NT128):
    kb = KB()
    names = ("i_y", "i_r", "i_kp", "i_v", "i_g")
    ins = {n: kb.dram(n, [NT128 * 128, AW], F32, kind="ExternalInput") for n in names}
    cst = kb.dram("cst", [3, 128, AW], F32, kind="ExternalInput")
    ya = kb.dram("ya", [NT128 * 128, AW], F32, kind="ExternalOutput")
    cs = kb.sb("cs", [128, 3, AW], F32)
    b_c = Buf("c")
    for i in range(3):
        kb.dma("sp", lambda e, i=i: e.dma_start(out=cs[:, i, :], in_=cst[i, :, :]), writes=[b_c], via=b_c)
    T = {n: kb.sb("t_" + n, [128, AW], F32) for n in names}
    b_T = {n: Buf("t_" + n) for n in names}
    tmp = kb.sb("tmp", [128, AW], F32)
    b_tmp = Buf("tmp")
    hs = kb.sb("hs", [128, 96], F32)
    b_hs = Buf("hs")
    X = mybir.AxisListType.X
    v3 = lambda ap: ap.rearrange("p (h k) -> p h k", k=64)
    bc = lambda ap: ap.unsqueeze(2).to_broadcast([128, 32, 64])
    for j in range(NT128):
        rows = slice(j * 128, (j + 1) * 128)
        for qi, n in enumerate(names):
            kb.dma("sp" if qi % 2 == 0 else "act", lambda e, n=n, s=ins[n][rows, :]: e.dma_start(out=T[n][:, :], in_=s),
                   writes=[b_T[n]], via=b_T[n])
        y, r, kp, v, g = (T[n] for n in names)
        by, br, bkp, bv, bg = (b_T[n] for n in names)
        kb.op("dve", lambda e: e.reduce_sum(hs[:, 0:32], v3(y[:, :]), X), reads=[by], writes=[b_hs])
        kb.op("dve", lambda e: e.tensor_scalar(hs[:, 0:32], hs[:, 0:32], 1.0 / 64, None, ALU.mult), reads=[b_hs], writes=[b_hs])
        kb.op("dve", lambda e: e.tensor_tensor(v3(y[:, :]), v3(y[:, :]), bc(hs[:, 0:32]), ALU.subtract),
              reads=[by, b_hs], writes=[by])
        kb.op("pool", lambda e: e.tensor_tensor(tmp[:, :], y[:, :], y[:, :], ALU.mult), reads=[by], writes=[b_tmp])
        kb.op("dve", lambda e: e.reduce_sum(hs[:, 32:64], v3(tmp[:, :]), X), reads=[b_tmp], writes=[b_hs])
        kb.op("dve", lambda e: e.tensor_scalar(hs[:, 32:64], hs[:, 32:64], 1.0 / 64, A_GN_EPS, ALU.mult, ALU.add),
              reads=[b_hs], writes=[b_hs])
        kb.op("act", lambda e: e.activation(hs[:, 32:64], hs[:, 32:64], AF.Sqrt), reads=[b_hs], writes=[b_hs])
        kb.op("dve", lambda e: e.reciprocal(hs[:, 32:64], hs[:, 32:64]), reads=[b_hs], writes=[b_hs])
        kb.op("dve", lambda e: e.tensor_tensor(v3(y[:, :]), v3(y[:, :]), bc(hs[:, 32:64]), ALU.mult),
              reads=[by, b_hs], writes=[by])
        kb.op("pool", lambda e: e.tensor_tensor(y[:, :], y[:, :], cs[:, 0, :], ALU.mult), reads=[by, b_c], writes=[by])
        kb.op("pool", lambda e: e.tensor_tensor(y[:, :], y[:, :], cs[:, 1, :], ALU.add), reads=[by, b_c], writes=[by])
        kb.op("dve", lambda e: e.tensor_tensor(tmp[:, :], r[:, :], kp[:, :], ALU.mult), reads=[br, bkp, b_tmp], writes=[b_tmp])
        kb.op("dve", lambda e: e.tensor_tensor(tmp[:, :], tmp[:, :], cs[:, 2, :], ALU.mult), reads=[b_tmp, b_c], writes=[b_tmp])
        kb.op("dve", lambda e: e.reduce_sum(hs[:, 64:96], v3(tmp[:, :]), X), reads=[b_tmp], writes=[b_hs])
        kb.op("dve", lambda e: e.tensor_tensor(v3(v[:, :]), v3(v[:, :]), bc(hs[:, 64:96]), ALU.mult),
              reads=[bv, b_hs], writes=[bv])
        kb.op("pool", lambda e: e.tensor_tensor(y[:, :], y[:, :], v[:, :], ALU.add), reads=[by, bv], writes=[by])
        kb.op("pool", lambda e: e.tensor_tensor(y[:, :], y[:, :], g[:, :], ALU.mult), reads=[by, bg], writes=[by])
        kb.dma("pool", lambda e, d=ya[rows, :]: e.dma_start(out=d, in_=y[:, :]), reads=[by], via=by)
    return kb.emit()


def run_rwkv_post(y, P, ln_w, ln_b, r_k):
    L = y.shape[0]
    NT128 = -(-(-(-L // 128)) // NCORES)
    tot = NT128 * NCORES * 128

    def pad(a):
        o = np.zeros((tot, AW), np.float32)
        o[:L] = a
        return o
    arrs = {"i_y": pad(y), "i_r": pad(P["o_r"]), "i_kp": pad(P["o_kp"]), "i_v": pad(P["o_v"]), "i_g": pad(P["o_g"])}
    bc = lambda v: np.ascontiguousarray(np.broadcast_to(v.reshape(1, -1), (128, AW))).astype(np.float32)
    cst = np.stack([bc(ln_w), bc(ln_b), bc(r_k)])
    nc = build_rwkv_post(NT128)
    maps = []
    for c in range(NCORES):
        rr = slice(c * NT128 * 128, (c + 1) * NT128 * 128)
        maps.append(dict({n: a[rr] for n, a in arrs.items()}, cst=cst))
    res = run_bass_kernel_spmd(nc, maps, core_ids=list(range(NCORES)))
    return np.concatenate([res.results[c]["ya"] for c in range(NCORES)], axis=0)[:L]


def build_mixout(D, DA, NT128):
    kb = KB()
    KC, KA = D // 128, DA // 128
    hr = kb.dram("hr", [NT128 * 128, D], F32, kind="ExternalInput")
    yar = kb.dram("yar", [NT128 * 128, DA], F32, kind="ExternalInput")
    ybr = kb.dram("ybr", [NT128 * 128, DA], F32, kind="ExternalInput")
    gr = kb.dram("gr", [NT128 * 128, 2 * D], F32, kind="ExternalInput")
    wpa = kb.dram("wpa", [DA, D], F32, kind="ExternalInput")
    wpb = kb.dram("wpb", [DA, D], F32, kind="ExternalInput")
    wou = kb.dram("wou", [D, D], F32, kind="ExternalInput")
    ident_in = kb.dram("ident_in", [128, 128], F32, kind="ExternalInput")
    h1 = kb.dram("h1", [NT128 * 128, D], F32, kind="ExternalOutput")
    S = Shared(kb, D, ident_in, KC)
    sa, b_sa = emit_cast_tiled(S, wpa, DA, D, "s_wpa")
    sb_, b_sb_ = emit_cast_tiled(S, wpb, DA, D, "s_wpb")
    so, b_so = emit_cast_tiled(S, wou, D, D, "s_wou")
    h_t = kb.sb("h_t", [128, D], F32)
    b_h = Buf("h_t")
    g_t = kb.sb("g_t", [128, 2 * D], F32)
    b_g = Buf("g_t")
    m_t = kb.sb("m_t", [128, D], F32)
    b_m = Buf("m_t")
    ys = kb.sb("ys", [128, DA], F32)
    b_ys = Buf("ys")
    tmp = [kb.sb(f"tmp{i}", [128, 512], F32) for i in range(2)]
    b_tmp = [Buf(f"tmp{i}") for i in range(2)]
    yaT = kb.sb("yaT", [128, KA, 128], BF16)
    ybT = kb.sb("ybT", [128, KA, 128], BF16)
    mT = kb.sb("mT", [128, KC, 128], BF16)
    b_yaT, b_ybT, b_mT = Buf("yaT"), Buf("ybT"), Buf("mT")
    tc_ = [0]
    for j in range(NT128):
        rows = slice(j * 128, (j + 1) * 128)
        kb.dma("pool", lambda e, s=hr[rows, :]: e.dma_start(out=h_t[:, :], in_=s), writes=[b_h], via=b_h)
        kb.dma("pool", lambda e, s=gr[rows, :]: e.dma_start(out=g_t[:, :], in_=s), writes=[b_g], via=b_g)
        for (srcy, dT, b_dT) in ((yar, yaT, b_yaT), (ybr, ybT, b_ybT)):
            kb.dma("pool", lambda e, s=srcy[rows, :]: e.dma_start(out=ys[:, :], in_=s), writes=[b_ys], via=b_ys)
            kb.op("dve", lambda e: e.tensor_copy(S.ub[:, :DA], ys[:, :]), reads=[b_ys], writes=[S.b_ub])
            emit_transposeT(S, DA, dT, b_dT, 0)

        def evac_a(t, nb, ps, b_ps):
            kb.op("dve", lambda e: e.tensor_tensor(m_t[:, nb * 512:(nb + 1) * 512], ps, g_t[:, nb * 512:(nb + 1) * 512],
                                                   ALU.mult), reads=[b_ps, b_g], writes=[b_m])

        def evac_b(t, nb, ps, b_ps):
            i = tc_[0] % 2
            tc_[0] += 1
            kb.op("dve", lambda e: e.tensor_tensor(tmp[i][:, :], ps, g_t[:, D + nb * 512:D + (nb + 1) * 512], ALU.mult),
                  reads=[b_ps, b_g], writes=[b_tmp[i]])
            kb.op("pool", lambda e: e.tensor_tensor(m_t[:, nb * 512:(nb + 1) * 512], m_t[:, nb * 512:(nb + 1) * 512],
                                                    tmp[i][:, :], ALU.add), reads=[b_tmp[i], b_m], writes=[b_m])

        def evac_o(t, nb, ps, b_ps):
            kb.op("dve", lambda e: e.tensor_tensor(h_t[:, nb * 512:(nb + 1) * 512], ps, h_t[:, nb * 512:(nb + 1) * 512],
                                                   ALU.add), reads=[b_ps, b_h], writes=[b_h])
        emit_gemm_tok(S, yaT, b_yaT, KA, sa, b_sa, D // 512, 1, evac_a)
        emit_gemm_tok(S, ybT, b_ybT, KA, sb_, b_sb_, D // 512, 1, evac_b)
        kb.op("dve", lambda e: e.tensor_copy(S.ub[:, :], m_t[:, :]), reads=[b_m], writes=[S.b_ub])
        emit_transposeT(S, D, mT, b_mT, 0)
        emit_gemm_tok(S, mT, b_mT, KC, so, b_so, D // 512, 1, evac_o)
        kb.dma("pool", lambda e, d=h1[rows, :]: e.dma_start(out=d, in_=h_t[:, :]), reads=[b_h], via=b_h)
    return kb.emit()


def run_mixout(hfull, ya, yb, gates, w_proj_a, w_proj_b, w_out):
    L, D = hfull.shape
    DA = ya.shape[1]
    NT128 = -(-(-(-L // 128)) // NCORES)
    tot = NT128 * NCORES * 128

    def pad(a):
        o = np.zeros((tot, a.shape[1]), np.float32)
        o[:L] = a
        return o
    hp, yap, ybp, gp = pad(hfull), pad(ya), pad(yb), pad(gates)
    nc = build_mixout(D, DA, NT128)
    ident = np.eye(128, dtype=np.float32)
    maps = []
    for c in range(NCORES):
        r = slice(c * NT128 * 128, (c + 1) * NT128 * 128)
        maps.append({"hr": hp[r], "yar": yap[r], "ybr": ybp[r], "gr": gp[r],
                     "wpa": np.ascontiguousarray(w_proj_a), "wpb": np.ascontiguousarray(w_proj_b),
                     "wou": np.ascontiguousarray(w_out), "ident_in": ident})
    res = run_bass_kernel_spmd(nc, maps, core_ids=list(range(NCORES)))
    return np.concatenate([res.results[c]["h1"] for c in range(NCORES)], axis=0)[:L]


NEG_ADM = -1.0e30
NEG_SEL = -2.0e30
IDX_C0 = (32 ** -0.5) * (64 ** -0.5)
TOPK = 256


def build_dsa(Lp, NQ):
    kb = KB()
    NKT = Lp // 128
    NBLK = Lp // 64
    ckv = kb.dram("ckv", [Lp, 512], F32, kind="ExternalInput")
    kidx = kb.dram("kidx", [Lp, 64], F32, kind="ExternalInput")
    qr = kb.dram("qr", [NQ * 128, 2048], F32, kind="ExternalInput")
    qir = kb.dram("qir", [NQ * 128, 2048], F32, kind="ExternalInput")
    wir = kb.dram("wir", [NQ * 128, 32], F32, kind="ExternalInput")
    qch = kb.dram("qch", [NQ * 128, 1], F32, kind="ExternalInput")
    wukT = kb.dram("wukT", [512, 2048], F32, kind="ExternalInput")
    wuvR = kb.dram("wuvR", [512, 2048], F32, kind="ExternalInput")
    kvw_b = kb.dram("kvw_b", [128, 512], F32, kind="ExternalInput")
    lnw_b = kb.dram("lnw_b", [128, 64], F32, kind="ExternalInput")
    lnb_b = kb.dram("lnb_b", [128, 64], F32, kind="ExternalInput")
    blk_in = kb.dram("blk_in", [128, NBLK], F32, kind="ExternalInput")
    ident_in = kb.dram("ident_in", [128, 128], F32, kind="ExternalInput")
    yb = kb.dram("yb", [NQ * 128, 2048], F32, kind="ExternalOutput")
    Kscr = kb.dram("Kscr", [4 * NKT * 128, 512], BF16)
    Vscr = kb.dram("Vscr", [4 * NKT * 128, 512], BF16)
    kIscr = kb.dram("kIscr", [128, Lp], BF16)
    b_Kscr, b_Vscr, b_kIscr = Buf("Kscr"), Buf("Vscr"), Buf("kIscr")

    ident_f = kb.sb("ident_f", [128, 128], F32)
    ident4 = kb.sb("ident4", [128, 512], BF16)
    ones_bf = kb.sb("ones_bf", [128, 1], BF16)
    kvw = kb.sb("kvw", [128, 512], F32)
    lnw = kb.sb("lnw", [128, 64], F32)
    lnb = kb.sb("lnb", [128, 64], F32)
    blk = kb.sb("blk", [128, NBLK], F32)
    b_c = Buf("consts")
    for dst, s in ((ident_f, ident_in), (kvw, kvw_b), (lnw, lnw_b), (lnb, lnb_b), (blk, blk_in)):
        kb.dma("sp", lambda e, dst=dst, s=s: e.dma_start(out=dst[:, :], in_=s[:, :]), writes=[b_c], via=b_c)
    b_id = Buf("ident4")
    for a in range(4):
        kb.op("dve", lambda e, a=a: e.tensor_copy(ident4[:, a * 128:(a + 1) * 128], ident_f[:, :]),
              reads=[b_c], writes=[b_id])
    kb.op("pool", lambda e: e.memset(ones_bf[:, :], 1.0), writes=[b_id])
    ident = ident4[:, 0:128]

    big8 = kb.sb("big8", [128, 2048], F32)
    b_big8 = Buf("big8")
    wuk_sb = kb.sb("wuk_sb", [128, 4, 2048], BF16)
    wuv_sb = kb.sb("wuv_sb", [128, 4, 2048], BF16)
    b_wuk, b_wuv = Buf("wuk"), Buf("wuv")
    for (srcw, dstw, bw) in ((wukT, wuk_sb, b_wuk), (wuvR, wuv_sb, b_wuv)):
        for rc in range(4):
            kb.dma("sp", lambda e, s=srcw[rc * 128:(rc + 1) * 128, :]: e.dma_start(out=big8[:, :], in_=s),
                   writes=[b_big8], via=b_big8)
            kb.op("dve", lambda e, dstw=dstw, rc=rc: e.tensor_copy(dstw[:, rc, :], big8[:, :]),
                  reads=[b_big8], writes=[bw])

    pT_ = kb.ps("pT", [128, 1024], BF16)
    b_pT = Buf("pT")
    pI = [kb.ps(f"pI{i}", [128, 1024], F32) for i in range(2)]
    b_pI = [[Buf(f"pI{i}_{h}") for h in range(2)] for i in range(2)]
    pL = [kb.ps(f"pL{i}", [128, 512], F32) for i in range(2)]
    b_pL = [Buf(f"pL{i}") for i in range(2)]
    pR = kb.ps("pR", [128, 512], F32)
    b_pR = Buf("pR")

    ss = kb.sb("ss", [128, 8], F32)
    b_ss = Buf("ss")
    sc = kb.sb("sc", [128, 512], F32)
    b_sc = Buf("sc")

    def rstd_of(src_ap, b_src, width, col, eps):
        kb.op("dve", lambda e: e.tensor_tensor(sc[:, :width], src_ap, src_ap, ALU.mult), reads=[b_src], writes=[b_sc])
        kb.op("dve", lambda e: e.reduce_sum(ss[:, col:col + 1], sc[:, :width], mybir.AxisListType.X),
              reads=[b_sc], writes=[b_ss])
        kb.op("dve", lambda e: e.tensor_scalar(ss[:, col:col + 1], ss[:, col:col + 1], 1.0 / width, eps,
                                               ALU.mult, ALU.add), reads=[b_ss], writes=[b_ss])
        kb.op("act", lambda e: e.activation(ss[:, col:col + 1], ss[:, col:col + 1], AF.Sqrt),
              reads=[b_ss], writes=[b_ss])
        kb.op("dve", lambda e: e.reciprocal(ss[:, col:col + 1], ss[:, col:col + 1]), reads=[b_ss], writes=[b_ss])

    ckt = [kb.sb(f"ckt{i}", [128, 512], F32) for i in range(2)]
    kxt = [kb.sb(f"kxt{i}", [128, 64], F32) for i in range(2)]
    b_ckt = [Buf(f"ckt{i}") for i in range(2)]
    b_kxt = [Buf(f"kxt{i}") for i in range(2)]
    cb = kb.sb("cb", [128, 512], BF16)
    b_cb = Buf("cb")
    cT = [kb.sb(f"cT{i}", [128, 4, 128], BF16) for i in range(2)]
    b_cT = [Buf(f"cT{i}") for i in range(2)]
    kst = [kb.sb(f"kst{i}", [128, 512], BF16) for i in range(3)]
    b_kst = [Buf(f"kst{i}") for i in range(3)]
    xc = kb.sb("xc", [128, 64], F32)
    b_xc = Buf("xc")
    kb2 = kb.sb("kb2", [128, 128], BF16)
    b_kb2 = Buf("kb2")
    kIt = [kb.sb(f"kIt{i}", [128, 128], BF16) for i in range(2)]
    b_kIt = [Buf(f"kIt{i}") for i in range(2)]
    kc = [0]
    lc = [0]
    for st in range(NKT):
        i = st % 2
        kb.dma("sp", lambda e, i=i, s=ckv[st * 128:(st + 1) * 128, :]: e.dma_start(out=ckt[i][:, :], in_=s),
               writes=[b_ckt[i]], via=b_ckt[i])
        kb.dma("act", lambda e, i=i, s=kidx[st * 128:(st + 1) * 128, :]: e.dma_start(out=kxt[i][:, :], in_=s),
               writes=[b_kxt[i]], via=b_kxt[i])
        rstd_of(ckt[i][:, :], b_ckt[i], 512, 0, EPS)
        kb.op("dve", lambda e, i=i: e.scalar_tensor_tensor(cb[:, :], ckt[i][:, :], ss[:, 0:1], kvw[:, :],
                                                            ALU.mult, ALU.mult),
              reads=[b_ckt[i], b_ss, b_c], writes=[b_cb])
        for rc in range(4):
            kb.op("pe", lambda e, rc=rc: e.transpose(pT_[:, rc * 128:(rc + 1) * 128], cb[:, rc * 128:(rc + 1) * 128],
                                                     ident), reads=[b_cb, b_id], writes=[b_pT])
        kb.op("act", lambda e, i=i: e.activation(cT[i][:, :, :].rearrange("p a b -> p (a b)"), pT_[:, 0:512], AF.Copy),
              reads=[b_pT], writes=[b_cT[i]])
        for hg in range(4):
            li = lc[0] % 2
            lc[0] += 1
            for h4 in range(4):
                h = hg * 4 + h4
                for rc in range(4):
                    kb.op("pe", lambda e, li=li, h4=h4, h=h, rc=rc, i=i: e.matmul(
                        pL[li][:, h4 * 128:(h4 + 1) * 128], wuk_sb[:, rc, h * 128:(h + 1) * 128], cT[i][:, rc, :],
                        start=(rc == 0), stop=(rc == 3)), reads=[b_wuk, b_cT[i]], writes=[b_pL[li]])
            ki = kc[0] % 3
            kc[0] += 1
            kb.op("act", lambda e, ki=ki, li=li: e.activation(kst[ki][:, :], pL[li][:, :], AF.Copy),
                  reads=[b_pL[li]], writes=[b_kst[ki]])
            r0 = (hg * NKT + st) * 128
            kb.dma("sp", lambda e, ki=ki, d=Kscr[r0:r0 + 128, :]: e.dma_start(out=d, in_=kst[ki][:, :]),
                   reads=[b_kst[ki]], writes=[b_Kscr], via=b_kst[ki])
        for hg in range(4):
            li = lc[0] % 2
            lc[0] += 1
            for rc in range(4):
                kb.op("pe", lambda e, li=li, hg=hg, rc=rc, i=i: e.matmul(
                    pL[li][:, :], cT[i][:, rc, :], wuv_sb[:, rc, hg * 512:(hg + 1) * 512],
                    start=(rc == 0), stop=(rc == 3)), reads=[b_wuv, b_cT[i]], writes=[b_pL[li]])
            ki = kc[0] % 3
            kc[0] += 1
            kb.op("dve", lambda e, ki=ki, li=li: e.tensor_copy(kst[ki][:, :], pL[li][:, :]),
                  reads=[b_pL[li]], writes=[b_kst[ki]])
            r0 = (hg * NKT + st) * 128
            kb.dma("act", lambda e, ki=ki, d=Vscr[r0:r0 + 128, :]: e.dma_start(out=d, in_=kst[ki][:, :]),
                   reads=[b_kst[ki]], writes=[b_Vscr], via=b_kst[ki])
        kb.op("dve", lambda e, i=i: e.reduce_sum(ss[:, 1:2], kxt[i][:, :], mybir.AxisListType.X),
              reads=[b_kxt[i]], writes=[b_ss])
        kb.op("dve", lambda e: e.tensor_scalar(ss[:, 1:2], ss[:, 1:2], 1.0 / 64, None, ALU.mult),
              reads=[b_ss], writes=[b_ss])
        kb.op("dve", lambda e, i=i: e.tensor_scalar(xc[:, :], kxt[i][:, :], ss[:, 1:2], None, ALU.subtract),
              reads=[b_kxt[i], b_ss], writes=[b_xc])
        rstd_of(xc[:, :], b_xc, 64, 2, 1e-6)
        kb.op("dve", lambda e: e.scalar_tensor_tensor(xc[:, :], xc[:, :], ss[:, 2:3], lnw[:, :], ALU.mult, ALU.mult),
              reads=[b_xc, b_ss, b_c], writes=[b_xc])
        for a in range(2):
            kb.op("dve", lambda e, a=a: e.tensor_tensor(kb2[:, a * 64:(a + 1) * 64], xc[:, :], lnb[:, :], ALU.add),
                  reads=[b_xc, b_c], writes=[b_kb2])
        kb.op("pe", lambda e: e.transpose(pT_[:, 512:640], kb2[:, :], ident), reads=[b_kb2, b_id], writes=[b_pT])
        kb.op("act", lambda e, i=i: e.activation(kIt[i][:, :], pT_[:, 512:640], AF.Copy),
              reads=[b_pT], writes=[b_kIt[i]])
        kb.dma("sp", lambda e, i=i, d=kIscr[:, st * 128:(st + 1) * 128]: e.dma_start(out=d, in_=kIt[i][:, :]),
               reads=[b_kIt[i]], writes=[b_kIscr], via=b_kIt[i])

    score = kb.sb("score", [128, Lp], F32)
    b_score = Buf("score")
    mb = kb.sb("mb", [128, Lp], BF16)
    b_mb = Buf("mb")
    qb16 = kb.sb("qb16", [128, 2048], BF16)
    b_qb16 = Buf("qb16")
    qT = kb.sb("qT", [128, 16, 128], BF16)
    qiT = kb.sb("qiT", [128, 16, 128], BF16)
    b_qT, b_qiT = Buf("qT"), Buf("qiT")
    wq = kb.sb("wq", [128, 32], F32)
    qc = kb.sb("qc", [128, 1], F32)
    b_wq = Buf("wq")
    rl = [kb.sb(f"rl{i}", [128, 1024], F32) for i in range(2)]
    b_rl = [Buf(f"rl{i}") for i in range(2)]
    kIb = [kb.sb(f"kIb{i}", [128, 1024], BF16) for i in range(2)]
    b_kIb = [Buf(f"kIb{i}") for i in range(2)]
    kt = [kb.sb(f"kt{i}", [128, 512], BF16) for i in range(3)]
    vt = [kb.sb(f"vt{i}", [128, 4, 132], BF16) for i in range(3)]
    b_kt = [Buf(f"kt{i}") for i in range(3)]
    b_vt = [Buf(f"vt{i}") for i in range(3)]
    for i in range(3):
        kb.op("pool", lambda e, i=i: e.memset(vt[i][:, :, 128:129], 1.0), writes=[b_vt[i]])
    pTs = [kb.sb(f"pTs{i}", [128, 512], BF16) for i in range(2)]
    b_pTs = [Buf(f"pTs{i}") for i in range(2)]
    mx = kb.sb("mx", [128, 8], F32)
    mx16 = kb.sb("mx16", [128, 16], F32)
    b_mx = Buf("mx")
    rinv = kb.sb("rinv", [128, 4], F32)
    b_rinv = Buf("rinv")
    blkB = blk[:, :].unsqueeze(2).to_broadcast([128, NBLK, 64])
    mb3 = mb[:, :].rearrange("p (a b) -> p a b", b=64)
    po = [(pI[0], 0, b_pI[0][0]), (pI[0], 1, b_pI[0][1]), (pI[1], 0, b_pI[1][0]), (pI[1], 1, b_pI[1][1])]
    scale = 128 ** -0.5
    cnt = [0]
    for j in range(NQ):
        rows = slice(j * 128, (j + 1) * 128)
        kb.dma("pool", lambda e, s=wir[rows, :]: e.dma_start(out=wq[:, :], in_=s), writes=[b_wq], via=b_wq)
        kb.dma("pool", lambda e, s=qch[rows, :]: e.dma_start(out=qc[:, :], in_=s), writes=[b_wq], via=b_wq)
        kb.op("dve", lambda e: e.tensor_scalar(wq[:, :], wq[:, :], IDX_C0, None, ALU.mult), reads=[b_wq], writes=[b_wq])
        for (srcq, dstT, b_dst) in ((qr, qT, b_qT), (qir, qiT, b_qiT)):
            kb.dma("pool", lambda e, s=srcq[rows, :]: e.dma_start(out=big8[:, :], in_=s), writes=[b_big8], via=b_big8)
            kb.op("dve", lambda e: e.tensor_copy(qb16[:, :], big8[:, :]), reads=[b_big8], writes=[b_qb16])
            for b8 in range(2):
                for k in range(8):
                    kb.op("pe", lambda e, b8=b8, k=k: e.transpose(
                        pT_[:, k * 128:(k + 1) * 128], qb16[:, (b8 * 8 + k) * 128:(b8 * 8 + k + 1) * 128], ident),
                        reads=[b_qb16, b_id], writes=[b_pT])
                kb.op("act", lambda e, b8=b8, dstT=dstT: e.activation(
                    dstT[:, b8 * 8:(b8 + 1) * 8, :].rearrange("p a b -> p (a b)"), pT_[:, :], AF.Copy),
                    reads=[b_pT], writes=[b_dst])
        kb.op("pool", lambda e: e.memset(score[:, :], 0.0), writes=[b_score])
        for k0 in range(0, Lp, 1024):
            wd = min(1024, Lp - k0)
            gi_ = (k0 // 1024) % 2
            kb.dma("sp", lambda e, gi_=gi_, wd=wd, s=kIscr[:, k0:k0 + wd]: e.dma_start(out=kIb[gi_][:, :wd], in_=s),
                   reads=[b_kIscr], writes=[b_kIb[gi_]], via=b_kIb[gi_])
            for h in range(32):
                pi = h % 2
                half = (h % 2) * 64
                hp = h // 2
                for c0 in range(0, wd, 512):
                    cw_ = min(512, wd - c0)
                    kb.op("pe", lambda e, pi=pi, half=half, hp=hp, c0=c0, cw_=cw_, gi_=gi_: e.matmul(
                        pI[pi][:, c0:c0 + cw_], qiT[half:half + 64, hp, :], kIb[gi_][half:half + 64, c0:c0 + cw_],
                        start=True, stop=True), reads=[b_qiT, b_kIb[gi_]], writes=[b_pI[pi][c0 // 512]])
                kb.op("act", lambda e, pi=pi, wd=wd: e.activation(rl[pi][:, :wd], pI[pi][:, :wd], AF.Relu),
                      reads=b_pI[pi][:(wd + 511) // 512], writes=[b_rl[pi]])
                kb.op("dve", lambda e, pi=pi, wd=wd, k0=k0, h=h: e.scalar_tensor_tensor(
                    score[:, k0:k0 + wd], rl[pi][:, :wd], wq[:, h:h + 1], score[:, k0:k0 + wd], ALU.mult, ALU.add),
                    reads=[b_rl[pi], b_wq, b_score], writes=[b_score])
        kb.op("dve", lambda e: e.tensor_scalar(mb3, blkB, qc[:, 0:1], None, ALU.is_le), reads=[b_c, b_wq], writes=[b_mb])
        kb.op("pool", lambda e: e.memset(mb[:, 0:48], 0.0), writes=[b_mb])
        kb.op("dve", lambda e: e.tensor_tensor(score[:, :], score[:, :], mb[:, :], ALU.mult),
              reads=[b_score, b_mb], writes=[b_score])
        kb.op("dve", lambda e: e.tensor_scalar(mb[:, :], mb[:, :], -NEG_ADM, NEG_ADM, ALU.mult, ALU.add),
              reads=[b_mb], writes=[b_mb])
        kb.op("dve", lambda e: e.tensor_tensor(score[:, :], score[:, :], mb[:, :], ALU.add),
              reads=[b_score, b_mb], writes=[b_score])
        HA = Lp if Lp <= 16384 else 8192
        for r_ in range(TOPK // 8):
            if HA == Lp:
                kb.op("dve", lambda e: e.max(out=mx[:, :], in_=score[:, :]), reads=[b_score], writes=[b_mx])
            else:
                kb.op("dve", lambda e: e.max(out=mx16[:, 0:8], in_=score[:, :HA]), reads=[b_score], writes=[b_mx])
                kb.op("dve", lambda e: e.max(out=mx16[:, 8:16], in_=score[:, HA:]), reads=[b_score], writes=[b_mx])
                kb.op("dve", lambda e: e.max(out=mx[:, :], in_=mx16[:, :]), reads=[b_mx], writes=[b_mx])
            for (a0_, a1_) in ((0, HA), (HA, Lp)):
                if a1_ > a0_:
                    kb.op("dve", lambda e, a0_=a0_, a1_=a1_: e.match_replace(
                        out=score[:, a0_:a1_], in_to_replace=mx[:, :], in_values=score[:, a0_:a1_],
                        imm_value=NEG_SEL), reads=[b_mx, b_score], writes=[b_score])
        kb.op("dve", lambda e: e.tensor_scalar(mb[:, :], score[:, :], 1.5 * NEG_ADM, None, ALU.is_le),
              reads=[b_score], writes=[b_mb])
        kb.op("dve", lambda e: e.scalar_tensor_tensor(mb3, blkB, qc[:, 0:1], mb3, ALU.is_le, ALU.mult),
              reads=[b_c, b_wq, b_mb], writes=[b_mb])
        kb.op("pool", lambda e: e.memset(mb[:, 0:48], 0.0), writes=[b_mb])
        kb.op("dve", lambda e: e.tensor_scalar(mb[:, :], mb[:, :], 1.0, 30000.0, ALU.subtract, ALU.mult),
              reads=[b_mb], writes=[b_mb])
        for hg in range(4):
            for st in range(NKT):
                ti = cnt[0] % 3
                li = cnt[0] % 2
                cnt[0] += 1
                r0 = (hg * NKT + st) * 128
                kb.dma("sp", lambda e, ti=ti, s=Kscr[r0:r0 + 128, :]: e.dma_start(out=kt[ti][:, :], in_=s),
                       reads=[b_Kscr], writes=[b_kt[ti]], via=b_kt[ti])
                kb.dma("act", lambda e, ti=ti, s=Vscr[r0:r0 + 128, :]: e.dma_start(
                    out=vt[ti][:, :, 0:128], in_=s.rearrange("p (a b) -> p a b", a=4)),
                    reads=[b_Vscr], writes=[b_vt[ti]], via=b_vt[ti])
                kb.op("pe", lambda e, li=li, st=st: e.matmul(pL[li][:, :], mb[:, st * 128:(st + 1) * 128], ident4[:, :],
                                                             start=True, stop=False, skip_group_check=True),
                      reads=[b_mb, b_id], writes=[b_pL[li]])
                for h4 in range(4):
                    kb.op("pe", lambda e, li=li, h4=h4, ti=ti, hg=hg: e.matmul(
                        pL[li][:, h4 * 128:(h4 + 1) * 128], kt[ti][:, h4 * 128:(h4 + 1) * 128], qT[:, hg * 4 + h4, :],
                        start=False, stop=True, skip_group_check=True),
                        reads=[b_kt[ti], b_qT], writes=[b_pL[li]])
                kb.op("act", lambda e, li=li: e.activation(pTs[li][:, :], pL[li][:, :], AF.Exp, scale=scale),
                      reads=[b_pL[li]], writes=[b_pTs[li]])
                for h4 in range(4):
                    pt, hf, bpo = po[h4]
                    kb.op("pe", lambda e, li=li, h4=h4, ti=ti, pt=pt, hf=hf, st=st: e.matmul(
                        pt[:, hf * 512:hf * 512 + 129], pTs[li][:, h4 * 128:(h4 + 1) * 128],
                        vt[ti][:, h4, 0:129], start=(st == 0), stop=(st == NKT - 1),
                        skip_group_check=True), reads=[b_pTs[li], b_vt[ti]], writes=[bpo])
            for h4 in range(4):
                pt, hf, bpo = po[h4]
                h = hg * 4 + h4
                kb.op("dve", lambda e, pt=pt, hf=hf, h4=h4: e.reciprocal(
                    rinv[:, h4:h4 + 1], pt[:, hf * 512 + 128:hf * 512 + 129]), reads=[bpo], writes=[b_rinv])
                kb.op("dve", lambda e, pt=pt, hf=hf, h4=h4, h=h: e.tensor_scalar(
                    big8[:, h * 128:(h + 1) * 128], pt[:, hf * 512:hf * 512 + 128], rinv[:, h4:h4 + 1], None, ALU.mult),
                    reads=[bpo, b_rinv], writes=[b_big8])
        kb.dma("pool", lambda e, d=yb[rows, :]: e.dma_start(out=d, in_=big8[:, :]), reads=[b_big8], via=b_big8)
    return kb.emit()


def run_dsa(zb, kv_norm_w, w_uk, w_uv, idx_ln_w, idx_ln_b):
    L = zb.shape[0]
    q, ckv, qi, ki, wi = np.split(zb, np.cumsum([2048, 512, 2048, 64])[:4], axis=1)
    Lp = -(-(48 + L) // 128) * 128
    NQT = -(-L // 128)
    NQ = -(-NQT // NCORES)
    ckv_p = np.zeros((Lp, 512), np.float32)
    ckv_p[48:48 + L] = ckv
    ki_p = np.zeros((Lp, 64), np.float32)
    ki_p[48:48 + L] = ki
    pos = np.arange(NQ * NCORES * 128)
    chunk = np.where(pos < N_META, 0, 1 + (pos - N_META) // 64).astype(np.float32)

    def pad_rows(a):
        o = np.zeros((NQ * NCORES * 128, a.shape[1]), np.float32)
        o[:L] = a
        return o
    qp, qip, wip = pad_rows(q), pad_rows(qi), pad_rows(wi)
    nc = build_dsa(Lp, NQ)
    common = {
        "ckv": ckv_p, "kidx": ki_p,
        "wukT": np.ascontiguousarray(w_uk.transpose(1, 0, 2).reshape(512, 2048)),
        "wuvR": np.ascontiguousarray(w_uv.transpose(1, 0, 2).reshape(512, 2048)),
        "kvw_b": np.ascontiguousarray(np.broadcast_to(kv_norm_w[None, :], (128, 512))),
        "lnw_b": np.ascontiguousarray(np.broadcast_to(idx_ln_w[None, :], (128, 64))),
        "lnb_b": np.ascontiguousarray(np.broadcast_to(idx_ln_b[None, :], (128, 64))),
        "blk_in": np.ascontiguousarray(np.broadcast_to(np.arange(Lp // 64, dtype=np.float32)[None, :], (128, Lp // 64))),
        "ident_in": np.eye(128, dtype=np.float32),
    }
    maps = []
    for c in range(NCORES):
        r = slice(c * NQ * 128, (c + 1) * NQ * 128)
        maps.append(dict(common, qr=qp[r], qir=qip[r], wir=wip[r], qch=chunk[r, None].copy()))
    res = run_bass_kernel_spmd(nc, maps, core_ids=list(range(NCORES)))
    return np.concatenate([res.results[c]["yb"] for c in range(NCORES)], axis=0)[:L]


def _host_layout(x, meta_tokens, norm_ffn_w, w_ffn_in, ffn_conv_w, ffn_conv_b, w_ffn_out, norm_final_w, NG, use_cc=True):
    D = x.shape[-1]
    DFF = w_ffn_out.shape[1]
    KC, FC = D // 128, DFF // 128
    hfull = x if meta_tokens is None else np.concatenate([meta_tokens.astype(np.float32), x[0]], axis=0)
    L = hfull.shape[0]
    maps = []
    cwc = np.zeros((128, 2 * FC * 4), np.float32)
    cw3 = ffn_conv_w[0]
    for half in range(2):
        for fc in range(FC):
            cols = slice(half * DFF + fc * 128, half * DFF + (fc + 1) * 128)
            c4 = (half * FC + fc) * 4
            cwc[:, c4 + 0] = cw3[0, cols]
            cwc[:, c4 + 1] = cw3[1, cols]
            cwc[:, c4 + 2] = cw3[2, cols]
            cwc[:, c4 + 3] = ffn_conv_b[0, cols]
    nfw_col = np.ascontiguousarray(norm_ffn_w[0].reshape(KC, 128).T)
    nfin_b = np.ascontiguousarray(np.broadcast_to(norm_final_w[None, :], (128, D)))
    ident = np.eye(128, dtype=np.float32)
    for c in range(NCORES):
        rows = np.zeros((NG * TB, D), np.float32)
        for j in range(NG):
            gidx = c * NG + j
            p0 = N_META + gidx * OWN - 2
            p1 = min(p0 + TB, L)
            if p1 > p0:
                rows[j * TB:j * TB + (p1 - p0)] = hfull[p0:p1]
        KR = D // NCORES
        if use_cc:
            wi_c = np.ascontiguousarray(w_ffn_in[0][c * KR:(c + 1) * KR, :])
            wo_c = np.ascontiguousarray(w_ffn_out[0][:, c * KR:(c + 1) * KR])
        else:
            wi_c = np.ascontiguousarray(w_ffn_in[0])
            wo_c = np.ascontiguousarray(np.concatenate(
                [w_ffn_out[0][:, r * KR:(r + 1) * KR] for r in range(NCORES)], axis=0))
        maps.append({
            "xt": rows,
            "wi_sh": wi_c,
            "wo_sh": wo_c,
            "nfw_col": nfw_col, "nfin_b": nfin_b, "cw_col": cwc, "ident_in": ident,
        })
    return maps


def run_ffn(x, meta_tokens, norm_ffn_w, w_ffn_in, ffn_conv_w, ffn_conv_b, w_ffn_out, norm_final_w, use_cc=True):
    if meta_tokens is None:
        S, D = x.shape[0] - N_META, x.shape[1]
    else:
        S, D = x.shape[1], x.shape[2]
    DFF = w_ffn_out.shape[1]
    n_groups = -(-S // OWN)
    NG = -(-n_groups // NCORES)
    nc = build_program(D, DFF, NG, use_cc)
    maps = _host_layout(x, meta_tokens, norm_ffn_w, w_ffn_in, ffn_conv_w, ffn_conv_b, w_ffn_out,
                        norm_final_w, NG, use_cc)
    res = run_bass_kernel_spmd(nc, maps, core_ids=list(range(NCORES)))
    y = np.concatenate([res.results[c]["out"] for c in range(NCORES)], axis=0)[:S]
    return y[None].astype(np.float32)


A_COLS = 6592


def kernel(x, meta_tokens, norm_mix_w, w_in, mu_shift, rwkv_w0, rwkv_w2, rwkv_a0, rwkv_a2,
           rwkv_g2, rwkv_k_k, rwkv_k_a, rwkv_r_k, rwkv_ln_w, rwkv_ln_b, kv_norm_w, w_uk, w_uv,
           idx_ln_w, idx_ln_b, w_proj_a, w_proj_b, w_gate, w_out, norm_ffn_w, w_ffn_in,
           ffn_conv_w, ffn_conv_b, w_ffn_out, norm_final_w):
    f = lambda a: np.asarray(a, np.float32)
    hfull = np.concatenate([f(meta_tokens), f(x)[0]], axis=0)
    z, gates = run_mixin(hfull, f(norm_mix_w)[0], f(w_in)[0], f(w_gate)[0])
    yb = run_dsa(np.ascontiguousarray(z[:, A_COLS:]), f(kv_norm_w)[0], f(w_uk)[0], f(w_uv)[0],
                 f(idx_ln_w)[0], f(idx_ln_b)[0])
    P = run_rwkv_prep(np.ascontiguousarray(z[:, :A_COLS]), f(mu_shift)[0], f(rwkv_w0)[0], f(rwkv_w2)[0],
                      f(rwkv_a0)[0], f(rwkv_a2)[0], f(rwkv_g2)[0], f(rwkv_k_k)[0], f(rwkv_k_a)[0])
    yscan = run_rwkv_scan(P)
    ya = run_rwkv_post(yscan, P, f(rwkv_ln_w)[0], f(rwkv_ln_b)[0], f(rwkv_r_k)[0])
    del P, yscan
    h1 = run_mixout(hfull, ya, yb, gates, f(w_proj_a)[0], f(w_proj_b)[0], f(w_out)[0])
    return run_ffn(h1, None, f(norm_ffn_w), f(w_ffn_in), f(ffn_conv_w), f(ffn_conv_b), f(w_ffn_out),
                   f(norm_final_w), use_cc=False)
```

```python
from contextlib import ExitStack

import numpy as np
import concourse.bass as bass
import concourse.mybir as mybir
from concourse.bass_utils import run_bass_kernel_spmd

F32 = mybir.dt.float32
BF16 = mybir.dt.bfloat16
AF = mybir.ActivationFunctionType
ALU = mybir.AluOpType

NCORES = 8
N_META = 16
EPS = 1e-6
TB = 256
OWN = TB - 2


class Buf:
    __slots__ = ("name", "w", "rd", "dsem", "dcnt")

    def __init__(self, name):
        self.name = name
        self.w = None
        self.rd = {}
        self.dsem = None
        self.dcnt = 0


class KB:
    ENG = ("pe", "act", "dve", "pool", "sp")

    def __init__(self):
        self.nc = bass.Bass("TRN2", target_bir_lowering=False)
        self.st = ExitStack()
        self.ops = {e: [] for e in self.ENG}
        self.cnt = {e: 0 for e in self.ENG}
        self.semnames = []
        self.csem = {e: self._newsem(e) for e in ("pe", "act", "dve", "pool")}
        self.abufs = []
        self.cfinal = {}

    def _newsem(self, name):
        self.semnames.append(name)
        return len(self.semnames) - 1

    def sb(self, name, shape, dt):
        return self.st.enter_context(self.nc.sbuf_tensor(name, list(shape), dt))

    def ps(self, name, shape, dt=F32):
        return self.st.enter_context(self.nc.psum_tensor(name, list(shape), dt))

    def dram(self, name, shape, dt, kind="Internal"):
        if kind == "Internal":
            return self.nc.dram_tensor(name, list(shape), dt)
        return self.nc.dram_tensor(name, list(shape), dt, kind=kind)

    def _deps(self, reads, writes, own):
        waits = []
        for b in list(reads) + list(writes):
            if b.w is not None and b.w[0] != own:
                waits.append(b.w)
        for b in writes:
            for s, v in b.rd.items():
                if s != own:
                    waits.append((s, v))
        return waits

    def _commit(self, reads, writes, ev):
        for b in reads:
            if b.rd.get(ev[0], 0) < ev[1]:
                b.rd[ev[0]] = ev[1]
        for b in writes:
            b.w = ev
            b.rd = {}

    SEM_EPOCH = 48000

    def op(self, eng, fn, reads=(), writes=()):
        if self.cnt[eng] >= self.SEM_EPOCH:
            self.csem[eng] = self._newsem(f"{eng}_e{len(self.semnames)}")
            self.cnt[eng] = 0
        own = self.csem[eng]
        waits = self._deps(reads, writes, own if eng == "pe" else -1)
        self.cnt[eng] += 1
        ev = (own, self.cnt[eng])
        self.cfinal[own] = self.cnt[eng]
        self._commit(reads, writes, ev)
        self.ops[eng].append((fn, waits, (own, 1)))

    def dma(self, q, fn, reads=(), writes=(), via=None, inc=16):
        if via.dsem is None:
            via.dsem = self._newsem("d_" + via.name)
            self.abufs.append(via)
        waits = self._deps(reads, writes, -1)
        via.dcnt += inc
        ev = (via.dsem, via.dcnt)
        self._commit(reads, writes, ev)
        self.ops[q].append((fn, waits, (via.dsem, inc)))

    def emit(self):
        nc = self.nc
        with ExitStack() as st:
            sems = [st.enter_context(nc.semaphore(n)) for n in self.semnames]
            block = st.enter_context(nc.Block())
            final = [(b.dsem, b.dcnt) for b in self.abufs]

            def mk(name):
                def body(e):
                    waited = {}
                    for fn, waits, inc in self.ops[name]:
                        for s, v in waits:
                            if waited.get(s, 0) < v:
                                e.wait_ge(sems[s], v)
                                waited[s] = v
                        ins = fn(e)
                        if inc[1] == 1:
                            ins.then_inc(sems[inc[0]])
                        else:
                            ins.then_inc(sems[inc[0]], inc[1])
                    if name == "sp":
                        for s, v in final:
                            if waited.get(s, 0) < v:
                                e.wait_ge(sems[s], v)
                        for s, v in self.cfinal.items():
                            e.wait_ge(sems[s], v)
                return body

            block.tensor(mk("pe"))
            block.scalar(mk("act"))
            block.vector(mk("dve"))
            block.gpsimd(mk("pool"))
            block.sync(mk("sp"))
        self.st.close()
        return nc


def build_program(D, DFF, NG, use_cc=True):
    kb = KB()
    nc = kb.nc
    KC = D // 128
    FC = DFF // 128
    KR = D // NCORES
    NBI = 2
    CBI = 2 * DFF // NBI
    NRW = D // NCORES
    NBO = 2
    RBO = DFF // NBO
    assert KR % 128 == 0 and CBI % 128 == 0 and NRW == 512 or True

    xt = kb.dram("xt", [NG * TB, D], F32, kind="ExternalInput")
    NR = 1 if use_cc else NCORES
    wi_sh = kb.dram("wi_sh", [NR * KR, 2 * DFF], F32, kind="ExternalInput")
    wo_sh = kb.dram("wo_sh", [NR * DFF, NRW], F32, kind="ExternalInput")
    nfw_col = kb.dram("nfw_col", [128, KC], F32, kind="ExternalInput")
    nfin_b = kb.dram("nfin_b", [128, D], F32, kind="ExternalInput")
    cw_col = kb.dram("cw_col", [128, 2 * FC * 4], F32, kind="ExternalInput")
    ident_in = kb.dram("ident_in", [128, 128], F32, kind="ExternalInput")
    out = kb.dram("out", [NG * OWN, D], F32, kind="ExternalOutput")

    KK = KR // 128
    assert FC % NBI == 0
    FCB = FC // NBI
    bi = [kb.dram(f"bi{q}", [FCB * 128, KK * 256], BF16) for q in range(NBI)]
    gi = [kb.dram(f"gi{q}", [NCORES * FCB * 128, KK * 256], BF16) for q in range(NBI)]
    bo = [kb.dram(f"bo{q}", [RBO, NRW], BF16) for q in range(NBO)]
    go = [kb.dram(f"go{q}", [NCORES * RBO, NRW], BF16) for q in range(NBO)]
    b_bi = [Buf(f"bi{q}") for q in range(NBI)]
    b_gi = [Buf(f"gi{q}") for q in range(NBI)]
    b_bo = [Buf(f"bo{q}") for q in range(NBO)]
    b_go = [Buf(f"go{q}") for q in range(NBO)]

    ident_f = kb.sb("ident_f", [128, 128], F32)
    ident = kb.sb("ident", [128, 128], BF16)
    nfw = kb.sb("nfw", [128, KC], F32)
    nfin = kb.sb("nfin", [128, D], F32)
    cw = kb.sb("cw", [128, 2 * FC * 4], F32)
    b_const = Buf("const")
    kb.dma("sp", lambda e: e.dma_start(out=ident_f[:, :], in_=ident_in[:, :]), writes=[b_const], via=b_const)
    kb.dma("sp", lambda e: e.dma_start(out=nfw[:, :], in_=nfw_col[:, :]), writes=[b_const], via=b_const)
    kb.dma("sp", lambda e: e.dma_start(out=nfin[:, :], in_=nfin_b[:, :]), writes=[b_const], via=b_const)
    kb.dma("sp", lambda e: e.dma_start(out=cw[:, :], in_=cw_col[:, :]), writes=[b_const], via=b_const)
    b_ident = Buf("ident")
    kb.op("dve", lambda e: e.tensor_copy(ident[:, :], ident_f[:, :]), reads=[b_const], writes=[b_ident])

    CW = 1024
    stg_f = [kb.sb(f"stg_f{i}", [128, CW], F32) for i in range(2)]
    stg_b = [kb.sb(f"stg_b{i}", [128, CW], BF16) for i in range(2)]
    b_sf = [Buf(f"stg_f{i}") for i in range(2)]
    b_sb = [Buf(f"stg_b{i}") for i in range(2)]
    it = [0]

    def cast_piece(src_ap, dst_ap, dst_buf, rows, cols, nf=None):
        i = it[0] % 2
        it[0] += 1
        q = "sp" if (it[0] % 2) else "act"
        kb.dma(q, lambda e: e.dma_start(out=stg_f[i][:rows, :cols], in_=src_ap), writes=[b_sf[i]], via=b_sf[i])
        eng = "dve" if (it[0] % 2) else "pool"
        kb.op(eng, lambda e: e.tensor_copy(stg_b[i][:rows, :cols], stg_f[i][:rows, :cols]),
              reads=[b_sf[i]], writes=[b_sb[i]])
        sview = stg_b[i][:rows, :cols]
        if nf is not None:
            sview = sview.rearrange("p (f n) -> p f n", f=nf)
        kb.dma("sp", lambda e: e.dma_start(out=dst_ap, in_=sview),
               reads=[b_sb[i]], writes=[dst_buf], via=b_sb[i])

    for q in range(NBI):
        for r in range(NR):
            if use_cc:
                bv = bi[q].ap().rearrange("(f p) (k h n) -> p f k h n", p=128, k=KK, h=2)
                dbuf = b_bi[q]
            else:
                bv = gi[q].ap().rearrange("(r f p) (k h n) -> r p f k h n", r=NCORES, p=128, k=KK, h=2)[r]
                dbuf = b_gi[q]
            for kk in range(KK):
                for half in range(2):
                    for f0 in range(0, FCB, CW // 128):
                        nf = min(CW // 128, FCB - f0)
                        c0 = half * DFF + (q * FCB + f0) * 128
                        cast_piece(wi_sh[r * KR + kk * 128:r * KR + (kk + 1) * 128, c0:c0 + nf * 128],
                                   bv[:, f0:f0 + nf, kk, half, :], dbuf, 128, nf * 128, nf)
        if use_cc:
            kb.dma("pool", (lambda q: lambda e: e.collective_compute(
                "AllGather", ALU.bypass, replica_groups=[list(range(NCORES))],
                ins=[bi[q].ap().opt()], outs=[gi[q].ap().opt()]))(q),
                reads=[b_bi[q]], writes=[b_gi[q]], via=b_gi[q], inc=1)
    for q in range(NBO):
        for r in range(NR):
            for r0 in range(0, RBO, 128):
                rr = min(128, RBO - r0)
                if use_cc:
                    dst, dbuf = bo[q][r0:r0 + rr, :], b_bo[q]
                else:
                    dst, dbuf = go[q][r * RBO + r0:r * RBO + r0 + rr, :], b_go[q]
                cast_piece(wo_sh[r * DFF + q * RBO + r0:r * DFF + q * RBO + r0 + rr, :], dst, dbuf, rr, NRW)
        if use_cc:
            kb.dma("pool", (lambda q: lambda e: e.collective_compute(
                "AllGather", ALU.bypass, replica_groups=[list(range(NCORES))],
                ins=[bo[q].ap().opt()], outs=[go[q].ap().opt()]))(q),
                reads=[b_bo[q]], writes=[b_go[q]], via=b_go[q], inc=1)

    NT = TB // 128
    h_t = [kb.sb(f"h{t}", [128, D], F32) for t in range(NT)]
    b_h = [Buf(f"h{t}") for t in range(NT)]
    ub = kb.sb("ub", [128, D], BF16)
    b_ub = Buf("ub")
    ss = kb.sb("ss", [128, 2], F32)
    b_ss = Buf("ss")
    uT = kb.sb("uT", [128, KC, TB], BF16)
    b_uT = Buf("uT")
    hid = kb.sb("hid", [128, FC, TB], BF16)
    b_hid = Buf("hid")
    wi_t = [kb.sb(f"wi_t{i}", [128, KC, 256], BF16) for i in range(2)]
    b_wi = [Buf(f"wi_t{i}") for i in range(2)]
    wo_t = [kb.sb(f"wo_t{i}", [128, 2048], BF16) for i in range(3)]
    b_wo = [Buf(f"wo_t{i}") for i in range(3)]
    yf = [kb.sb(f"yf{i}", [128, 2, TB], F32) for i in range(2)]
    b_yf = [Buf(f"yf{i}") for i in range(2)]
    zf = [kb.sb(f"zf{i}", [128, 2, TB], F32) for i in range(2)]
    b_zf = [Buf(f"zf{i}") for i in range(2)]
    p_tr = [kb.ps(f"p_tr{i}", [128, 1024], BF16) for i in range(2)]
    b_ptr = [Buf(f"p_tr{i}") for i in range(2)]
    p_y = [kb.ps(f"p_y{i}", [128, 2, TB], F32) for i in range(2)]
    b_py = [Buf(f"p_y{i}") for i in range(2)]
    p_o = kb.ps("p_o", [128, 4, 512], F32)
    b_po = [Buf(f"p_o{i}") for i in range(4)]

    def rms_rstd(src_t, sbuf, col):
        kb.op("dve", lambda e, src_t=src_t: e.tensor_tensor(ub[:, :], src_t[:, :], src_t[:, :], ALU.mult),
              reads=[sbuf], writes=[b_ub])
        kb.op("dve", lambda e, col=col: e.reduce_sum(ss[:, col:col + 1], ub[:, :], mybir.AxisListType.X),
              reads=[b_ub], writes=[b_ss])
        kb.op("dve", lambda e, col=col: e.tensor_scalar(ss[:, col:col + 1], ss[:, col:col + 1], 1.0 / D, EPS,
                                                        ALU.mult, ALU.add), reads=[b_ss], writes=[b_ss])
        kb.op("act", lambda e, col=col: e.activation(ss[:, col:col + 1], ss[:, col:col + 1], AF.Sqrt),
              reads=[b_ss], writes=[b_ss])
        kb.op("dve", lambda e, col=col: e.reciprocal(ss[:, col:col + 1], ss[:, col:col + 1]),
              reads=[b_ss], writes=[b_ss])

    wcount = [0]
    for g in range(NG):
        for t in range(NT):
            kb.dma("pool", lambda e, dst=h_t[t][:, :], s=xt[g * TB + t * 128:g * TB + (t + 1) * 128, :]:
                   e.dma_start(out=dst, in_=s), writes=[b_h[t]], via=b_h[t])
        for t in range(NT):
            rms_rstd(h_t[t], b_h[t], 0)
            kb.op("dve", lambda e, t=t: e.tensor_scalar(ub[:, :], h_t[t][:, :], ss[:, 0:1], None, ALU.mult),
                  reads=[b_h[t], b_ss], writes=[b_ub])
            for k8 in range(0, KC, 8):
                pi = (k8 // 8) % 2
                nk = min(8, KC - k8)
                for k in range(k8, k8 + nk):
                    kb.op("pe", lambda e, k=k, pi=pi: e.transpose(
                        p_tr[pi][:, (k % 8) * 128:(k % 8 + 1) * 128], ub[:, k * 128:(k + 1) * 128], ident[:, :]),
                        reads=[b_ub, b_ident], writes=[b_ptr[pi]])
                for k in range(k8, k8 + nk):
                    kb.op("dve", lambda e, k=k, pi=pi, t=t: e.tensor_scalar(
                        uT[:, k, t * 128:(t + 1) * 128], p_tr[pi][:, (k % 8) * 128:(k % 8 + 1) * 128],
                        nfw[:, k:k + 1], None, ALU.mult),
                        reads=[b_ptr[pi], b_const], writes=[b_uT])
        for fc in range(FC):
            wi = wcount[0] % 2
            wcount[0] += 1
            qi, fl = divmod(fc, FCB)
            gvi = gi[qi].ap().rearrange("(r f p) (k c) -> f p r k c", r=NCORES, p=128, k=KK)
            kb.dma("sp" if fc % 2 == 0 else "act",
                   lambda e, s=gvi[fl], dst=wi_t[wi][:, :, :].rearrange("p (r k) c -> p r k c", r=NCORES):
                   e.dma_start(out=dst, in_=s),
                   reads=[b_gi[qi]], writes=[b_wi[wi]], via=b_wi[wi])
            pi = fc % 2
            for half in range(2):
                for k in range(KC):
                    kb.op("pe", lambda e, k=k, half=half, wi=wi, pi=pi: e.matmul(
                        p_y[pi][:, half, :], wi_t[wi][:, k, half * 128:(half + 1) * 128], uT[:, k, :],
                        start=(k == 0), stop=(k == KC - 1)),
                        reads=[b_wi[wi], b_uT], writes=[b_py[pi]])
            yi = fc % 2
            kb.op("act", lambda e, pi=pi, yi=yi: e.activation(yf[yi][:, :, :], p_y[pi][:, :, :], AF.Copy),
                  reads=[b_py[pi]], writes=[b_yf[yi]])
            for half in range(2):
                c4 = (half * FC + fc) * 4
                kb.op("dve", lambda e, half=half, yi=yi, c4=c4: e.tensor_scalar(
                    zf[yi][:, half, :], yf[yi][:, half, :], cw[:, c4 + 2:c4 + 3], cw[:, c4 + 3:c4 + 4],
                    ALU.mult, ALU.add), reads=[b_yf[yi], b_const], writes=[b_zf[yi]])
                kb.op("dve", lambda e, half=half, yi=yi, c4=c4: e.scalar_tensor_tensor(
                    zf[yi][:, half, 1:TB], yf[yi][:, half, 0:TB - 1], cw[:, c4 + 1:c4 + 2], zf[yi][:, half, 1:TB],
                    ALU.mult, ALU.add), reads=[b_yf[yi], b_const, b_zf[yi]], writes=[b_zf[yi]])
                kb.op("dve", lambda e, half=half, yi=yi, c4=c4: e.scalar_tensor_tensor(
                    zf[yi][:, half, 2:TB], yf[yi][:, half, 0:TB - 2], cw[:, c4:c4 + 1], zf[yi][:, half, 2:TB],
                    ALU.mult, ALU.add), reads=[b_yf[yi], b_const, b_zf[yi]], writes=[b_zf[yi]])
            kb.op("act", lambda e, yi=yi: e.activation(yf[yi][:, 0, :], zf[yi][:, 0, :], AF.Silu),
                  reads=[b_zf[yi]], writes=[b_yf[yi]])
            kb.op("pool", lambda e, yi=yi, fc=fc: e.tensor_tensor(
                hid[:, fc, :], yf[yi][:, 0, :], zf[yi][:, 1, :], ALU.mult),
                reads=[b_yf[yi], b_zf[yi]], writes=[b_hid])
        for t in range(NT):
            for nh in range(-(-D // 2048)):
                ncols = min(2048, D - nh * 2048)
                nb = ncols // 512
                for fc in range(FC):
                    wo = wcount[0] % 3
                    wcount[0] += 1
                    qo, fr = divmod(fc * 128, RBO)
                    gv = go[qo].ap().rearrange("(r f) n -> f r n", r=NCORES)
                    src_ap = gv[fr:fr + 128, nh * 4:nh * 4 + nb, :]
                    dst_ap = wo_t[wo][:, :nb * 512].rearrange("p (r n) -> p r n", n=512)
                    kb.dma("sp" if fc % 2 == 0 else "act",
                           lambda e, s=src_ap, dst=dst_ap: e.dma_start(out=dst, in_=s),
                           reads=[b_go[qo]], writes=[b_wo[wo]], via=b_wo[wo])
                    for j in range(nb):
                        kb.op("pe", lambda e, j=j, fc=fc, wo=wo, t=t: e.matmul(
                            p_o[:, j, :], hid[:, fc, t * 128:(t + 1) * 128], wo_t[wo][:, j * 512:(j + 1) * 512],
                            start=(fc == 0), stop=(fc == FC - 1)),
                            reads=[b_hid, b_wo[wo]], writes=[b_po[j]])
                for j in range(nb):
                    c0 = nh * 2048 + j * 512
                    kb.op("dve", lambda e, j=j, c0=c0, t=t: e.tensor_tensor(
                        h_t[t][:, c0:c0 + 512], p_o[:, j, :], h_t[t][:, c0:c0 + 512], ALU.add),
                        reads=[b_po[j], b_h[t]], writes=[b_h[t]])
            rms_rstd(h_t[t], b_h[t], 1)
            kb.op("dve", lambda e, t=t: e.scalar_tensor_tensor(
                h_t[t][:, :], h_t[t][:, :], ss[:, 1:2], nfin[:, :], ALU.mult, ALU.mult),
                reads=[b_h[t], b_ss, b_const], writes=[b_h[t]])
            if t == 0:
                kb.dma("pool", lambda e, dst=out[g * OWN:g * OWN + 126, :], s=h_t[0][2:128, :]:
                       e.dma_start(out=dst, in_=s), reads=[b_h[0]], via=b_h[0])
            else:
                r0 = g * OWN + 126 + (t - 1) * 128
                kb.dma("pool", lambda e, dst=out[r0:r0 + 128, :], s=h_t[t][:, :]:
                       e.dma_start(out=dst, in_=s), reads=[b_h[t]], via=b_h[t])
    return kb.emit()


class Shared:
    def __init__(self, kb, D, ident_in, wt_kc):
        self.kb = kb
        self.D = D
        self.ident_f = kb.sb("ident_f", [128, 128], F32)
        self.ident = kb.sb("ident", [128, 128], BF16)
        self.b_ident = Buf("ident")
        b0 = Buf("ident_f")
        kb.dma("sp", lambda e: e.dma_start(out=self.ident_f[:, :], in_=ident_in[:, :]), writes=[b0], via=b0)
        kb.op("dve", lambda e: e.tensor_copy(self.ident[:, :], self.ident_f[:, :]), reads=[b0], writes=[self.b_ident])
        self.ub = kb.sb("ub", [128, D], BF16)
        self.b_ub = Buf("ub")
        self.ss = kb.sb("ss", [128, 4], F32)
        self.b_ss = Buf("ss")
        self.p_tr = [kb.ps(f"p_tr{i}", [128, 1024], BF16) for i in range(2)]
        self.b_ptr = [Buf(f"p_tr{i}") for i in range(2)]
        self.p_o = kb.ps("p_o", [128, 4, 512], F32)
        self.b_po = [Buf(f"p_o{i}") for i in range(4)]
        self.po_i = 0
        self.wt = [kb.sb(f"wt{i}", [128, wt_kc, 512], BF16) for i in range(2)]
        self.b_wt = [Buf(f"wt{i}") for i in range(2)]
        self.wt_i = 0
        CW = 1024
        self.CW = CW
        self.stg_f = [kb.sb(f"stg_f{i}", [128, CW], F32) for i in range(2)]
        self.stg_b = [kb.sb(f"stg_b{i}", [128, CW], BF16) for i in range(2)]
        self.b_sf = [Buf(f"stg_f{i}") for i in range(2)]
        self.b_sb = [Buf(f"stg_b{i}") for i in range(2)]
        self.stg_i = 0


def emit_cast_tiled(S, src, K, N, name):
    kb = S.kb
    KC, NB = K // 128, N // 512
    assert N % 512 == 0 and K % 128 == 0
    scr = kb.dram(name, [NB * 128, KC * 512], BF16)
    b_scr = Buf(name)
    sv = scr.ap().rearrange("(nb p) (k n) -> p nb k n", p=128, k=KC)
    for k in range(KC):
        for nb0 in range(0, NB, 2):
            nn = min(2, NB - nb0)
            i = S.stg_i % 2
            S.stg_i += 1
            cols = nn * 512
            kb.dma("sp" if S.stg_i % 2 else "act",
                   lambda e, i=i, cols=cols, s=src[k * 128:(k + 1) * 128, nb0 * 512:nb0 * 512 + cols]:
                   e.dma_start(out=S.stg_f[i][:, :cols], in_=s), writes=[S.b_sf[i]], via=S.b_sf[i])
            kb.op("dve" if S.stg_i % 2 else "pool",
                  lambda e, i=i, cols=cols: e.tensor_copy(S.stg_b[i][:, :cols], S.stg_f[i][:, :cols]),
                  reads=[S.b_sf[i]], writes=[S.b_sb[i]])
            kb.dma("sp", lambda e, i=i, cols=cols, nn=nn, d=sv[:, nb0:nb0 + nn, k, :]:
                   e.dma_start(out=d, in_=S.stg_b[i][:, :cols].rearrange("p (a n) -> p a n", a=nn)),
                   reads=[S.b_sb[i]], writes=[b_scr], via=S.b_sb[i])
    return scr, b_scr


def emit_rstd(S, src_t, b_src, col, width):
    kb = S.kb
    kb.op("dve", lambda e: e.tensor_tensor(S.ub[:, :width], src_t[:, :width], src_t[:, :width], ALU.mult),
          reads=[b_src], writes=[S.b_ub])
    kb.op("dve", lambda e: e.reduce_sum(S.ss[:, col:col + 1], S.ub[:, :width], mybir.AxisListType.X),
          reads=[S.b_ub], writes=[S.b_ss])
    kb.op("dve", lambda e: e.tensor_scalar(S.ss[:, col:col + 1], S.ss[:, col:col + 1], 1.0 / width, EPS,
                                           ALU.mult, ALU.add), reads=[S.b_ss], writes=[S.b_ss])
    kb.op("act", lambda e: e.activation(S.ss[:, col:col + 1], S.ss[:, col:col + 1], AF.Sqrt),
          reads=[S.b_ss], writes=[S.b_ss])
    kb.op("dve", lambda e: e.reciprocal(S.ss[:, col:col + 1], S.ss[:, col:col + 1]),
          reads=[S.b_ss], writes=[S.b_ss])


def emit_transposeT(S, width, dstT, b_dst, t, scale_cols=None, b_scale=None):
    kb = S.kb
    KC = width // 128
    for k8 in range(0, KC, 8):
        pi = (k8 // 8) % 2
        nk = min(8, KC - k8)
        for k in range(k8, k8 + nk):
            kb.op("pe", lambda e, k=k, pi=pi: e.transpose(
                S.p_tr[pi][:, (k % 8) * 128:(k % 8 + 1) * 128], S.ub[:, k * 128:(k + 1) * 128], S.ident[:, :]),
                reads=[S.b_ub, S.b_ident], writes=[S.b_ptr[pi]])
        for k in range(k8, k8 + nk):
            if scale_cols is not None:
                kb.op("dve", lambda e, k=k, pi=pi: e.tensor_scalar(
                    dstT[:, k, t * 128:(t + 1) * 128], S.p_tr[pi][:, (k % 8) * 128:(k % 8 + 1) * 128],
                    scale_cols[:, k:k + 1], None, ALU.mult),
                    reads=[S.b_ptr[pi], b_scale], writes=[b_dst])
            else:
                kb.op("dve", lambda e, k=k, pi=pi: e.tensor_copy(
                    dstT[:, k, t * 128:(t + 1) * 128], S.p_tr[pi][:, (k % 8) * 128:(k % 8 + 1) * 128]),
                    reads=[S.b_ptr[pi]], writes=[b_dst])


def emit_gemm_tok(S, lhsT, b_lhs, KC, scr, b_scr, NB, NT, evac):
    kb = S.kb
    for nb in range(NB):
        wi = S.wt_i % 2
        S.wt_i += 1
        kb.dma("sp" if nb % 2 == 0 else "act",
               lambda e, wi=wi, s=scr[nb * 128:(nb + 1) * 128, :]:
               e.dma_start(out=S.wt[wi][:, :KC, :], in_=s.rearrange("p (k n) -> p k n", k=KC)),
               reads=[b_scr], writes=[S.b_wt[wi]], via=S.b_wt[wi])
        for t in range(NT):
            j = S.po_i % 4
            S.po_i += 1
            for k in range(KC):
                kb.op("pe", lambda e, k=k, wi=wi, j=j, t=t: e.matmul(
                    S.p_o[:, j, :], lhsT[:, k, t * 128:(t + 1) * 128], S.wt[wi][:, k, :],
                    start=(k == 0), stop=(k == KC - 1)),
                    reads=[b_lhs, S.b_wt[wi]], writes=[S.b_po[j]])
            evac(t, nb, S.p_o[:, j, :], S.b_po[j])


def build_mixin(D, N1, N2, NG):
    kb = KB()
    KC = D // 128
    NT = TB // 128
    xt = kb.dram("xt", [NG * TB, D], F32, kind="ExternalInput")
    w1 = kb.dram("w1", [D, N1], F32, kind="ExternalInput")
    w2 = kb.dram("w2", [D, N2], F32, kind="ExternalInput")
    nw_col = kb.dram("nw_col", [128, KC], F32, kind="ExternalInput")
    ident_in = kb.dram("ident_in", [128, 128], F32, kind="ExternalInput")
    z = kb.dram("z", [NG * TB, N1], F32, kind="ExternalOutput")
    gt = kb.dram("gt", [NG * TB, N2], F32, kind="ExternalOutput")
    S = Shared(kb, D, ident_in, KC)
    nw = kb.sb("nw", [128, KC], F32)
    b_nw = Buf("nw")
    kb.dma("sp", lambda e: e.dma_start(out=nw[:, :], in_=nw_col[:, :]), writes=[b_nw], via=b_nw)
    s1, b_s1 = emit_cast_tiled(S, w1, D, N1, "s_w1")
    s2, b_s2 = emit_cast_tiled(S, w2, D, N2, "s_w2")
    h_t = [kb.sb(f"h{t}", [128, D], F32) for t in range(NT)]
    b_h = [Buf(f"h{t}") for t in range(NT)]
    uT = kb.sb("uT", [128, KC, TB], BF16)
    b_uT = Buf("uT")
    og = [kb.sb(f"og{i}", [128, 512], F32) for i in range(4)]
    b_og = [Buf(f"og{i}") for i in range(4)]
    oc = [0]
    for g in range(NG):
        for t in range(NT):
            kb.dma("pool", lambda e, dst=h_t[t][:, :], s=xt[g * TB + t * 128:g * TB + (t + 1) * 128, :]:
                   e.dma_start(out=dst, in_=s), writes=[b_h[t]], via=b_h[t])
        for t in range(NT):
            emit_rstd(S, h_t[t], b_h[t], 0, D)
            kb.op("dve", lambda e, t=t: e.tensor_scalar(S.ub[:, :], h_t[t][:, :], S.ss[:, 0:1], None, ALU.mult),
                  reads=[b_h[t], S.b_ss], writes=[S.b_ub])
            emit_transposeT(S, D, uT, b_uT, t, nw, b_nw)

        def mk_evac(dst, func):
            def evac(t, nb, ps, b_ps):
                i = oc[0] % 4
                oc[0] += 1
                kb.op("act", lambda e: e.activation(og[i][:, :], ps, func), reads=[b_ps], writes=[b_og[i]])
                r0 = g * TB + t * 128
                kb.dma("pool", lambda e: e.dma_start(out=dst[r0:r0 + 128, nb * 512:(nb + 1) * 512], in_=og[i][:, :]),
                       reads=[b_og[i]], via=b_og[i])
            return evac
        emit_gemm_tok(S, uT, b_uT, KC, s1, b_s1, N1 // 512, NT, mk_evac(z, AF.Copy))
        emit_gemm_tok(S, uT, b_uT, KC, s2, b_s2, N2 // 512, NT, mk_evac(gt, AF.Sigmoid))
    return kb.emit()


def _group_rows(hfull, NG, halo):
    own = TB - halo
    L, D = hfull.shape
    outs = []
    for c in range(NCORES):
        rows = np.zeros((NG * TB, D), hfull.dtype)
        for j in range(NG):
            p0 = (c * NG + j) * own - halo
            a, b = max(p0, 0), min(p0 + TB, L)
            if b > a:
                rows[j * TB + (a - p0):j * TB + (b - p0)] = hfull[a:b]
        outs.append(rows)
    return outs


def _ungroup_rows(per_core, NG, halo, L):
    own = TB - halo
    chunks = []
    for c in range(NCORES):
        a = per_core[c].reshape(NG, TB, -1)[:, halo:, :]
        chunks.append(a.reshape(NG * own, -1))
    return np.concatenate(chunks, axis=0)[:L]


def run_mixin(hfull, norm_mix_w, w_in, w_gate):
    L, D = hfull.shape
    N1 = -(-w_in.shape[1] // 512) * 512
    N2 = w_gate.shape[1]
    NG = -(-(-(-L // TB)) // NCORES)
    nc = build_mixin(D, N1, N2, NG)
    w1 = np.zeros((D, N1), np.float32)
    w1[:, :w_in.shape[1]] = w_in
    rows = _group_rows(hfull, NG, 0)
    nw_col = np.ascontiguousarray(norm_mix_w.reshape(D // 128, 128).T)
    ident = np.eye(128, dtype=np.float32)
    maps = [{"xt": rows[c], "w1": w1, "w2": np.ascontiguousarray(w_gate), "nw_col": nw_col, "ident_in": ident}
            for c in range(NCORES)]
    res = run_bass_kernel_spmd(nc, maps, core_ids=list(range(NCORES)))
    z = _ungroup_rows([res.results[c]["z"] for c in range(NCORES)], NG, 0, L)[:, :w_in.shape[1]]
    gt = _ungroup_rows([res.results[c]["gt"] for c in range(NCORES)], NG, 0, L)
    return z, gt


AW = 2048
LW, LA, LG = 96, 96, 256


def build_rwkv_prep(NT128):
    kb = KB()
    ACOLS = 3 * AW + LW + LA + LG
    za = kb.dram("za", [NT128 * 128, ACOLS], F32, kind="ExternalInput")
    zp = kb.dram("zp", [NT128 * 128, ACOLS], F32, kind="ExternalInput")
    mu_b = kb.dram("mu_b", [128, ACOLS], F32, kind="ExternalInput")
    cst = kb.dram("cst", [4, 128, AW], F32, kind="ExternalInput")
    w2i = kb.dram("w2i", [LW, AW], F32, kind="ExternalInput")
    a2i = kb.dram("a2i", [LA, AW], F32, kind="ExternalInput")
    g2i = kb.dram("g2i", [LG, AW], F32, kind="ExternalInput")
    ident_in = kb.dram("ident_in", [128, 128], F32, kind="ExternalInput")
    outs = {n: kb.dram(n, [NT128 * 128, AW], F32, kind="ExternalOutput")
            for n in ("o_r", "o_kp", "o_v", "o_kap", "o_b", "o_dec", "o_g")}
    ident = kb.sb("ident", [128, 128], F32)
    mu = kb.sb("mu", [128, ACOLS], F32)
    cs = kb.sb("cs", [128, 4, AW], F32)
    w2 = kb.sb("w2", [128, AW], F32)
    a2 = kb.sb("a2", [128, AW], F32)
    g2 = kb.sb("g2", [128, 2, AW], F32)
    b_c = Buf("c")
    kb.dma("sp", lambda e: e.dma_start(out=ident[:, :], in_=ident_in[:, :]), writes=[b_c], via=b_c)
    kb.dma("sp", lambda e: e.dma_start(out=mu[:, :], in_=mu_b[:, :]), writes=[b_c], via=b_c)
    for i in range(4):
        kb.dma("act", lambda e, i=i: e.dma_start(out=cs[:, i, :], in_=cst[i, :, :]), writes=[b_c], via=b_c)
    kb.dma("sp", lambda e: e.dma_start(out=w2[0:LW, :], in_=w2i[:, :]), writes=[b_c], via=b_c)
    kb.dma("sp", lambda e: e.dma_start(out=a2[0:LA, :], in_=a2i[:, :]), writes=[b_c], via=b_c)
    kb.dma("sp", lambda e: e.dma_start(out=g2[:, :, :], in_=g2i.ap().rearrange("(k p) n -> p k n", p=128)),
           writes=[b_c], via=b_c)
    zt = kb.sb("zt", [128, ACOLS], F32)
    pt = kb.sb("pt", [128, ACOLS], F32)
    b_zt, b_pt = Buf("zt"), Buf("pt")
    W = [kb.sb(f"W{i}", [128, AW], F32) for i in range(5)]
    b_W = [Buf(f"W{i}") for i in range(5)]
    sm = kb.sb("sm", [128, 512], F32)
    b_sm = Buf("sm")
    lT = kb.sb("lT", [128, 4, 128], F32)
    b_lT = Buf("lT")
    hs = kb.sb("hs", [128, 64], F32)
    b_hs = Buf("hs")
    ptr = kb.ps("ptr", [128, 512], F32)
    b_ptr = Buf("ptr")
    pm = [kb.ps(f"pm{i}", [128, 512], F32) for i in range(2)]
    b_pm = [Buf(f"pm{i}") for i in range(2)]
    pc = [0]
    X = mybir.AxisListType.X
    R0, K0, V0 = 0, AW, 2 * AW
    WL0, AL0, GL0 = 3 * AW, 3 * AW + LW, 3 * AW + LW + LA

    def store(name, tile_ap, buf, rows):
        kb.dma("pool", lambda e, d=outs[name][rows, :]: e.dma_start(out=d, in_=tile_ap), reads=[buf], via=buf)

    def lora_mm(dst, b_dst, slots, kparts, rhs_fn, bias_idx):
        for nb in range(AW // 512):
            i = pc[0] % 2
            pc[0] += 1
            for si, slot in enumerate(slots):
                kb.op("pe", lambda e, i=i, nb=nb, si=si, slot=slot: e.matmul(
                    pm[i][:, :], lT[0:kparts, slot, :], rhs_fn(si)[0:kparts, nb * 512:(nb + 1) * 512],
                    start=(si == 0), stop=(si == len(slots) - 1)), reads=[b_lT, b_c], writes=[b_pm[i]])
            if bias_idx is None:
                kb.op("act", lambda e, i=i, nb=nb: e.activation(dst[:, nb * 512:(nb + 1) * 512], pm[i][:, :], AF.Copy),
                      reads=[b_pm[i]], writes=[b_dst])
            else:
                kb.op("dve", lambda e, i=i, nb=nb: e.tensor_tensor(
                    dst[:, nb * 512:(nb + 1) * 512], pm[i][:, :], cs[:, bias_idx, nb * 512:(nb + 1) * 512], ALU.add),
                    reads=[b_pm[i], b_c], writes=[b_dst])

    for j in range(NT128):
        rows = slice(j * 128, (j + 1) * 128)
        kb.dma("sp", lambda e, s=za[rows, :]: e.dma_start(out=zt[:, :], in_=s), writes=[b_zt], via=b_zt)
        kb.dma("act", lambda e, s=zp[rows, :]: e.dma_start(out=pt[:, :], in_=s), writes=[b_pt], via=b_pt)
        kb.op("dve", lambda e: e.tensor_tensor(pt[:, :], pt[:, :], zt[:, :], ALU.subtract), reads=[b_zt, b_pt], writes=[b_pt])
        kb.op("pool", lambda e: e.tensor_tensor(pt[:, :], pt[:, :], mu[:, :], ALU.mult), reads=[b_pt, b_c], writes=[b_pt])
        kb.op("dve", lambda e: e.tensor_tensor(zt[:, :], zt[:, :], pt[:, :], ALU.add), reads=[b_zt, b_pt], writes=[b_zt])
        store("o_r", zt[:, R0:R0 + AW], b_zt, rows)
        store("o_v", zt[:, V0:V0 + AW], b_zt, rows)
        kb.op("act", lambda e: e.activation(sm[:, 0:LW], zt[:, WL0:WL0 + LW], AF.Tanh), reads=[b_zt], writes=[b_sm])
        kb.op("act", lambda e: e.activation(sm[:, 256:512], zt[:, GL0:GL0 + LG], AF.Sigmoid), reads=[b_zt], writes=[b_sm])
        kb.op("dve", lambda e: e.tensor_copy(sm[:, 128:128 + LA], zt[:, AL0:AL0 + LA]), reads=[b_zt], writes=[b_sm])
        for slot, (c0, wdt) in enumerate(((0, LW), (128, LA), (256, 128), (384, 128))):
            kb.op("pe", lambda e, slot=slot, c0=c0, wdt=wdt: e.transpose(
                ptr[0:wdt, slot * 128:(slot + 1) * 128], sm[:, c0:c0 + wdt], ident[:, :]),
                reads=[b_sm, b_c], writes=[b_ptr])
            kb.op("dve", lambda e, slot=slot, wdt=wdt: e.tensor_copy(lT[0:wdt, slot, :], ptr[0:wdt, slot * 128:(slot + 1) * 128]),
                  reads=[b_ptr], writes=[b_lT])
        lora_mm(W[0], b_W[0], [0], LW, lambda si: w2, 0)
        kb.op("act", lambda e: e.activation(W[0][:, :], W[0][:, :], AF.Exp, scale=-1.0), reads=[b_W[0]], writes=[b_W[0]])
        kb.op("act", lambda e: e.activation(W[0][:, :], W[0][:, :], AF.Ln, bias=1.0), reads=[b_W[0]], writes=[b_W[0]])
        kb.op("act", lambda e: e.activation(W[0][:, :], W[0][:, :], AF.Exp, scale=-1.0, bias=-0.5),
              reads=[b_W[0]], writes=[b_W[0]])
        kb.op("act", lambda e: e.activation(W[0][:, :], W[0][:, :], AF.Exp, scale=-1.0), reads=[b_W[0]], writes=[b_W[0]])
        store("o_dec", W[0][:, :], b_W[0], rows)
        lora_mm(W[1], b_W[1], [1], LA, lambda si: a2, 1)
        kb.op("act", lambda e: e.activation(W[1][:, :], W[1][:, :], AF.Sigmoid), reads=[b_W[1]], writes=[b_W[1]])
        lora_mm(W[2], b_W[2], [2, 3], 128, lambda si: g2[:, si, :], None)
        store("o_g", W[2][:, :], b_W[2], rows)
        kb.op("dve", lambda e: e.tensor_tensor(W[3][:, :], zt[:, K0:K0 + AW], cs[:, 2, :], ALU.mult),
              reads=[b_zt, b_c], writes=[b_W[3]])
        kb.op("pool", lambda e: e.tensor_tensor(W[4][:, :], W[3][:, :], W[3][:, :], ALU.mult), reads=[b_W[3]], writes=[b_W[4]])
        kb.op("dve", lambda e: e.reduce_sum(hs[:, 0:32], W[4][:, :].rearrange("p (h k) -> p h k", k=64), X),
              reads=[b_W[4]], writes=[b_hs])
        kb.op("dve", lambda e: e.tensor_scalar(hs[:, 0:32], hs[:, 0:32], 1e-12, None, ALU.add), reads=[b_hs], writes=[b_hs])
        kb.op("act", lambda e: e.activation(hs[:, 0:32], hs[:, 0:32], AF.Sqrt), reads=[b_hs], writes=[b_hs])
        kb.op("dve", lambda e: e.reciprocal(hs[:, 0:32], hs[:, 0:32]), reads=[b_hs], writes=[b_hs])
        kb.op("dve", lambda e: e.tensor_tensor(
            W[3][:, :].rearrange("p (h k) -> p h k", k=64), W[3][:, :].rearrange("p (h k) -> p h k", k=64),
            hs[:, 0:32].unsqueeze(2).to_broadcast([128, 32, 64]), ALU.mult), reads=[b_W[3], b_hs], writes=[b_W[3]])
        store("o_kap", W[3][:, :], b_W[3], rows)
        kb.op("pool", lambda e: e.tensor_tensor(W[4][:, :], W[3][:, :], W[1][:, :], ALU.mult),
              reads=[b_W[3], b_W[1]], writes=[b_W[4]])
        store("o_b", W[4][:, :], b_W[4], rows)
        kb.op("dve", lambda e: e.scalar_tensor_tensor(W[1][:, :], W[1][:, :], 1.0, cs[:, 3, :], ALU.subtract, ALU.mult),
              reads=[b_W[1], b_c], writes=[b_W[1]])
        kb.op("dve", lambda e: e.tensor_scalar(W[1][:, :], W[1][:, :], 1.0, None, ALU.add), reads=[b_W[1]], writes=[b_W[1]])
        kb.op("pool", lambda e: e.tensor_tensor(W[1][:, :], W[1][:, :], zt[:, K0:K0 + AW], ALU.mult),
              reads=[b_W[1], b_zt], writes=[b_W[1]])
        store("o_kp", W[1][:, :], b_W[1], rows)
    return kb.emit()


def run_rwkv_prep(za, mu_shift, w0, w2, a0, a2, g2, k_k, k_a):
    L = za.shape[0]
    NT128 = -(-(-(-L // 128)) // NCORES)
    tot = NT128 * NCORES * 128
    zap = np.zeros((tot, za.shape[1]), np.float32)
    zap[:L] = za
    zpp = np.zeros_like(zap)
    zpp[1:L] = za[:L - 1]
    bc = lambda v: np.ascontiguousarray(np.broadcast_to(v[None, :], (128, v.shape[0]))).astype(np.float32)
    common = {"mu_b": bc(mu_shift), "cst": np.stack([bc(w0), bc(a0), bc(k_k), bc(k_a)]),
              "w2i": np.ascontiguousarray(w2), "a2i": np.ascontiguousarray(a2), "g2i": np.ascontiguousarray(g2),
              "ident_in": np.eye(128, dtype=np.float32)}
    nc = build_rwkv_prep(NT128)
    maps = []
    for c in range(NCORES):
        r = slice(c * NT128 * 128, (c + 1) * NT128 * 128)
        maps.append(dict(common, za=zap[r], zp=zpp[r]))
    res = run_bass_kernel_spmd(nc, maps, core_ids=list(range(NCORES)))
    return {n: np.concatenate([res.results[c][n] for c in range(NCORES)], axis=0)[:L]
            for n in ("o_r", "o_kp", "o_v", "o_kap", "o_b", "o_dec", "o_g")}


SBLK = 32


def build_rwkv_scan(Tp):
    kb = KB()
    NB = Tp // SBLK
    L1 = kb.dram("L1", [128, Tp * 8], F32, kind="ExternalInput")
    LB = kb.dram("LB", [4, Tp * 128], F32, kind="ExternalInput")
    LK = kb.dram("LK", [4, Tp * 128], F32, kind="ExternalInput")
    VM = kb.dram("VM", [4, Tp * 128], F32, kind="ExternalInput")
    DC = kb.dram("DC", [128, Tp * 2], F32, kind="ExternalInput")
    M8 = kb.dram("M8", [8, 128], F32, kind="ExternalInput")
    yo = kb.dram("yo", [4, Tp * 128], F32, kind="ExternalOutput")
    ST = kb.sb("ST", [128, 128], F32)
    b_STh = [Buf("ST0"), Buf("ST1")]
    m8 = kb.sb("m8", [8, 128], F32)
    b_m8 = Buf("m8")
    kb.dma("sp", lambda e: e.dma_start(out=m8[:, :], in_=M8[:, :]), writes=[b_m8], via=b_m8)
    kb.op("pool", lambda e: e.memset(ST[:, :], 0.0), writes=b_STh)
    l1 = [kb.sb(f"l1_{i}", [128, SBLK, 8], F32) for i in range(2)]
    lb = [kb.sb(f"lb_{i}", [4, SBLK, 128], F32) for i in range(2)]
    lk = [kb.sb(f"lk_{i}", [4, SBLK, 128], F32) for i in range(2)]
    vm = [kb.sb(f"vm_{i}", [4, SBLK, 128], F32) for i in range(2)]
    dc = [kb.sb(f"dc_{i}", [128, SBLK, 2], F32) for i in range(2)]
    r2 = [kb.sb(f"r2_{i}", [8, SBLK, 128], F32) for i in range(2)]
    b_in = [Buf(f"in{i}") for i in range(2)]
    b_r2 = [Buf(f"r2_{i}") for i in range(2)]
    p1 = kb.ps("p1", [8, 128], F32)
    pU = kb.ps("pU", [128, 128], F32)
    b_p1, b_pU = Buf("p1"), Buf("pU")
    for blk in range(NB):
        i = blk % 2
        t0 = blk * SBLK
        for (dst, srcd, w, q) in ((l1[i], L1, 8, "sp"), (lb[i], LB, 128, "act"), (lk[i], LK, 128, "sp"),
                                  (vm[i], VM, 128, "act"), (dc[i], DC, 2, "sp")):
            kb.dma(q, lambda e, dst=dst, w=w, s=srcd[:, t0 * w:(t0 + SBLK) * w]: e.dma_start(
                out=dst[:, :, :].rearrange("p a b -> p (a b)"), in_=s), writes=[b_in[i]], via=b_in[i])
        for s in range(SBLK):
            kb.op("pe", lambda e, i=i, s=s: e.matmul(p1[:, :], l1[i][:, s, :], ST[:, :], start=True, stop=True),
                  reads=[b_in[i], b_STh[0], b_STh[1]], writes=[b_p1])
            kb.op("dve", lambda e, i=i, s=s: e.tensor_tensor(r2[i][:, s, :], p1[:, :], m8[:, :], ALU.mult),
                  reads=[b_p1, b_m8], writes=[b_r2[i]])
            kb.op("pe", lambda e, i=i, s=s: e.matmul(pU[:, :], lk[i][:, s, :], vm[i][:, s, :], start=True, stop=False),
                  reads=[b_in[i]], writes=[b_pU])
            kb.op("pe", lambda e, i=i, s=s: e.matmul(pU[:, :], lb[i][:, s, :], r2[i][0:4, s, :], start=False, stop=True),
                  reads=[b_in[i], b_r2[i]], writes=[b_pU])
            for g in range(2):
                kb.op("dve", lambda e, i=i, s=s, g=g: e.scalar_tensor_tensor(
                    ST[:, g * 64:(g + 1) * 64], ST[:, g * 64:(g + 1) * 64], dc[i][:, s, g:g + 1],
                    pU[:, g * 64:(g + 1) * 64], ALU.mult, ALU.add),
                    reads=[b_STh[g], b_in[i], b_pU], writes=[b_STh[g]])
        kb.dma("pool", lambda e, i=i, d=yo[:, t0 * 128:(t0 + SBLK) * 128]: e.dma_start(
            out=d, in_=r2[i][4:8, :, :].rearrange("p a b -> p (a b)")), reads=[b_r2[i]], via=b_r2[i])
    return kb.emit()


def run_rwkv_scan(P):
    L = P["o_r"].shape[0]
    Tp = -(-(L + 1) // SBLK) * SBLK
    H = lambda a: a.reshape(L, 32, 64)
    r, kp, v, kap, b, dec = (H(P[n]) for n in ("o_r", "o_kp", "o_v", "o_kap", "o_b", "o_dec"))
    m8 = np.zeros((8, 128), np.float32)
    for j in range(4):
        g = j // 2
        m8[j, g * 64:(g + 1) * 64] = -1.0
        m8[4 + j, g * 64:(g + 1) * 64] = 1.0
    nc = build_rwkv_scan(Tp)
    maps = []
    for c in range(NCORES):
        L1 = np.zeros((128, Tp, 8), np.float32)
        LB = np.zeros((4, Tp, 128), np.float32)
        LK = np.zeros((4, Tp, 128), np.float32)
        VM = np.zeros((4, Tp, 128), np.float32)
        DC = np.ones((128, Tp, 2), np.float32)
        for j in range(4):
            g, h2 = j // 2, j % 2
            hd = 4 * c + j
            ps = slice(h2 * 64, (h2 + 1) * 64)
            L1[ps, 0:L, j] = kap[:, hd, :].T
            L1[ps, 1:L + 1, 4 + j] = r[:, hd, :].T
            LB[j, 0:L, ps] = b[:, hd, :]
            LK[j, 0:L, ps] = kp[:, hd, :]
            VM[j, 0:L, g * 64:(g + 1) * 64] = v[:, hd, :]
            DC[ps, 0:L, g] = dec[:, hd, :].T
        maps.append({"L1": L1.reshape(128, -1), "LB": LB.reshape(4, -1), "LK": LK.reshape(4, -1),
                     "VM": VM.reshape(4, -1), "DC": DC.reshape(128, -1), "M8": m8})
    res = run_bass_kernel_spmd(nc, maps, core_ids=list(range(NCORES)))
    y = np.zeros((L, 32, 64), np.float32)
    for c in range(NCORES):
        yo = res.results[c]["yo"].reshape(4, Tp, 128)
        for j in range(4):
            g = j // 2
            y[:, 4 * c + j, :] = yo[j, 1:L + 1, g * 64:(g + 1) * 64]
    return y.reshape(L, 2048)


A_GN_EPS = 64e-5


def build_rwkv_post(NT128):
    kb = KB()
    names = ("i_y", "i_r", "i_kp", "i_v", "i_g")
    ins = {n: kb.dram(n, [NT128 * 128, AW], F32, kind="ExternalInput") for n in names}
    cst = kb.dram("cst", [3, 128, AW], F32, kind="ExternalInput")
    ya = kb.dram("ya", [NT128 * 128, AW], F32, kind="ExternalOutput")
    cs = kb.sb("cs", [128, 3, AW], F32)
    b_c = Buf("c")
    for i in range(3):
        kb.dma("sp", lambda e, i=i: e.dma_start(out=cs[:, i, :], in_=cst[i, :, :]), writes=[b_c], via=b_c)
    T = {n: kb.sb("t_" + n, [128, AW], F32) for n in names}
    b_T = {n: Buf("t_" + n) for n in names}
    tmp = kb.sb("tmp", [128, AW], F32)
    b_tmp = Buf("tmp")
    hs = kb.sb("hs", [128, 96], F32)
    b_hs = Buf("hs")
    X = mybir.AxisListType.X
    v3 = lambda ap: ap.rearrange("p (h k) -> p h k", k=64)
    bc = lambda ap: ap.unsqueeze(2).to_broadcast([128, 32, 64])
    for j in range(NT128):
        rows = slice(j * 128, (j + 1) * 128)
        for qi, n in enumerate(names):
            kb.dma("sp" if qi % 2 == 0 else "act", lambda e, n=n, s=ins[n][rows, :]: e.dma_start(out=T[n][:, :], in_=s),
                   writes=[b_T[n]], via=b_T[n])
        y, r, kp, v, g = (T[n] for n in names)
        by, br, bkp, bv, bg = (b_T[n] for n in names)
        kb.op("dve", lambda e: e.reduce_sum(hs[:, 0:32], v3(y[:, :]), X), reads=[by], writes=[b_hs])
        kb.op("dve", lambda e: e.tensor_scalar(hs[:, 0:32], hs[:, 0:32], 1.0 / 64, None, ALU.mult), reads=[b_hs], writes=[b_hs])
        kb.op("dve", lambda e: e.tensor_tensor(v3(y[:, :]), v3(y[:, :]), bc(hs[:, 0:32]), ALU.subtract),
              reads=[by, b_hs], writes=[by])
        kb.op("pool", lambda e: e.tensor_tensor(tmp[:, :], y[:, :], y[:, :], ALU.mult), reads=[by], writes=[b_tmp])
        kb.op("dve", lambda e: e.reduce_sum(hs[:, 32:64], v3(tmp[:, :]), X), reads=[b_tmp], writes=[b_hs])
        kb.op("dve", lambda e: e.tensor_scalar(hs[:, 32:64], hs[:, 32:64], 1.0 / 64, A_GN_EPS, ALU.mult, ALU.add),
              reads=[b_hs], writes=[b_hs])
        kb.op("act", lambda e: e.activation(hs[:, 32:64], hs[:, 32:64], AF.Sqrt), reads=[b_hs], writes=[b_hs])
        kb.op("dve", lambda e: e.reciprocal(hs[:, 32:64], hs[:, 32:64]), reads=[b_hs], writes=[b_hs])
        kb.op("dve", lambda e: e.tensor_tensor(v3(y[:, :]), v3(y[:, :]), bc(hs[:, 32:64]), ALU.mult),
              reads=[by, b_hs], writes=[by])
        kb.op("pool", lambda e: e.tensor_tensor(y[:, :], y[:, :], cs[:, 0, :], ALU.mult), reads=[by, b_c], writes=[by])
        kb.op("pool", lambda e: e.tensor_tensor(y[:, :], y[:, :], cs[:, 1, :], ALU.add), reads=[by, b_c], writes=[by])
        kb.op("dve", lambda e: e.tensor_tensor(tmp[:, :], r[:, :], kp[:, :], ALU.mult), reads=[br, bkp, b_tmp], writes=[b_tmp])
        kb.op("dve", lambda e: e.tensor_tensor(tmp[:, :], tmp[:, :], cs[:, 2, :], ALU.mult), reads=[b_tmp, b_c], writes=[b_tmp])
        kb.op("dve", lambda e: e.reduce_sum(hs[:, 64:96], v3(tmp[:, :]), X), reads=[b_tmp], writes=[b_hs])
        kb.op("dve", lambda e: e.tensor_tensor(v3(v[:, :]), v3(v[:, :]), bc(hs[:, 64:96]), ALU.mult),
              reads=[bv, b_hs], writes=[bv])
        kb.op("pool", lambda e: e.tensor_tensor(y[:, :], y[:, :], v[:, :], ALU.add), reads=[by, bv], writes=[by])
        kb.op("pool", lambda e: e.tensor_tensor(y[:, :], y[:, :], g[:, :], ALU.mult), reads=[by, bg], writes=[by])
        kb.dma("pool", lambda e, d=ya[rows, :]: e.dma_start(out=d, in_=y[:, :]), reads=[by], via=by)
    return kb.emit()


def run_rwkv_post(y, P, ln_w, ln_b, r_k):
    L = y.shape[0]
    NT128 = -(-(-(-L // 128)) // NCORES)
    tot = NT128 * NCORES * 128

    def pad(a):
        o = np.zeros((tot, AW), np.float32)
        o[:L] = a
        return o
    arrs = {"i_y": pad(y), "i_r": pad(P["o_r"]), "i_kp": pad(P["o_kp"]), "i_v": pad(P["o_v"]), "i_g": pad(P["o_g"])}
    bc = lambda v: np.ascontiguousarray(np.broadcast_to(v.reshape(1, -1), (128, AW))).astype(np.float32)
    cst = np.stack([bc(ln_w), bc(ln_b), bc(r_k)])
    nc = build_rwkv_post(NT128)
    maps = []
    for c in range(NCORES):
        rr = slice(c * NT128 * 128, (c + 1) * NT128 * 128)
        maps.append(dict({n: a[rr] for n, a in arrs.items()}, cst=cst))
    res = run_bass_kernel_spmd(nc, maps, core_ids=list(range(NCORES)))
    return np.concatenate([res.results[c]["ya"] for c in range(NCORES)], axis=0)[:L]


def build_mixout(D, DA, NT128):
    kb = KB()
    KC, KA = D // 128, DA // 128
    hr = kb.dram("hr", [NT128 * 128, D], F32, kind="ExternalInput")
    yar = kb.dram("yar", [NT128 * 128, DA], F32, kind="ExternalInput")
    ybr = kb.dram("ybr", [NT128 * 128, DA], F32, kind="ExternalInput")
    gr = kb.dram("gr", [NT128 * 128, 2 * D], F32, kind="ExternalInput")
    wpa = kb.dram("wpa", [DA, D], F32, kind="ExternalInput")
    wpb = kb.dram("wpb", [DA, D], F32, kind="ExternalInput")
    wou = kb.dram("wou", [D, D], F32, kind="ExternalInput")
    ident_in = kb.dram("ident_in", [128, 128], F32, kind="ExternalInput")
    h1 = kb.dram("h1", [NT128 * 128, D], F32, kind="ExternalOutput")
    S = Shared(kb, D, ident_in, KC)
    sa, b_sa = emit_cast_tiled(S, wpa, DA, D, "s_wpa")
    sb_, b_sb_ = emit_cast_tiled(S, wpb, DA, D, "s_wpb")
    so, b_so = emit_cast_tiled(S, wou, D, D, "s_wou")
    h_t = kb.sb("h_t", [128, D], F32)
    b_h = Buf("h_t")
    g_t = kb.sb("g_t", [128, 2 * D], F32)
    b_g = Buf("g_t")
    m_t = kb.sb("m_t", [128, D], F32)
    b_m = Buf("m_t")
    ys = kb.sb("ys", [128, DA], F32)
    b_ys = Buf("ys")
    tmp = [kb.sb(f"tmp{i}", [128, 512], F32) for i in range(2)]
    b_tmp = [Buf(f"tmp{i}") for i in range(2)]
    yaT = kb.sb("yaT", [128, KA, 128], BF16)
    ybT = kb.sb("ybT", [128, KA, 128], BF16)
    mT = kb.sb("mT", [128, KC, 128], BF16)
    b_yaT, b_ybT, b_mT = Buf("yaT"), Buf("ybT"), Buf("mT")
    tc_ = [0]
    for j in range(NT128):
        rows = slice(j * 128, (j + 1) * 128)
        kb.dma("pool", lambda e, s=hr[rows, :]: e.dma_start(out=h_t[:, :], in_=s), writes=[b_h], via=b_h)
        kb.dma("pool", lambda e, s=gr[rows, :]: e.dma_start(out=g_t[:, :], in_=s), writes=[b_g], via=b_g)
        for (srcy, dT, b_dT) in ((yar, yaT, b_yaT), (ybr, ybT, b_ybT)):
            kb.dma("pool", lambda e, s=srcy[rows, :]: e.dma_start(out=ys[:, :], in_=s), writes=[b_ys], via=b_ys)
            kb.op("dve", lambda e: e.tensor_copy(S.ub[:, :DA], ys[:, :]), reads=[b_ys], writes=[S.b_ub])
            emit_transposeT(S, DA, dT, b_dT, 0)

        def evac_a(t, nb, ps, b_ps):
            kb.op("dve", lambda e: e.tensor_tensor(m_t[:, nb * 512:(nb + 1) * 512], ps, g_t[:, nb * 512:(nb + 1) * 512],
                                                   ALU.mult), reads=[b_ps, b_g], writes=[b_m])

        def evac_b(t, nb, ps, b_ps):
            i = tc_[0] % 2
            tc_[0] += 1
            kb.op("dve", lambda e: e.tensor_tensor(tmp[i][:, :], ps, g_t[:, D + nb * 512:D + (nb + 1) * 512], ALU.mult),
                  reads=[b_ps, b_g], writes=[b_tmp[i]])
            kb.op("pool", lambda e: e.tensor_tensor(m_t[:, nb * 512:(nb + 1) * 512], m_t[:, nb * 512:(nb + 1) * 512],
                                                    tmp[i][:, :], ALU.add), reads=[b_tmp[i], b_m], writes=[b_m])

        def evac_o(t, nb, ps, b_ps):
            kb.op("dve", lambda e: e.tensor_tensor(h_t[:, nb * 512:(nb + 1) * 512], ps, h_t[:, nb * 512:(nb + 1) * 512],
                                                   ALU.add), reads=[b_ps, b_h], writes=[b_h])
        emit_gemm_tok(S, yaT, b_yaT, KA, sa, b_sa, D // 512, 1, evac_a)
        emit_gemm_tok(S, ybT, b_ybT, KA, sb_, b_sb_, D // 512, 1, evac_b)
        kb.op("dve", lambda e: e.tensor_copy(S.ub[:, :], m_t[:, :]), reads=[b_m], writes=[S.b_ub])
        emit_transposeT(S, D, mT, b_mT, 0)
        emit_gemm_tok(S, mT, b_mT, KC, so, b_so, D // 512, 1, evac_o)
        kb.dma("pool", lambda e, d=h1[rows, :]: e.dma_start(out=d, in_=h_t[:, :]), reads=[b_h], via=b_h)
    return kb.emit()


def run_mixout(hfull, ya, yb, gates, w_proj_a, w_proj_b, w_out):
    L, D = hfull.shape
    DA = ya.shape[1]
    NT128 = -(-(-(-L // 128)) // NCORES)
    tot = NT128 * NCORES * 128

    def pad(a):
        o = np.zeros((tot, a.shape[1]), np.float32)
        o[:L] = a
        return o
    hp, yap, ybp, gp = pad(hfull), pad(ya), pad(yb), pad(gates)
    nc = build_mixout(D, DA, NT128)
    ident = np.eye(128, dtype=np.float32)
    maps = []
    for c in range(NCORES):
        r = slice(c * NT128 * 128, (c + 1) * NT128 * 128)
        maps.append({"hr": hp[r], "yar": yap[r], "ybr": ybp[r], "gr": gp[r],
                     "wpa": np.ascontiguousarray(w_proj_a), "wpb": np.ascontiguousarray(w_proj_b),
                     "wou": np.ascontiguousarray(w_out), "ident_in": ident})
    res = run_bass_kernel_spmd(nc, maps, core_ids=list(range(NCORES)))
    return np.concatenate([res.results[c]["h1"] for c in range(NCORES)], axis=0)[:L]


NEG_ADM = -1.0e30
NEG_SEL = -2.0e30
IDX_C0 = (32 ** -0.5) * (64 ** -0.5)
TOPK = 256


def build_dsa(Lp, NQ, klim=None):
    kb = KB()
    NKT = Lp // 128
    NBLK = Lp // 64
    ckv = kb.dram("ckv", [Lp, 512], F32, kind="ExternalInput")
    kidx = kb.dram("kidx", [Lp, 64], F32, kind="ExternalInput")
    qr = kb.dram("qr", [NQ * 128, 2048], F32, kind="ExternalInput")
    qir = kb.dram("qir", [NQ * 128, 2048], F32, kind="ExternalInput")
    wir = kb.dram("wir", [NQ * 128, 32], F32, kind="ExternalInput")
    qch = kb.dram("qch", [NQ * 128, 1], F32, kind="ExternalInput")
    wukT = kb.dram("wukT", [512, 2048], F32, kind="ExternalInput")
    wuvR = kb.dram("wuvR", [512, 2048], F32, kind="ExternalInput")
    kvw_b = kb.dram("kvw_b", [128, 512], F32, kind="ExternalInput")
    lnw_b = kb.dram("lnw_b", [128, 64], F32, kind="ExternalInput")
    lnb_b = kb.dram("lnb_b", [128, 64], F32, kind="ExternalInput")
    blk_in = kb.dram("blk_in", [128, NBLK], F32, kind="ExternalInput")
    ident_in = kb.dram("ident_in", [128, 128], F32, kind="ExternalInput")
    yb = kb.dram("yb", [NQ * 128, 2048], F32, kind="ExternalOutput")
    Kscr = kb.dram("Kscr", [4 * NKT * 128, 512], BF16)
    Vscr = kb.dram("Vscr", [4 * NKT * 128, 512], BF16)
    kIscr = kb.dram("kIscr", [128, Lp], BF16)
    b_Kscr, b_Vscr, b_kIscr = Buf("Kscr"), Buf("Vscr"), Buf("kIscr")

    ident_f = kb.sb("ident_f", [128, 128], F32)
    ident4 = kb.sb("ident4", [128, 512], BF16)
    ones_bf = kb.sb("ones_bf", [128, 1], BF16)
    kvw = kb.sb("kvw", [128, 512], F32)
    lnw = kb.sb("lnw", [128, 64], F32)
    lnb = kb.sb("lnb", [128, 64], F32)
    blk = kb.sb("blk", [128, NBLK], F32)
    b_c = Buf("consts")
    for dst, s in ((ident_f, ident_in), (kvw, kvw_b), (lnw, lnw_b), (lnb, lnb_b), (blk, blk_in)):
        kb.dma("sp", lambda e, dst=dst, s=s: e.dma_start(out=dst[:, :], in_=s[:, :]), writes=[b_c], via=b_c)
    b_id = Buf("ident4")
    for a in range(4):
        kb.op("dve", lambda e, a=a: e.tensor_copy(ident4[:, a * 128:(a + 1) * 128], ident_f[:, :]),
              reads=[b_c], writes=[b_id])
    kb.op("pool", lambda e: e.memset(ones_bf[:, :], 1.0), writes=[b_id])
    ident = ident4[:, 0:128]

    big8 = kb.sb("big8", [128, 2048], F32)
    b_big8 = Buf("big8")
    wuk_sb = kb.sb("wuk_sb", [128, 4, 2048], BF16)
    wuv_sb = kb.sb("wuv_sb", [128, 4, 2048], BF16)
    b_wuk, b_wuv = Buf("wuk"), Buf("wuv")
    for (srcw, dstw, bw) in ((wukT, wuk_sb, b_wuk), (wuvR, wuv_sb, b_wuv)):
        for rc in range(4):
            kb.dma("sp", lambda e, s=srcw[rc * 128:(rc + 1) * 128, :]: e.dma_start(out=big8[:, :], in_=s),
                   writes=[b_big8], via=b_big8)
            kb.op("dve", lambda e, dstw=dstw, rc=rc: e.tensor_copy(dstw[:, rc, :], big8[:, :]),
                  reads=[b_big8], writes=[bw])

    pT_ = kb.ps("pT", [128, 1024], BF16)
    b_pT = Buf("pT")
    pI = [kb.ps(f"pI{i}", [128, 1024], F32) for i in range(2)]
    b_pI = [[Buf(f"pI{i}_{h}") for h in range(2)] for i in range(2)]
    pL = [kb.ps(f"pL{i}", [128, 512], F32) for i in range(2)]
    b_pL = [Buf(f"pL{i}") for i in range(2)]
    pR = kb.ps("pR", [128, 512], F32)
    b_pR = Buf("pR")

    ss = kb.sb("ss", [128, 8], F32)
    b_ss = Buf("ss")
    sc = kb.sb("sc", [128, 512], F32)
    b_sc = Buf("sc")

    def rstd_of(src_ap, b_src, width, col, eps):
        kb.op("dve", lambda e: e.tensor_tensor(sc[:, :width], src_ap, src_ap, ALU.mult), reads=[b_src], writes=[b_sc])
        kb.op("dve", lambda e: e.reduce_sum(ss[:, col:col + 1], sc[:, :width], mybir.AxisListType.X),
              reads=[b_sc], writes=[b_ss])
        kb.op("dve", lambda e: e.tensor_scalar(ss[:, col:col + 1], ss[:, col:col + 1], 1.0 / width, eps,
                                               ALU.mult, ALU.add), reads=[b_ss], writes=[b_ss])
        kb.op("act", lambda e: e.activation(ss[:, col:col + 1], ss[:, col:col + 1], AF.Sqrt),
              reads=[b_ss], writes=[b_ss])
        kb.op("dve", lambda e: e.reciprocal(ss[:, col:col + 1], ss[:, col:col + 1]), reads=[b_ss], writes=[b_ss])

    ckt = [kb.sb(f"ckt{i}", [128, 512], F32) for i in range(2)]
    kxt = [kb.sb(f"kxt{i}", [128, 64], F32) for i in range(2)]
    b_ckt = [Buf(f"ckt{i}") for i in range(2)]
    b_kxt = [Buf(f"kxt{i}") for i in range(2)]
    cb = kb.sb("cb", [128, 512], BF16)
    b_cb = Buf("cb")
    cT = [kb.sb(f"cT{i}", [128, 4, 128], BF16) for i in range(2)]
    b_cT = [Buf(f"cT{i}") for i in range(2)]
    kst = [kb.sb(f"kst{i}", [128, 512], BF16) for i in range(3)]
    b_kst = [Buf(f"kst{i}") for i in range(3)]
    xc = kb.sb("xc", [128, 64], F32)
    b_xc = Buf("xc")
    kb2 = kb.sb("kb2", [128, 128], BF16)
    b_kb2 = Buf("kb2")
    kIt = [kb.sb(f"kIt{i}", [128, 128], BF16) for i in range(2)]
    b_kIt = [Buf(f"kIt{i}") for i in range(2)]
    kc = [0]
    lc = [0]
    for st in range(NKT):
        i = st % 2
        kb.dma("sp", lambda e, i=i, s=ckv[st * 128:(st + 1) * 128, :]: e.dma_start(out=ckt[i][:, :], in_=s),
               writes=[b_ckt[i]], via=b_ckt[i])
        kb.dma("act", lambda e, i=i, s=kidx[st * 128:(st + 1) * 128, :]: e.dma_start(out=kxt[i][:, :], in_=s),
               writes=[b_kxt[i]], via=b_kxt[i])
        rstd_of(ckt[i][:, :], b_ckt[i], 512, 0, EPS)
        kb.op("dve", lambda e, i=i: e.scalar_tensor_tensor(cb[:, :], ckt[i][:, :], ss[:, 0:1], kvw[:, :],
                                                            ALU.mult, ALU.mult),
              reads=[b_ckt[i], b_ss, b_c], writes=[b_cb])
        for rc in range(4):
            kb.op("pe", lambda e, rc=rc: e.transpose(pT_[:, rc * 128:(rc + 1) * 128], cb[:, rc * 128:(rc + 1) * 128],
                                                     ident), reads=[b_cb, b_id], writes=[b_pT])
        kb.op("act", lambda e, i=i: e.activation(cT[i][:, :, :].rearrange("p a b -> p (a b)"), pT_[:, 0:512], AF.Copy),
              reads=[b_pT], writes=[b_cT[i]])
        for hg in range(4):
            li = lc[0] % 2
            lc[0] += 1
            for h4 in range(4):
                h = hg * 4 + h4
                for rc in range(4):
                    kb.op("pe", lambda e, li=li, h4=h4, h=h, rc=rc, i=i: e.matmul(
                        pL[li][:, h4 * 128:(h4 + 1) * 128], wuk_sb[:, rc, h * 128:(h + 1) * 128], cT[i][:, rc, :],
                        start=(rc == 0), stop=(rc == 3)), reads=[b_wuk, b_cT[i]], writes=[b_pL[li]])
            ki = kc[0] % 3
            kc[0] += 1
            kb.op("act", lambda e, ki=ki, li=li: e.activation(kst[ki][:, :], pL[li][:, :], AF.Copy),
                  reads=[b_pL[li]], writes=[b_kst[ki]])
            r0 = (hg * NKT + st) * 128
            kb.dma("sp", lambda e, ki=ki, d=Kscr[r0:r0 + 128, :]: e.dma_start(out=d, in_=kst[ki][:, :]),
                   reads=[b_kst[ki]], writes=[b_Kscr], via=b_kst[ki])
        for hg in range(4):
            li = lc[0] % 2
            lc[0] += 1
            for rc in range(4):
                kb.op("pe", lambda e, li=li, hg=hg, rc=rc, i=i: e.matmul(
                    pL[li][:, :], cT[i][:, rc, :], wuv_sb[:, rc, hg * 512:(hg + 1) * 512],
                    start=(rc == 0), stop=(rc == 3)), reads=[b_wuv, b_cT[i]], writes=[b_pL[li]])
            ki = kc[0] % 3
            kc[0] += 1
            kb.op("dve", lambda e, ki=ki, li=li: e.tensor_copy(kst[ki][:, :], pL[li][:, :]),
                  reads=[b_pL[li]], writes=[b_kst[ki]])
            r0 = (hg * NKT + st) * 128
            kb.dma("act", lambda e, ki=ki, d=Vscr[r0:r0 + 128, :]: e.dma_start(out=d, in_=kst[ki][:, :]),
                   reads=[b_kst[ki]], writes=[b_Vscr], via=b_kst[ki])
        kb.op("dve", lambda e, i=i: e.reduce_sum(ss[:, 1:2], kxt[i][:, :], mybir.AxisListType.X),
              reads=[b_kxt[i]], writes=[b_ss])
        kb.op("dve", lambda e: e.tensor_scalar(ss[:, 1:2], ss[:, 1:2], 1.0 / 64, None, ALU.mult),
              reads=[b_ss], writes=[b_ss])
        kb.op("dve", lambda e, i=i: e.tensor_scalar(xc[:, :], kxt[i][:, :], ss[:, 1:2], None, ALU.subtract),
              reads=[b_kxt[i], b_ss], writes=[b_xc])
        rstd_of(xc[:, :], b_xc, 64, 2, 1e-6)
        kb.op("dve", lambda e: e.scalar_tensor_tensor(xc[:, :], xc[:, :], ss[:, 2:3], lnw[:, :], ALU.mult, ALU.mult),
              reads=[b_xc, b_ss, b_c], writes=[b_xc])
        for a in range(2):
            kb.op("dve", lambda e, a=a: e.tensor_tensor(kb2[:, a * 64:(a + 1) * 64], xc[:, :], lnb[:, :], ALU.add),
                  reads=[b_xc, b_c], writes=[b_kb2])
        kb.op("pe", lambda e: e.transpose(pT_[:, 512:640], kb2[:, :], ident), reads=[b_kb2, b_id], writes=[b_pT])
        kb.op("act", lambda e, i=i: e.activation(kIt[i][:, :], pT_[:, 512:640], AF.Copy),
              reads=[b_pT], writes=[b_kIt[i]])
        kb.dma("sp", lambda e, i=i, d=kIscr[:, st * 128:(st + 1) * 128]: e.dma_start(out=d, in_=kIt[i][:, :]),
               reads=[b_kIt[i]], writes=[b_kIscr], via=b_kIt[i])

    score = kb.sb("score", [128, Lp], F32)
    b_score = Buf("score")
    mb = kb.sb("mb", [128, Lp], BF16)
    b_mb = Buf("mb")
    qb16 = kb.sb("qb16", [128, 2048], BF16)
    b_qb16 = Buf("qb16")
    qT = kb.sb("qT", [128, 16, 128], BF16)
    qiT = kb.sb("qiT", [128, 16, 128], BF16)
    b_qT, b_qiT = Buf("qT"), Buf("qiT")
    wq = kb.sb("wq", [128, 32], F32)
    qc = kb.sb("qc", [128, 1], F32)
    b_wq = Buf("wq")
    rl = [kb.sb(f"rl{i}", [128, 1024], F32) for i in range(2)]
    b_rl = [Buf(f"rl{i}") for i in range(2)]
    kIb = [kb.sb(f"kIb{i}", [128, 1024], BF16) for i in range(2)]
    b_kIb = [Buf(f"kIb{i}") for i in range(2)]
    kt = [kb.sb(f"kt{i}", [128, 512], BF16) for i in range(3)]
    vt = [kb.sb(f"vt{i}", [128, 4, 132], BF16) for i in range(3)]
    b_kt = [Buf(f"kt{i}") for i in range(3)]
    b_vt = [Buf(f"vt{i}") for i in range(3)]
    for i in range(3):
        kb.op("pool", lambda e, i=i: e.memset(vt[i][:, :, 128:129], 1.0), writes=[b_vt[i]])
    pTs = [kb.sb(f"pTs{i}", [128, 512], BF16) for i in range(2)]
    b_pTs = [Buf(f"pTs{i}") for i in range(2)]
    mx = kb.sb("mx", [128, 8], F32)
    mx16 = kb.sb("mx16", [128, 16], F32)
    b_mx = Buf("mx")
    rinv = kb.sb("rinv", [128, 4], F32)
    b_rinv = Buf("rinv")
    po = [(pI[0], 0, b_pI[0][0]), (pI[0], 1, b_pI[0][1]), (pI[1], 0, b_pI[1][0]), (pI[1], 1, b_pI[1][1])]
    scale = 128 ** -0.5
    cnt = [0]
    for j in range(NQ):
        rows = slice(j * 128, (j + 1) * 128)
        Lj = Lp if klim is None else klim[j]
        NKTj = Lj // 128
        NBj = Lj // 64
        HA = Lj if Lj <= 16384 else 8192
        blkB = blk[:, :NBj].unsqueeze(2).to_broadcast([128, NBj, 64])
        mb3 = mb[:, :Lj].rearrange("p (a b) -> p a b", b=64)
        kb.dma("pool", lambda e, s=wir[rows, :]: e.dma_start(out=wq[:, :], in_=s), writes=[b_wq], via=b_wq)
        kb.dma("pool", lambda e, s=qch[rows, :]: e.dma_start(out=qc[:, :], in_=s), writes=[b_wq], via=b_wq)
        kb.op("dve", lambda e: e.tensor_scalar(wq[:, :], wq[:, :], IDX_C0, None, ALU.mult), reads=[b_wq], writes=[b_wq])
        for (srcq, dstT, b_dst) in ((qr, qT, b_qT), (qir, qiT, b_qiT)):
            kb.dma("pool", lambda e, s=srcq[rows, :]: e.dma_start(out=big8[:, :], in_=s), writes=[b_big8], via=b_big8)
            kb.op("dve", lambda e: e.tensor_copy(qb16[:, :], big8[:, :]), reads=[b_big8], writes=[b_qb16])
            for b8 in range(2):
                for k in range(8):
                    kb.op("pe", lambda e, b8=b8, k=k: e.transpose(
                        pT_[:, k * 128:(k + 1) * 128], qb16[:, (b8 * 8 + k) * 128:(b8 * 8 + k + 1) * 128], ident),
                        reads=[b_qb16, b_id], writes=[b_pT])
                kb.op("act", lambda e, b8=b8, dstT=dstT: e.activation(
                    dstT[:, b8 * 8:(b8 + 1) * 8, :].rearrange("p a b -> p (a b)"), pT_[:, :], AF.Copy),
                    reads=[b_pT], writes=[b_dst])
        kb.op("pool", lambda e, Lj=Lj, HA=HA, mb3=mb3, blkB=blkB: e.memset(score[:, :Lj], 0.0), writes=[b_score])
        for k0 in range(0, Lj, 1024):
            wd = min(1024, Lj - k0)
            gi_ = (k0 // 1024) % 2
            kb.dma("sp", lambda e, gi_=gi_, wd=wd, s=kIscr[:, k0:k0 + wd]: e.dma_start(out=kIb[gi_][:, :wd], in_=s),
                   reads=[b_kIscr], writes=[b_kIb[gi_]], via=b_kIb[gi_])
            for h in range(32):
                pi = h % 2
                half = (h % 2) * 64
                hp = h // 2
                for c0 in range(0, wd, 512):
                    cw_ = min(512, wd - c0)
                    kb.op("pe", lambda e, pi=pi, half=half, hp=hp, c0=c0, cw_=cw_, gi_=gi_: e.matmul(
                        pI[pi][:, c0:c0 + cw_], qiT[half:half + 64, hp, :], kIb[gi_][half:half + 64, c0:c0 + cw_],
                        start=True, stop=True), reads=[b_qiT, b_kIb[gi_]], writes=[b_pI[pi][c0 // 512]])
                kb.op("act", lambda e, pi=pi, wd=wd: e.activation(rl[pi][:, :wd], pI[pi][:, :wd], AF.Relu),
                      reads=b_pI[pi][:(wd + 511) // 512], writes=[b_rl[pi]])
                kb.op("dve", lambda e, pi=pi, wd=wd, k0=k0, h=h: e.scalar_tensor_tensor(
                    score[:, k0:k0 + wd], rl[pi][:, :wd], wq[:, h:h + 1], score[:, k0:k0 + wd], ALU.mult, ALU.add),
                    reads=[b_rl[pi], b_wq, b_score], writes=[b_score])
        kb.op("dve", lambda e, Lj=Lj, HA=HA, mb3=mb3, blkB=blkB: e.tensor_scalar(mb3, blkB, qc[:, 0:1], None, ALU.is_le), reads=[b_c, b_wq], writes=[b_mb])
        kb.op("pool", lambda e, Lj=Lj, HA=HA, mb3=mb3, blkB=blkB: e.memset(mb[:, 0:48], 0.0), writes=[b_mb])
        kb.op("dve", lambda e, Lj=Lj, HA=HA, mb3=mb3, blkB=blkB: e.tensor_tensor(score[:, :Lj], score[:, :Lj], mb[:, :Lj], ALU.mult),
              reads=[b_score, b_mb], writes=[b_score])
        kb.op("dve", lambda e, Lj=Lj, HA=HA, mb3=mb3, blkB=blkB: e.tensor_scalar(mb[:, :Lj], mb[:, :Lj], -NEG_ADM, NEG_ADM, ALU.mult, ALU.add),
              reads=[b_mb], writes=[b_mb])
        kb.op("dve", lambda e, Lj=Lj, HA=HA, mb3=mb3, blkB=blkB: e.tensor_tensor(score[:, :Lj], score[:, :Lj], mb[:, :Lj], ALU.add),
              reads=[b_score, b_mb], writes=[b_score])
        for r_ in range(TOPK // 8):
            if HA == Lj:
                kb.op("dve", lambda e, Lj=Lj, HA=HA, mb3=mb3, blkB=blkB: e.max(out=mx[:, :], in_=score[:, :Lj]), reads=[b_score], writes=[b_mx])
            else:
                kb.op("dve", lambda e, Lj=Lj, HA=HA, mb3=mb3, blkB=blkB: e.max(out=mx16[:, 0:8], in_=score[:, :HA]), reads=[b_score], writes=[b_mx])
                kb.op("dve", lambda e, Lj=Lj, HA=HA, mb3=mb3, blkB=blkB: e.max(out=mx16[:, 8:16], in_=score[:, HA:Lj]), reads=[b_score], writes=[b_mx])
                kb.op("dve", lambda e, Lj=Lj, HA=HA, mb3=mb3, blkB=blkB: e.max(out=mx[:, :], in_=mx16[:, :]), reads=[b_mx], writes=[b_mx])
            for (a0_, a1_) in ((0, HA), (HA, Lj)):
                if a1_ > a0_:
                    kb.op("dve", lambda e, a0_=a0_, a1_=a1_: e.match_replace(
                        out=score[:, a0_:a1_], in_to_replace=mx[:, :], in_values=score[:, a0_:a1_],
                        imm_value=NEG_SEL), reads=[b_mx, b_score], writes=[b_score])
        kb.op("dve", lambda e, Lj=Lj, HA=HA, mb3=mb3, blkB=blkB: e.tensor_scalar(mb[:, :Lj], score[:, :Lj], 1.5 * NEG_ADM, None, ALU.is_le),
              reads=[b_score], writes=[b_mb])
        kb.op("dve", lambda e, Lj=Lj, HA=HA, mb3=mb3, blkB=blkB: e.scalar_tensor_tensor(mb3, blkB, qc[:, 0:1], mb3, ALU.is_le, ALU.mult),
              reads=[b_c, b_wq, b_mb], writes=[b_mb])
        kb.op("pool", lambda e, Lj=Lj, HA=HA, mb3=mb3, blkB=blkB: e.memset(mb[:, 0:48], 0.0), writes=[b_mb])
        kb.op("dve", lambda e, Lj=Lj, HA=HA, mb3=mb3, blkB=blkB: e.tensor_scalar(mb[:, :Lj], mb[:, :Lj], 1.0, 30000.0, ALU.subtract, ALU.mult),
              reads=[b_mb], writes=[b_mb])
        for hg in range(4):
            for st in range(NKTj):
                ti = cnt[0] % 3
                li = cnt[0] % 2
                cnt[0] += 1
                r0 = (hg * NKT + st) * 128
                kb.dma("sp", lambda e, ti=ti, s=Kscr[r0:r0 + 128, :]: e.dma_start(out=kt[ti][:, :], in_=s),
                       reads=[b_Kscr], writes=[b_kt[ti]], via=b_kt[ti])
                kb.dma("act", lambda e, ti=ti, s=Vscr[r0:r0 + 128, :]: e.dma_start(
                    out=vt[ti][:, :, 0:128], in_=s.rearrange("p (a b) -> p a b", a=4)),
                    reads=[b_Vscr], writes=[b_vt[ti]], via=b_vt[ti])
                kb.op("pe", lambda e, li=li, st=st: e.matmul(pL[li][:, :], mb[:, st * 128:(st + 1) * 128], ident4[:, :],
                                                             start=True, stop=False, skip_group_check=True),
                      reads=[b_mb, b_id], writes=[b_pL[li]])
                for h4 in range(4):
                    kb.op("pe", lambda e, li=li, h4=h4, ti=ti, hg=hg: e.matmul(
                        pL[li][:, h4 * 128:(h4 + 1) * 128], kt[ti][:, h4 * 128:(h4 + 1) * 128], qT[:, hg * 4 + h4, :],
                        start=False, stop=True, skip_group_check=True),
                        reads=[b_kt[ti], b_qT], writes=[b_pL[li]])
                kb.op("act", lambda e, li=li: e.activation(pTs[li][:, :], pL[li][:, :], AF.Exp, scale=scale),
                      reads=[b_pL[li]], writes=[b_pTs[li]])
                for h4 in range(4):
                    pt, hf, bpo = po[h4]
                    kb.op("pe", lambda e, li=li, h4=h4, ti=ti, pt=pt, hf=hf, st=st: e.matmul(
                        pt[:, hf * 512:hf * 512 + 129], pTs[li][:, h4 * 128:(h4 + 1) * 128],
                        vt[ti][:, h4, 0:129], start=(st == 0), stop=(st == NKTj - 1),
                        skip_group_check=True), reads=[b_pTs[li], b_vt[ti]], writes=[bpo])
            for h4 in range(4):
                pt, hf, bpo = po[h4]
                h = hg * 4 + h4
                kb.op("dve", lambda e, pt=pt, hf=hf, h4=h4: e.reciprocal(
                    rinv[:, h4:h4 + 1], pt[:, hf * 512 + 128:hf * 512 + 129]), reads=[bpo], writes=[b_rinv])
                kb.op("dve", lambda e, pt=pt, hf=hf, h4=h4, h=h: e.tensor_scalar(
                    big8[:, h * 128:(h + 1) * 128], pt[:, hf * 512:hf * 512 + 128], rinv[:, h4:h4 + 1], None, ALU.mult),
                    reads=[bpo, b_rinv], writes=[b_big8])
        kb.dma("pool", lambda e, d=yb[rows, :]: e.dma_start(out=d, in_=big8[:, :]), reads=[b_big8], via=b_big8)
    return kb.emit()


def run_dsa(zb, kv_norm_w, w_uk, w_uv, idx_ln_w, idx_ln_b):
    L = zb.shape[0]
    q, ckv, qi, ki, wi = np.split(zb, np.cumsum([2048, 512, 2048, 64])[:4], axis=1)
    Lp = -(-(48 + L) // 128) * 128
    NQT = -(-L // 128)
    NQ = -(-NQT // NCORES)
    ckv_p = np.zeros((Lp, 512), np.float32)
    ckv_p[48:48 + L] = ckv
    ki_p = np.zeros((Lp, 64), np.float32)
    ki_p[48:48 + L] = ki
    tot = NQ * NCORES * 128
    pos = np.arange(tot)
    chunk = np.where(pos < N_META, 0, 1 + (pos - N_META) // 64).astype(np.float32)

    def pad_rows(a):
        o = np.zeros((tot, a.shape[1]), np.float32)
        o[:L] = a
        return o
    qp, qip, wip = pad_rows(q), pad_rows(qi), pad_rows(wi)
    klim = []
    for j in range(NQ):
        pmax = min((8 * j + 8) * 128 - 1, L - 1)
        cmax = 0 if pmax < N_META else 1 + (pmax - N_META) // 64
        klim.append(min(Lp, -(-((cmax + 1) * 64) // 128) * 128))
    nc = build_dsa(Lp, NQ, klim)
    common = {
        "ckv": ckv_p, "kidx": ki_p,
        "wukT": np.ascontiguousarray(w_uk.transpose(1, 0, 2).reshape(512, 2048)),
        "wuvR": np.ascontiguousarray(w_uv.transpose(1, 0, 2).reshape(512, 2048)),
        "kvw_b": np.ascontiguousarray(np.broadcast_to(kv_norm_w[None, :], (128, 512))),
        "lnw_b": np.ascontiguousarray(np.broadcast_to(idx_ln_w[None, :], (128, 64))),
        "lnb_b": np.ascontiguousarray(np.broadcast_to(idx_ln_b[None, :], (128, 64))),
        "blk_in": np.ascontiguousarray(np.broadcast_to(np.arange(Lp // 64, dtype=np.float32)[None, :], (128, Lp // 64))),
        "ident_in": np.eye(128, dtype=np.float32),
    }
    maps = []
    for c in range(NCORES):
        idx = np.concatenate([np.arange((j * NCORES + c) * 128, (j * NCORES + c + 1) * 128) for j in range(NQ)])
        maps.append(dict(common, qr=qp[idx], qir=qip[idx], wir=wip[idx], qch=chunk[idx, None].copy()))
    res = run_bass_kernel_spmd(nc, maps, core_ids=list(range(NCORES)))
    out = np.zeros((tot, 2048), np.float32)
    for c in range(NCORES):
        for j in range(NQ):
            g = j * NCORES + c
            out[g * 128:(g + 1) * 128] = res.results[c]["yb"][j * 128:(j + 1) * 128]
    return out[:L]


def _host_layout(x, meta_tokens, norm_ffn_w, w_ffn_in, ffn_conv_w, ffn_conv_b, w_ffn_out, norm_final_w, NG, use_cc=True):
    D = x.shape[-1]
    DFF = w_ffn_out.shape[1]
    KC, FC = D // 128, DFF // 128
    hfull = x if meta_tokens is None else np.concatenate([meta_tokens.astype(np.float32), x[0]], axis=0)
    L = hfull.shape[0]
    maps = []
    cwc = np.zeros((128, 2 * FC * 4), np.float32)
    cw3 = ffn_conv_w[0]
    for half in range(2):
        for fc in range(FC):
            cols = slice(half * DFF + fc * 128, half * DFF + (fc + 1) * 128)
            c4 = (half * FC + fc) * 4
            cwc[:, c4 + 0] = cw3[0, cols]
            cwc[:, c4 + 1] = cw3[1, cols]
            cwc[:, c4 + 2] = cw3[2, cols]
            cwc[:, c4 + 3] = ffn_conv_b[0, cols]
    nfw_col = np.ascontiguousarray(norm_ffn_w[0].reshape(KC, 128).T)
    nfin_b = np.ascontiguousarray(np.broadcast_to(norm_final_w[None, :], (128, D)))
    ident = np.eye(128, dtype=np.float32)
    for c in range(NCORES):
        rows = np.zeros((NG * TB, D), np.float32)
        for j in range(NG):
            gidx = c * NG + j
            p0 = N_META + gidx * OWN - 2
            p1 = min(p0 + TB, L)
            if p1 > p0:
                rows[j * TB:j * TB + (p1 - p0)] = hfull[p0:p1]
        KR = D // NCORES
        if use_cc:
            wi_c = np.ascontiguousarray(w_ffn_in[0][c * KR:(c + 1) * KR, :])
            wo_c = np.ascontiguousarray(w_ffn_out[0][:, c * KR:(c + 1) * KR])
        else:
            wi_c = np.ascontiguousarray(w_ffn_in[0])
            wo_c = np.ascontiguousarray(np.concatenate(
                [w_ffn_out[0][:, r * KR:(r + 1) * KR] for r in range(NCORES)], axis=0))
        maps.append({
            "xt": rows,
            "wi_sh": wi_c,
            "wo_sh": wo_c,
            "nfw_col": nfw_col, "nfin_b": nfin_b, "cw_col": cwc, "ident_in": ident,
        })
    return maps


def run_ffn(x, meta_tokens, norm_ffn_w, w_ffn_in, ffn_conv_w, ffn_conv_b, w_ffn_out, norm_final_w, use_cc=True):
    if meta_tokens is None:
        S, D = x.shape[0] - N_META, x.shape[1]
    else:
        S, D = x.shape[1], x.shape[2]
    DFF = w_ffn_out.shape[1]
    n_groups = -(-S // OWN)
    NG = -(-n_groups // NCORES)
    nc = build_program(D, DFF, NG, use_cc)
    maps = _host_layout(x, meta_tokens, norm_ffn_w, w_ffn_in, ffn_conv_w, ffn_conv_b, w_ffn_out,
                        norm_final_w, NG, use_cc)
    res = run_bass_kernel_spmd(nc, maps, core_ids=list(range(NCORES)))
    y = np.concatenate([res.results[c]["out"] for c in range(NCORES)], axis=0)[:S]
    return y[None].astype(np.float32)


A_COLS = 6592


def kernel(x, meta_tokens, norm_mix_w, w_in, mu_shift, rwkv_w0, rwkv_w2, rwkv_a0, rwkv_a2,
           rwkv_g2, rwkv_k_k, rwkv_k_a, rwkv_r_k, rwkv_ln_w, rwkv_ln_b, kv_norm_w, w_uk, w_uv,
           idx_ln_w, idx_ln_b, w_proj_a, w_proj_b, w_gate, w_out, norm_ffn_w, w_ffn_in,
           ffn_conv_w, ffn_conv_b, w_ffn_out, norm_final_w):
    f = lambda a: np.asarray(a, np.float32)
    hfull = np.concatenate([f(meta_tokens), f(x)[0]], axis=0)
    z, gates = run_mixin(hfull, f(norm_mix_w)[0], f(w_in)[0], f(w_gate)[0])
    yb = run_dsa(np.ascontiguousarray(z[:, A_COLS:]), f(kv_norm_w)[0], f(w_uk)[0], f(w_uv)[0],
                 f(idx_ln_w)[0], f(idx_ln_b)[0])
    P = run_rwkv_prep(np.ascontiguousarray(z[:, :A_COLS]), f(mu_shift)[0], f(rwkv_w0)[0], f(rwkv_w2)[0],
                      f(rwkv_a0)[0], f(rwkv_a2)[0], f(rwkv_g2)[0], f(rwkv_k_k)[0], f(rwkv_k_a)[0])
    yscan = run_rwkv_scan(P)
    ya = run_rwkv_post(yscan, P, f(rwkv_ln_w)[0], f(rwkv_ln_b)[0], f(rwkv_r_k)[0])
    del P, yscan
    h1 = run_mixout(hfull, ya, yb, gates, f(w_proj_a)[0], f(w_proj_b)[0], f(w_out)[0])
    return run_ffn(h1, None, f(norm_ffn_w), f(w_ffn_in), f(ffn_conv_w), f(ffn_conv_b), f(w_ffn_out),
                   f(norm_final_w), use_cc=False)
```

```python
from contextlib import ExitStack

import numpy as np
import concourse.bass as bass
import concourse.mybir as mybir
from concourse.bass_utils import run_bass_kernel_spmd

F32 = mybir.dt.float32
BF16 = mybir.dt.bfloat16
AF = mybir.ActivationFunctionType
ALU = mybir.AluOpType

NCORES = 8
N_META = 16
EPS = 1e-6
TB = 256
OWN = TB - 2


class Buf:
    __slots__ = ("name", "w", "rd", "dsem", "dcnt")

    def __init__(self, name):
        self.name = name
        self.w = None
        self.rd = {}
        self.dsem = None
        self.dcnt = 0


class KB:
    ENG = ("pe", "act", "dve", "pool", "sp")

    def __init__(self):
        self.nc = bass.Bass("TRN2", target_bir_lowering=False)
        self.st = ExitStack()
        self.ops = {e: [] for e in self.ENG}
        self.cnt = {e: 0 for e in self.ENG}
        self.semnames = []
        self.csem = {e: self._newsem(e) for e in ("pe", "act", "dve", "pool")}
        self.abufs = []
        self.cfinal = {}

    def _newsem(self, name):
        self.semnames.append(name)
        return len(self.semnames) - 1

    def sb(self, name, shape, dt):
        return self.st.enter_context(self.nc.sbuf_tensor(name, list(shape), dt))

    def ps(self, name, shape, dt=F32):
        return self.st.enter_context(self.nc.psum_tensor(name, list(shape), dt))

    def dram(self, name, shape, dt, kind="Internal"):
        if kind == "Internal":
            return self.nc.dram_tensor(name, list(shape), dt)
        return self.nc.dram_tensor(name, list(shape), dt, kind=kind)

    def _deps(self, reads, writes, own):
        waits = []
        for b in list(reads) + list(writes):
            if b.w is not None and b.w[0] != own:
                waits.append(b.w)
        for b in writes:
            for s, v in b.rd.items():
                if s != own:
                    waits.append((s, v))
        return waits

    def _commit(self, reads, writes, ev):
        for b in reads:
            if b.rd.get(ev[0], 0) < ev[1]:
                b.rd[ev[0]] = ev[1]
        for b in writes:
            b.w = ev
            b.rd = {}

    SEM_EPOCH = 48000

    def op(self, eng, fn, reads=(), writes=()):
        if self.cnt[eng] >= self.SEM_EPOCH:
            self.csem[eng] = self._newsem(f"{eng}_e{len(self.semnames)}")
            self.cnt[eng] = 0
        own = self.csem[eng]
        waits = self._deps(reads, writes, own if eng == "pe" else -1)
        self.cnt[eng] += 1
        ev = (own, self.cnt[eng])
        self.cfinal[own] = self.cnt[eng]
        self._commit(reads, writes, ev)
        self.ops[eng].append((fn, waits, (own, 1)))

    def dma(self, q, fn, reads=(), writes=(), via=None, inc=16):
        if via.dsem is None:
            via.dsem = self._newsem("d_" + via.name)
            self.abufs.append(via)
        waits = self._deps(reads, writes, -1)
        via.dcnt += inc
        ev = (via.dsem, via.dcnt)
        self._commit(reads, writes, ev)
        self.ops[q].append((fn, waits, (via.dsem, inc)))

    def emit(self):
        nc = self.nc
        with ExitStack() as st:
            sems = [st.enter_context(nc.semaphore(n)) for n in self.semnames]
            block = st.enter_context(nc.Block())
            final = [(b.dsem, b.dcnt) for b in self.abufs]

            def mk(name):
                def body(e):
                    waited = {}
                    for fn, waits, inc in self.ops[name]:
                        for s, v in waits:
                            if waited.get(s, 0) < v:
                                e.wait_ge(sems[s], v)
                                waited[s] = v
                        ins = fn(e)
                        if inc[1] == 1:
                            ins.then_inc(sems[inc[0]])
                        else:
                            ins.then_inc(sems[inc[0]], inc[1])
                    if name == "sp":
                        for s, v in final:
                            if waited.get(s, 0) < v:
                                e.wait_ge(sems[s], v)
                        for s, v in self.cfinal.items():
                            e.wait_ge(sems[s], v)
                return body

            block.tensor(mk("pe"))
            block.scalar(mk("act"))
            block.vector(mk("dve"))
            block.gpsimd(mk("pool"))
            block.sync(mk("sp"))
        self.st.close()
        return nc


def build_program(D, DFF, NG, use_cc=True):
    kb = KB()
    nc = kb.nc
    KC = D // 128
    FC = DFF // 128
    KR = D // NCORES
    NBI = 2
    CBI = 2 * DFF // NBI
    NRW = D // NCORES
    NBO = 2
    RBO = DFF // NBO
    assert KR % 128 == 0 and CBI % 128 == 0 and NRW == 512 or True

    xt = kb.dram("xt", [NG * TB, D], F32, kind="ExternalInput")
    NR = 1 if use_cc else NCORES
    wi_sh = kb.dram("wi_sh", [NR * KR, 2 * DFF], F32, kind="ExternalInput")
    wo_sh = kb.dram("wo_sh", [NR * DFF, NRW], F32, kind="ExternalInput")
    nfw_col = kb.dram("nfw_col", [128, KC], F32, kind="ExternalInput")
    nfin_b = kb.dram("nfin_b", [128, D], F32, kind="ExternalInput")
    cw_col = kb.dram("cw_col", [128, 2 * FC * 4], F32, kind="ExternalInput")
    ident_in = kb.dram("ident_in", [128, 128], F32, kind="ExternalInput")
    out = kb.dram("out", [NG * OWN, D], F32, kind="ExternalOutput")

    KK = KR // 128
    assert FC % NBI == 0
    FCB = FC // NBI
    bi = [kb.dram(f"bi{q}", [FCB * 128, KK * 256], BF16) for q in range(NBI)]
    gi = [kb.dram(f"gi{q}", [NCORES * FCB * 128, KK * 256], BF16) for q in range(NBI)]
    bo = [kb.dram(f"bo{q}", [RBO, NRW], BF16) for q in range(NBO)]
    go = [kb.dram(f"go{q}", [NCORES * RBO, NRW], BF16) for q in range(NBO)]
    b_bi = [Buf(f"bi{q}") for q in range(NBI)]
    b_gi = [Buf(f"gi{q}") for q in range(NBI)]
    b_bo = [Buf(f"bo{q}") for q in range(NBO)]
    b_go = [Buf(f"go{q}") for q in range(NBO)]

    ident_f = kb.sb("ident_f", [128, 128], F32)
    ident = kb.sb("ident", [128, 128], BF16)
    nfw = kb.sb("nfw", [128, KC], F32)
    nfin = kb.sb("nfin", [128, D], F32)
    cw = kb.sb("cw", [128, 2 * FC * 4], F32)
    b_const = Buf("const")
    kb.dma("sp", lambda e: e.dma_start(out=ident_f[:, :], in_=ident_in[:, :]), writes=[b_const], via=b_const)
    kb.dma("sp", lambda e: e.dma_start(out=nfw[:, :], in_=nfw_col[:, :]), writes=[b_const], via=b_const)
    kb.dma("sp", lambda e: e.dma_start(out=nfin[:, :], in_=nfin_b[:, :]), writes=[b_const], via=b_const)
    kb.dma("sp", lambda e: e.dma_start(out=cw[:, :], in_=cw_col[:, :]), writes=[b_const], via=b_const)
    b_ident = Buf("ident")
    kb.op("dve", lambda e: e.tensor_copy(ident[:, :], ident_f[:, :]), reads=[b_const], writes=[b_ident])

    CW = 1024
    stg_f = [kb.sb(f"stg_f{i}", [128, CW], F32) for i in range(2)]
    stg_b = [kb.sb(f"stg_b{i}", [128, CW], BF16) for i in range(2)]
    b_sf = [Buf(f"stg_f{i}") for i in range(2)]
    b_sb = [Buf(f"stg_b{i}") for i in range(2)]
    it = [0]

    def cast_piece(src_ap, dst_ap, dst_buf, rows, cols, nf=None):
        i = it[0] % 2
        it[0] += 1
        q = "sp" if (it[0] % 2) else "act"
        kb.dma(q, lambda e: e.dma_start(out=stg_f[i][:rows, :cols], in_=src_ap), writes=[b_sf[i]], via=b_sf[i])
        eng = "dve" if (it[0] % 2) else "pool"
        kb.op(eng, lambda e: e.tensor_copy(stg_b[i][:rows, :cols], stg_f[i][:rows, :cols]),
              reads=[b_sf[i]], writes=[b_sb[i]])
        sview = stg_b[i][:rows, :cols]
        if nf is not None:
            sview = sview.rearrange("p (f n) -> p f n", f=nf)
        kb.dma("sp", lambda e: e.dma_start(out=dst_ap, in_=sview),
               reads=[b_sb[i]], writes=[dst_buf], via=b_sb[i])

    for q in range(NBI):
        for r in range(NR):
            if use_cc:
                bv = bi[q].ap().rearrange("(f p) (k h n) -> p f k h n", p=128, k=KK, h=2)
                dbuf = b_bi[q]
            else:
                bv = gi[q].ap().rearrange("(r f p) (k h n) -> r p f k h n", r=NCORES, p=128, k=KK, h=2)[r]
                dbuf = b_gi[q]
            for kk in range(KK):
                for half in range(2):
                    for f0 in range(0, FCB, CW // 128):
                        nf = min(CW // 128, FCB - f0)
                        c0 = half * DFF + (q * FCB + f0) * 128
                        cast_piece(wi_sh[r * KR + kk * 128:r * KR + (kk + 1) * 128, c0:c0 + nf * 128],
                                   bv[:, f0:f0 + nf, kk, half, :], dbuf, 128, nf * 128, nf)
        if use_cc:
            kb.dma("pool", (lambda q: lambda e: e.collective_compute(
                "AllGather", ALU.bypass, replica_groups=[list(range(NCORES))],
                ins=[bi[q].ap().opt()], outs=[gi[q].ap().opt()]))(q),
                reads=[b_bi[q]], writes=[b_gi[q]], via=b_gi[q], inc=1)
    for q in range(NBO):
        for r in range(NR):
            for r0 in range(0, RBO, 128):
                rr = min(128, RBO - r0)
                if use_cc:
                    dst, dbuf = bo[q][r0:r0 + rr, :], b_bo[q]
                else:
                    dst, dbuf = go[q][r * RBO + r0:r * RBO + r0 + rr, :], b_go[q]
                cast_piece(wo_sh[r * DFF + q * RBO + r0:r * DFF + q * RBO + r0 + rr, :], dst, dbuf, rr, NRW)
        if use_cc:
            kb.dma("pool", (lambda q: lambda e: e.collective_compute(
                "AllGather", ALU.bypass, replica_groups=[list(range(NCORES))],
                ins=[bo[q].ap().opt()], outs=[go[q].ap().opt()]))(q),
                reads=[b_bo[q]], writes=[b_go[q]], via=b_go[q], inc=1)

    NT = TB // 128
    h_t = [kb.sb(f"h{t}", [128, D], F32) for t in range(NT)]
    b_h = [Buf(f"h{t}") for t in range(NT)]
    ub = kb.sb("ub", [128, D], BF16)
    b_ub = Buf("ub")
    ss = kb.sb("ss", [128, 2], F32)
    b_ss = Buf("ss")
    uT = kb.sb("uT", [128, KC, TB], BF16)
    b_uT = Buf("uT")
    hid = kb.sb("hid", [128, FC, TB], BF16)
    b_hid = Buf("hid")
    wi_t = [kb.sb(f"wi_t{i}", [128, KC, 256], BF16) for i in range(3)]
    b_wi = [Buf(f"wi_t{i}") for i in range(3)]
    wo_t = [kb.sb(f"wo_t{i}", [128, 2048], BF16) for i in range(3)]
    b_wo = [Buf(f"wo_t{i}") for i in range(3)]
    yf = [kb.sb(f"yf{i}", [128, 2, TB], F32) for i in range(2)]
    b_yf = [Buf(f"yf{i}") for i in range(2)]
    zf = [kb.sb(f"zf{i}", [128, 2, TB], F32) for i in range(2)]
    b_zf = [Buf(f"zf{i}") for i in range(2)]
    p_tr = [kb.ps(f"p_tr{i}", [128, 1024], BF16) for i in range(2)]
    b_ptr = [Buf(f"p_tr{i}") for i in range(2)]
    p_y = [kb.ps(f"p_y{i}", [128, 2, TB], F32) for i in range(2)]
    b_py = [Buf(f"p_y{i}") for i in range(2)]
    p_o = kb.ps("p_o", [128, 4, 512], F32)
    b_po = [Buf(f"p_o{i}") for i in range(4)]

    def rms_rstd(src_t, sbuf, col):
        kb.op("dve", lambda e, src_t=src_t: e.tensor_tensor(ub[:, :], src_t[:, :], src_t[:, :], ALU.mult),
              reads=[sbuf], writes=[b_ub])
        kb.op("dve", lambda e, col=col: e.reduce_sum(ss[:, col:col + 1], ub[:, :], mybir.AxisListType.X),
              reads=[b_ub], writes=[b_ss])
        kb.op("dve", lambda e, col=col: e.tensor_scalar(ss[:, col:col + 1], ss[:, col:col + 1], 1.0 / D, EPS,
                                                        ALU.mult, ALU.add), reads=[b_ss], writes=[b_ss])
        kb.op("act", lambda e, col=col: e.activation(ss[:, col:col + 1], ss[:, col:col + 1], AF.Sqrt),
              reads=[b_ss], writes=[b_ss])
        kb.op("dve", lambda e, col=col: e.reciprocal(ss[:, col:col + 1], ss[:, col:col + 1]),
              reads=[b_ss], writes=[b_ss])

    wcount = [0]
    for g in range(NG):
        for t in range(NT):
            kb.dma("pool", lambda e, dst=h_t[t][:, :], s=xt[g * TB + t * 128:g * TB + (t + 1) * 128, :]:
                   e.dma_start(out=dst, in_=s), writes=[b_h[t]], via=b_h[t])
        for t in range(NT):
            rms_rstd(h_t[t], b_h[t], 0)
            kb.op("dve", lambda e, t=t: e.tensor_scalar(ub[:, :], h_t[t][:, :], ss[:, 0:1], None, ALU.mult),
                  reads=[b_h[t], b_ss], writes=[b_ub])
            for k8 in range(0, KC, 8):
                pi = (k8 // 8) % 2
                nk = min(8, KC - k8)
                for k in range(k8, k8 + nk):
                    kb.op("pe", lambda e, k=k, pi=pi: e.transpose(
                        p_tr[pi][:, (k % 8) * 128:(k % 8 + 1) * 128], ub[:, k * 128:(k + 1) * 128], ident[:, :]),
                        reads=[b_ub, b_ident], writes=[b_ptr[pi]])
                for k in range(k8, k8 + nk):
                    kb.op("dve", lambda e, k=k, pi=pi, t=t: e.tensor_scalar(
                        uT[:, k, t * 128:(t + 1) * 128], p_tr[pi][:, (k % 8) * 128:(k % 8 + 1) * 128],
                        nfw[:, k:k + 1], None, ALU.mult),
                        reads=[b_ptr[pi], b_const], writes=[b_uT])
        for fc in range(FC):
            wi = wcount[0] % 3
            wcount[0] += 1
            qi, fl = divmod(fc, FCB)
            gvi = gi[qi].ap().rearrange("(r f p) (k c) -> f p r k c", r=NCORES, p=128, k=KK)
            kb.dma("sp",
                   lambda e, s=gvi[fl], dst=wi_t[wi][:, :, :].rearrange("p (r k) c -> p r k c", r=NCORES):
                   e.dma_start(out=dst, in_=s),
                   reads=[b_gi[qi]], writes=[b_wi[wi]], via=b_wi[wi])
            pi = fc % 2
            for half in range(2):
                for k in range(KC):
                    kb.op("pe", lambda e, k=k, half=half, wi=wi, pi=pi: e.matmul(
                        p_y[pi][:, half, :], wi_t[wi][:, k, half * 128:(half + 1) * 128], uT[:, k, :],
                        start=(k == 0), stop=(k == KC - 1)),
                        reads=[b_wi[wi], b_uT], writes=[b_py[pi]])
            yi = fc % 2
            kb.op("act", lambda e, pi=pi, yi=yi: e.activation(yf[yi][:, :, :], p_y[pi][:, :, :], AF.Copy),
                  reads=[b_py[pi]], writes=[b_yf[yi]])
            for half in range(2):
                c4 = (half * FC + fc) * 4
                kb.op("dve", lambda e, half=half, yi=yi, c4=c4: e.tensor_scalar(
                    zf[yi][:, half, :], yf[yi][:, half, :], cw[:, c4 + 2:c4 + 3], cw[:, c4 + 3:c4 + 4],
                    ALU.mult, ALU.add), reads=[b_yf[yi], b_const], writes=[b_zf[yi]])
                kb.op("dve", lambda e, half=half, yi=yi, c4=c4: e.scalar_tensor_tensor(
                    zf[yi][:, half, 1:TB], yf[yi][:, half, 0:TB - 1], cw[:, c4 + 1:c4 + 2], zf[yi][:, half, 1:TB],
                    ALU.mult, ALU.add), reads=[b_yf[yi], b_const, b_zf[yi]], writes=[b_zf[yi]])
                kb.op("dve", lambda e, half=half, yi=yi, c4=c4: e.scalar_tensor_tensor(
                    zf[yi][:, half, 2:TB], yf[yi][:, half, 0:TB - 2], cw[:, c4:c4 + 1], zf[yi][:, half, 2:TB],
                    ALU.mult, ALU.add), reads=[b_yf[yi], b_const, b_zf[yi]], writes=[b_zf[yi]])
            kb.op("act", lambda e, yi=yi: e.activation(yf[yi][:, 0, :], zf[yi][:, 0, :], AF.Silu),
                  reads=[b_zf[yi]], writes=[b_yf[yi]])
            kb.op("pool", lambda e, yi=yi, fc=fc: e.tensor_tensor(
                hid[:, fc, :], yf[yi][:, 0, :], zf[yi][:, 1, :], ALU.mult),
                reads=[b_yf[yi], b_zf[yi]], writes=[b_hid])
        for t in range(NT):
            for nh in range(-(-D // 2048)):
                ncols = min(2048, D - nh * 2048)
                nb = ncols // 512
                for fc in range(FC):
                    wo = wcount[0] % 3
                    wcount[0] += 1
                    qo, fr = divmod(fc * 128, RBO)
                    gv = go[qo].ap().rearrange("(r f) n -> f r n", r=NCORES)
                    src_ap = gv[fr:fr + 128, nh * 4:nh * 4 + nb, :]
                    dst_ap = wo_t[wo][:, :nb * 512].rearrange("p (r n) -> p r n", n=512)
                    kb.dma("sp",
                           lambda e, s=src_ap, dst=dst_ap: e.dma_start(out=dst, in_=s),
                           reads=[b_go[qo]], writes=[b_wo[wo]], via=b_wo[wo])
                    for j in range(nb):
                        kb.op("pe", lambda e, j=j, fc=fc, wo=wo, t=t: e.matmul(
                            p_o[:, j, :], hid[:, fc, t * 128:(t + 1) * 128], wo_t[wo][:, j * 512:(j + 1) * 512],
                            start=(fc == 0), stop=(fc == FC - 1)),
                            reads=[b_hid, b_wo[wo]], writes=[b_po[j]])
                for j in range(nb):
                    c0 = nh * 2048 + j * 512
                    kb.op("dve", lambda e, j=j, c0=c0, t=t: e.tensor_tensor(
                        h_t[t][:, c0:c0 + 512], p_o[:, j, :], h_t[t][:, c0:c0 + 512], ALU.add),
                        reads=[b_po[j], b_h[t]], writes=[b_h[t]])
            rms_rstd(h_t[t], b_h[t], 1)
            kb.op("dve", lambda e, t=t: e.scalar_tensor_tensor(
                h_t[t][:, :], h_t[t][:, :], ss[:, 1:2], nfin[:, :], ALU.mult, ALU.mult),
                reads=[b_h[t], b_ss, b_const], writes=[b_h[t]])
            if t == 0:
                kb.dma("pool", lambda e, dst=out[g * OWN:g * OWN + 126, :], s=h_t[0][2:128, :]:
                       e.dma_start(out=dst, in_=s), reads=[b_h[0]], via=b_h[0])
            else:
                r0 = g * OWN + 126 + (t - 1) * 128
                kb.dma("pool", lambda e, dst=out[r0:r0 + 128, :], s=h_t[t][:, :]:
                       e.dma_start(out=dst, in_=s), reads=[b_h[t]], via=b_h[t])
    return kb.emit()


class Shared:
    def __init__(self, kb, D, ident_in, wt_kc):
        self.kb = kb
        self.D = D
        self.ident_f = kb.sb("ident_f", [128, 128], F32)
        self.ident = kb.sb("ident", [128, 128], BF16)
        self.b_ident = Buf("ident")
        b0 = Buf("ident_f")
        kb.dma("sp", lambda e: e.dma_start(out=self.ident_f[:, :], in_=ident_in[:, :]), writes=[b0], via=b0)
        kb.op("dve", lambda e: e.tensor_copy(self.ident[:, :], self.ident_f[:, :]), reads=[b0], writes=[self.b_ident])
        self.ub = kb.sb("ub", [128, D], BF16)
        self.b_ub = Buf("ub")
        self.ss = kb.sb("ss", [128, 4], F32)
        self.b_ss = Buf("ss")
        self.p_tr = [kb.ps(f"p_tr{i}", [128, 1024], BF16) for i in range(2)]
        self.b_ptr = [Buf(f"p_tr{i}") for i in range(2)]
        self.p_o = kb.ps("p_o", [128, 4, 512], F32)
        self.b_po = [Buf(f"p_o{i}") for i in range(4)]
        self.po_i = 0
        self.wt = [kb.sb(f"wt{i}", [128, wt_kc, 512], BF16) for i in range(2)]
        self.b_wt = [Buf(f"wt{i}") for i in range(2)]
        self.wt_i = 0
        CW = 1024
        self.CW = CW
        self.stg_f = [kb.sb(f"stg_f{i}", [128, CW], F32) for i in range(2)]
        self.stg_b = [kb.sb(f"stg_b{i}", [128, CW], BF16) for i in range(2)]
        self.b_sf = [Buf(f"stg_f{i}") for i in range(2)]
        self.b_sb = [Buf(f"stg_b{i}") for i in range(2)]
        self.stg_i = 0


def emit_cast_tiled(S, src, K, N, name):
    kb = S.kb
    KC, NB = K // 128, N // 512
    assert N % 512 == 0 and K % 128 == 0
    scr = kb.dram(name, [NB * 128, KC * 512], BF16)
    b_scr = Buf(name)
    sv = scr.ap().rearrange("(nb p) (k n) -> p nb k n", p=128, k=KC)
    for k in range(KC):
        for nb0 in range(0, NB, 2):
            nn = min(2, NB - nb0)
            i = S.stg_i % 2
            S.stg_i += 1
            cols = nn * 512
            kb.dma("sp" if S.stg_i % 2 else "act",
                   lambda e, i=i, cols=cols, s=src[k * 128:(k + 1) * 128, nb0 * 512:nb0 * 512 + cols]:
                   e.dma_start(out=S.stg_f[i][:, :cols], in_=s), writes=[S.b_sf[i]], via=S.b_sf[i])
            kb.op("dve" if S.stg_i % 2 else "pool",
                  lambda e, i=i, cols=cols: e.tensor_copy(S.stg_b[i][:, :cols], S.stg_f[i][:, :cols]),
                  reads=[S.b_sf[i]], writes=[S.b_sb[i]])
            kb.dma("sp", lambda e, i=i, cols=cols, nn=nn, d=sv[:, nb0:nb0 + nn, k, :]:
                   e.dma_start(out=d, in_=S.stg_b[i][:, :cols].rearrange("p (a n) -> p a n", a=nn)),
                   reads=[S.b_sb[i]], writes=[b_scr], via=S.b_sb[i])
    return scr, b_scr


def emit_rstd(S, src_t, b_src, col, width):
    kb = S.kb
    kb.op("dve", lambda e: e.tensor_tensor(S.ub[:, :width], src_t[:, :width], src_t[:, :width], ALU.mult),
          reads=[b_src], writes=[S.b_ub])
    kb.op("dve", lambda e: e.reduce_sum(S.ss[:, col:col + 1], S.ub[:, :width], mybir.AxisListType.X),
          reads=[S.b_ub], writes=[S.b_ss])
    kb.op("dve", lambda e: e.tensor_scalar(S.ss[:, col:col + 1], S.ss[:, col:col + 1], 1.0 / width, EPS,
                                           ALU.mult, ALU.add), reads=[S.b_ss], writes=[S.b_ss])
    kb.op("act", lambda e: e.activation(S.ss[:, col:col + 1], S.ss[:, col:col + 1], AF.Sqrt),
          reads=[S.b_ss], writes=[S.b_ss])
    kb.op("dve", lambda e: e.reciprocal(S.ss[:, col:col + 1], S.ss[:, col:col + 1]),
          reads=[S.b_ss], writes=[S.b_ss])


def emit_transposeT(S, width, dstT, b_dst, t, scale_cols=None, b_scale=None):
    kb = S.kb
    KC = width // 128
    for k8 in range(0, KC, 8):
        pi = (k8 // 8) % 2
        nk = min(8, KC - k8)
        for k in range(k8, k8 + nk):
            kb.op("pe", lambda e, k=k, pi=pi: e.transpose(
                S.p_tr[pi][:, (k % 8) * 128:(k % 8 + 1) * 128], S.ub[:, k * 128:(k + 1) * 128], S.ident[:, :]),
                reads=[S.b_ub, S.b_ident], writes=[S.b_ptr[pi]])
        for k in range(k8, k8 + nk):
            if scale_cols is not None:
                kb.op("dve", lambda e, k=k, pi=pi: e.tensor_scalar(
                    dstT[:, k, t * 128:(t + 1) * 128], S.p_tr[pi][:, (k % 8) * 128:(k % 8 + 1) * 128],
                    scale_cols[:, k:k + 1], None, ALU.mult),
                    reads=[S.b_ptr[pi], b_scale], writes=[b_dst])
            else:
                kb.op("dve", lambda e, k=k, pi=pi: e.tensor_copy(
                    dstT[:, k, t * 128:(t + 1) * 128], S.p_tr[pi][:, (k % 8) * 128:(k % 8 + 1) * 128]),
                    reads=[S.b_ptr[pi]], writes=[b_dst])


def emit_gemm_tok(S, lhsT, b_lhs, KC, scr, b_scr, NB, NT, evac):
    kb = S.kb
    for nb in range(NB):
        wi = S.wt_i % 2
        S.wt_i += 1
        kb.dma("sp",
               lambda e, wi=wi, s=scr[nb * 128:(nb + 1) * 128, :]:
               e.dma_start(out=S.wt[wi][:, :KC, :], in_=s.rearrange("p (k n) -> p k n", k=KC)),
               reads=[b_scr], writes=[S.b_wt[wi]], via=S.b_wt[wi])
        for t in range(NT):
            j = S.po_i % 4
            S.po_i += 1
            for k in range(KC):
                kb.op("pe", lambda e, k=k, wi=wi, j=j, t=t: e.matmul(
                    S.p_o[:, j, :], lhsT[:, k, t * 128:(t + 1) * 128], S.wt[wi][:, k, :],
                    start=(k == 0), stop=(k == KC - 1)),
                    reads=[b_lhs, S.b_wt[wi]], writes=[S.b_po[j]])
            evac(t, nb, S.p_o[:, j, :], S.b_po[j])


def build_mixin(D, N1, N2, NG):
    kb = KB()
    KC = D // 128
    NT = TB // 128
    xt = kb.dram("xt", [NG * TB, D], F32, kind="ExternalInput")
    w1 = kb.dram("w1", [D, N1], F32, kind="ExternalInput")
    w2 = kb.dram("w2", [D, N2], F32, kind="ExternalInput")
    nw_col = kb.dram("nw_col", [128, KC], F32, kind="ExternalInput")
    ident_in = kb.dram("ident_in", [128, 128], F32, kind="ExternalInput")
    z = kb.dram("z", [NG * TB, N1], F32, kind="ExternalOutput")
    gt = kb.dram("gt", [NG * TB, N2], F32, kind="ExternalOutput")
    S = Shared(kb, D, ident_in, KC)
    nw = kb.sb("nw", [128, KC], F32)
    b_nw = Buf("nw")
    kb.dma("sp", lambda e: e.dma_start(out=nw[:, :], in_=nw_col[:, :]), writes=[b_nw], via=b_nw)
    s1, b_s1 = emit_cast_tiled(S, w1, D, N1, "s_w1")
    s2, b_s2 = emit_cast_tiled(S, w2, D, N2, "s_w2")
    h_t = [kb.sb(f"h{t}", [128, D], F32) for t in range(NT)]
    b_h = [Buf(f"h{t}") for t in range(NT)]
    uT = kb.sb("uT", [128, KC, TB], BF16)
    b_uT = Buf("uT")
    og = [kb.sb(f"og{i}", [128, 512], F32) for i in range(4)]
    b_og = [Buf(f"og{i}") for i in range(4)]
    oc = [0]
    for g in range(NG):
        for t in range(NT):
            kb.dma("pool", lambda e, dst=h_t[t][:, :], s=xt[g * TB + t * 128:g * TB + (t + 1) * 128, :]:
                   e.dma_start(out=dst, in_=s), writes=[b_h[t]], via=b_h[t])
        for t in range(NT):
            emit_rstd(S, h_t[t], b_h[t], 0, D)
            kb.op("dve", lambda e, t=t: e.tensor_scalar(S.ub[:, :], h_t[t][:, :], S.ss[:, 0:1], None, ALU.mult),
                  reads=[b_h[t], S.b_ss], writes=[S.b_ub])
            emit_transposeT(S, D, uT, b_uT, t, nw, b_nw)

        def mk_evac(dst, func):
            def evac(t, nb, ps, b_ps):
                i = oc[0] % 4
                oc[0] += 1
                kb.op("act", lambda e: e.activation(og[i][:, :], ps, func), reads=[b_ps], writes=[b_og[i]])
                r0 = g * TB + t * 128
                kb.dma("pool", lambda e: e.dma_start(out=dst[r0:r0 + 128, nb * 512:(nb + 1) * 512], in_=og[i][:, :]),
                       reads=[b_og[i]], via=b_og[i])
            return evac
        emit_gemm_tok(S, uT, b_uT, KC, s1, b_s1, N1 // 512, NT, mk_evac(z, AF.Copy))
        emit_gemm_tok(S, uT, b_uT, KC, s2, b_s2, N2 // 512, NT, mk_evac(gt, AF.Sigmoid))
    return kb.emit()


def _group_rows(hfull, NG, halo):
    own = TB - halo
    L, D = hfull.shape
    outs = []
    for c in range(NCORES):
        rows = np.zeros((NG * TB, D), hfull.dtype)
        for j in range(NG):
            p0 = (c * NG + j) * own - halo
            a, b = max(p0, 0), min(p0 + TB, L)
            if b > a:
                rows[j * TB + (a - p0):j * TB + (b - p0)] = hfull[a:b]
        outs.append(rows)
    return outs


def _ungroup_rows(per_core, NG, halo, L):
    own = TB - halo
    chunks = []
    for c in range(NCORES):
        a = per_core[c].reshape(NG, TB, -1)[:, halo:, :]
        chunks.append(a.reshape(NG * own, -1))
    return np.concatenate(chunks, axis=0)[:L]


def run_mixin(hfull, norm_mix_w, w_in, w_gate):
    L, D = hfull.shape
    N1 = -(-w_in.shape[1] // 512) * 512
    N2 = w_gate.shape[1]
    NG = -(-(-(-L // TB)) // NCORES)
    nc = build_mixin(D, N1, N2, NG)
    w1 = np.zeros((D, N1), np.float32)
    w1[:, :w_in.shape[1]] = w_in
    rows = _group_rows(hfull, NG, 0)
    nw_col = np.ascontiguousarray(norm_mix_w.reshape(D // 128, 128).T)
    ident = np.eye(128, dtype=np.float32)
    maps = [{"xt": rows[c], "w1": w1, "w2": np.ascontiguousarray(w_gate), "nw_col": nw_col, "ident_in": ident}
            for c in range(NCORES)]
    res = run_bass_kernel_spmd(nc, maps, core_ids=list(range(NCORES)))
    z = _ungroup_rows([res.results[c]["z"] for c in range(NCORES)], NG, 0, L)[:, :w_in.shape[1]]
    gt = _ungroup_rows([res.results[c]["gt"] for c in range(NCORES)], NG, 0, L)
    return z, gt


AW = 2048
LW, LA, LG = 96, 96, 256


def build_rwkv_prep(NT128):
    kb = KB()
    ACOLS = 3 * AW + LW + LA + LG
    za = kb.dram("za", [NT128 * 128, ACOLS], F32, kind="ExternalInput")
    zp = kb.dram("zp", [NT128 * 128, ACOLS], F32, kind="ExternalInput")
    mu_b = kb.dram("mu_b", [128, ACOLS], F32, kind="ExternalInput")
    cst = kb.dram("cst", [4, 128, AW], F32, kind="ExternalInput")
    w2i = kb.dram("w2i", [LW, AW], F32, kind="ExternalInput")
    a2i = kb.dram("a2i", [LA, AW], F32, kind="ExternalInput")
    g2i = kb.dram("g2i", [LG, AW], F32, kind="ExternalInput")
    ident_in = kb.dram("ident_in", [128, 128], F32, kind="ExternalInput")
    outs = {n: kb.dram(n, [NT128 * 128, AW], F32, kind="ExternalOutput")
            for n in ("o_r", "o_kp", "o_v", "o_kap", "o_b", "o_dec", "o_g")}
    ident = kb.sb("ident", [128, 128], F32)
    mu = kb.sb("mu", [128, ACOLS], F32)
    cs = kb.sb("cs", [128, 4, AW], F32)
    w2 = kb.sb("w2", [128, AW], F32)
    a2 = kb.sb("a2", [128, AW], F32)
    g2 = kb.sb("g2", [128, 2, AW], F32)
    b_c = Buf("c")
    kb.dma("sp", lambda e: e.dma_start(out=ident[:, :], in_=ident_in[:, :]), writes=[b_c], via=b_c)
    kb.dma("sp", lambda e: e.dma_start(out=mu[:, :], in_=mu_b[:, :]), writes=[b_c], via=b_c)
    for i in range(4):
        kb.dma("act", lambda e, i=i: e.dma_start(out=cs[:, i, :], in_=cst[i, :, :]), writes=[b_c], via=b_c)
    kb.dma("sp", lambda e: e.dma_start(out=w2[0:LW, :], in_=w2i[:, :]), writes=[b_c], via=b_c)
    kb.dma("sp", lambda e: e.dma_start(out=a2[0:LA, :], in_=a2i[:, :]), writes=[b_c], via=b_c)
    kb.dma("sp", lambda e: e.dma_start(out=g2[:, :, :], in_=g2i.ap().rearrange("(k p) n -> p k n", p=128)),
           writes=[b_c], via=b_c)
    zt = kb.sb("zt", [128, ACOLS], F32)
    pt = kb.sb("pt", [128, ACOLS], F32)
    b_zt, b_pt = Buf("zt"), Buf("pt")
    W = [kb.sb(f"W{i}", [128, AW], F32) for i in range(5)]
    b_W = [Buf(f"W{i}") for i in range(5)]
    sm = kb.sb("sm", [128, 512], F32)
    b_sm = Buf("sm")
    lT = kb.sb("lT", [128, 4, 128], F32)
    b_lT = Buf("lT")
    hs = kb.sb("hs", [128, 64], F32)
    b_hs = Buf("hs")
    ptr = kb.ps("ptr", [128, 512], F32)
    b_ptr = Buf("ptr")
    pm = [kb.ps(f"pm{i}", [128, 512], F32) for i in range(2)]
    b_pm = [Buf(f"pm{i}") for i in range(2)]
    pc = [0]
    X = mybir.AxisListType.X
    R0, K0, V0 = 0, AW, 2 * AW
    WL0, AL0, GL0 = 3 * AW, 3 * AW + LW, 3 * AW + LW + LA

    def store(name, tile_ap, buf, rows):
        kb.dma("pool", lambda e, d=outs[name][rows, :]: e.dma_start(out=d, in_=tile_ap), reads=[buf], via=buf)

    def lora_mm(dst, b_dst, slots, kparts, rhs_fn, bias_idx):
        for nb in range(AW // 512):
            i = pc[0] % 2
            pc[0] += 1
            for si, slot in enumerate(slots):
                kb.op("pe", lambda e, i=i, nb=nb, si=si, slot=slot: e.matmul(
                    pm[i][:, :], lT[0:kparts, slot, :], rhs_fn(si)[0:kparts, nb * 512:(nb + 1) * 512],
                    start=(si == 0), stop=(si == len(slots) - 1)), reads=[b_lT, b_c], writes=[b_pm[i]])
            if bias_idx is None:
                kb.op("act", lambda e, i=i, nb=nb: e.activation(dst[:, nb * 512:(nb + 1) * 512], pm[i][:, :], AF.Copy),
                      reads=[b_pm[i]], writes=[b_dst])
            else:
                kb.op("dve", lambda e, i=i, nb=nb: e.tensor_tensor(
                    dst[:, nb * 512:(nb + 1) * 512], pm[i][:, :], cs[:, bias_idx, nb * 512:(nb + 1) * 512], ALU.add),
                    reads=[b_pm[i], b_c], writes=[b_dst])

    for j in range(NT128):
        rows = slice(j * 128, (j + 1) * 128)
        kb.dma("sp", lambda e, s=za[rows, :]: e.dma_start(out=zt[:, :], in_=s), writes=[b_zt], via=b_zt)
        kb.dma("act", lambda e, s=zp[rows, :]: e.dma_start(out=pt[:, :], in_=s), writes=[b_pt], via=b_pt)
        kb.op("dve", lambda e: e.tensor_tensor(pt[:, :], pt[:, :], zt[:, :], ALU.subtract), reads=[b_zt, b_pt], writes=[b_pt])
        kb.op("pool", lambda e: e.tensor_tensor(pt[:, :], pt[:, :], mu[:, :], ALU.mult), reads=[b_pt, b_c], writes=[b_pt])
        kb.op("dve", lambda e: e.tensor_tensor(zt[:, :], zt[:, :], pt[:, :], ALU.add), reads=[b_zt, b_pt], writes=[b_zt])
        store("o_r", zt[:, R0:R0 + AW], b_zt, rows)
        store("o_v", zt[:, V0:V0 + AW], b_zt, rows)
        kb.op("act", lambda e: e.activation(sm[:, 0:LW], zt[:, WL0:WL0 + LW], AF.Tanh), reads=[b_zt], writes=[b_sm])
        kb.op("act", lambda e: e.activation(sm[:, 256:512], zt[:, GL0:GL0 + LG], AF.Sigmoid), reads=[b_zt], writes=[b_sm])
        kb.op("dve", lambda e: e.tensor_copy(sm[:, 128:128 + LA], zt[:, AL0:AL0 + LA]), reads=[b_zt], writes=[b_sm])
        for slot, (c0, wdt) in enumerate(((0, LW), (128, LA), (256, 128), (384, 128))):
            kb.op("pe", lambda e, slot=slot, c0=c0, wdt=wdt: e.transpose(
                ptr[0:wdt, slot * 128:(slot + 1) * 128], sm[:, c0:c0 + wdt], ident[:, :]),
                reads=[b_sm, b_c], writes=[b_ptr])
            kb.op("dve", lambda e, slot=slot, wdt=wdt: e.tensor_copy(lT[0:wdt, slot, :], ptr[0:wdt, slot * 128:(slot + 1) * 128]),
                  reads=[b_ptr], writes=[b_lT])
        lora_mm(W[0], b_W[0], [0], LW, lambda si: w2, 0)
        kb.op("act", lambda e: e.activation(W[0][:, :], W[0][:, :], AF.Exp, scale=-1.0), reads=[b_W[0]], writes=[b_W[0]])
        kb.op("act", lambda e: e.activation(W[0][:, :], W[0][:, :], AF.Ln, bias=1.0), reads=[b_W[0]], writes=[b_W[0]])
        kb.op("act", lambda e: e.activation(W[0][:, :], W[0][:, :], AF.Exp, scale=-1.0, bias=-0.5),
              reads=[b_W[0]], writes=[b_W[0]])
        kb.op("act", lambda e: e.activation(W[0][:, :], W[0][:, :], AF.Exp, scale=-1.0), reads=[b_W[0]], writes=[b_W[0]])
        store("o_dec", W[0][:, :], b_W[0], rows)
        lora_mm(W[1], b_W[1], [1], LA, lambda si: a2, 1)
        kb.op("act", lambda e: e.activation(W[1][:, :], W[1][:, :], AF.Sigmoid), reads=[b_W[1]], writes=[b_W[1]])
        lora_mm(W[2], b_W[2], [2, 3], 128, lambda si: g2[:, si, :], None)
        store("o_g", W[2][:, :], b_W[2], rows)
        kb.op("dve", lambda e: e.tensor_tensor(W[3][:, :], zt[:, K0:K0 + AW], cs[:, 2, :], ALU.mult),
              reads=[b_zt, b_c], writes=[b_W[3]])
        kb.op("pool", lambda e: e.tensor_tensor(W[4][:, :], W[3][:, :], W[3][:, :], ALU.mult), reads=[b_W[3]], writes=[b_W[4]])
        kb.op("dve", lambda e: e.reduce_sum(hs[:, 0:32], W[4][:, :].rearrange("p (h k) -> p h k", k=64), X),
              reads=[b_W[4]], writes=[b_hs])
        kb.op("dve", lambda e: e.tensor_scalar(hs[:, 0:32], hs[:, 0:32], 1e-12, None, ALU.add), reads=[b_hs], writes=[b_hs])
        kb.op("act", lambda e: e.activation(hs[:, 0:32], hs[:, 0:32], AF.Sqrt), reads=[b_hs], writes=[b_hs])
        kb.op("dve", lambda e: e.reciprocal(hs[:, 0:32], hs[:, 0:32]), reads=[b_hs], writes=[b_hs])
        kb.op("dve", lambda e: e.tensor_tensor(
            W[3][:, :].rearrange("p (h k) -> p h k", k=64), W[3][:, :].rearrange("p (h k) -> p h k", k=64),
            hs[:, 0:32].unsqueeze(2).to_broadcast([128, 32, 64]), ALU.mult), reads=[b_W[3], b_hs], writes=[b_W[3]])
        store("o_kap", W[3][:, :], b_W[3], rows)
        kb.op("pool", lambda e: e.tensor_tensor(W[4][:, :], W[3][:, :], W[1][:, :], ALU.mult),
              reads=[b_W[3], b_W[1]], writes=[b_W[4]])
        store("o_b", W[4][:, :], b_W[4], rows)
        kb.op("dve", lambda e: e.scalar_tensor_tensor(W[1][:, :], W[1][:, :], 1.0, cs[:, 3, :], ALU.subtract, ALU.mult),
              reads=[b_W[1], b_c], writes=[b_W[1]])
        kb.op("dve", lambda e: e.tensor_scalar(W[1][:, :], W[1][:, :], 1.0, None, ALU.add), reads=[b_W[1]], writes=[b_W[1]])
        kb.op("pool", lambda e: e.tensor_tensor(W[1][:, :], W[1][:, :], zt[:, K0:K0 + AW], ALU.mult),
              reads=[b_W[1], b_zt], writes=[b_W[1]])
        store("o_kp", W[1][:, :], b_W[1], rows)
    return kb.emit()


def run_rwkv_prep(za, mu_shift, w0, w2, a0, a2, g2, k_k, k_a):
    L = za.shape[0]
    NT128 = -(-(-(-L // 128)) // NCORES)
    tot = NT128 * NCORES * 128
    zap = np.zeros((tot, za.shape[1]), np.float32)
    zap[:L] = za
    zpp = np.zeros_like(zap)
    zpp[1:L] = za[:L - 1]
    bc = lambda v: np.ascontiguousarray(np.broadcast_to(v[None, :], (128, v.shape[0]))).astype(np.float32)
    common = {"mu_b": bc(mu_shift), "cst": np.stack([bc(w0), bc(a0), bc(k_k), bc(k_a)]),
              "w2i": np.ascontiguousarray(w2), "a2i": np.ascontiguousarray(a2), "g2i": np.ascontiguousarray(g2),
              "ident_in": np.eye(128, dtype=np.float32)}
    nc = build_rwkv_prep(NT128)
    maps = []
    for c in range(NCORES):
        r = slice(c * NT128 * 128, (c + 1) * NT128 * 128)
        maps.append(dict(common, za=zap[r], zp=zpp[r]))
    res = run_bass_kernel_spmd(nc, maps, core_ids=list(range(NCORES)))
    return {n: np.concatenate([res.results[c][n] for c in range(NCORES)], axis=0)[:L]
            for n in ("o_r", "o_kp", "o_v", "o_kap", "o_b", "o_dec", "o_g")}


SBLK = 32


def build_rwkv_scan(Tp):
    kb = KB()
    NB = Tp // SBLK
    L1 = kb.dram("L1", [128, Tp * 8], F32, kind="ExternalInput")
    LB = kb.dram("LB", [4, Tp * 128], F32, kind="ExternalInput")
    LK = kb.dram("LK", [4, Tp * 128], F32, kind="ExternalInput")
    VM = kb.dram("VM", [4, Tp * 128], F32, kind="ExternalInput")
    DC = kb.dram("DC", [128, Tp * 2], F32, kind="ExternalInput")
    M8 = kb.dram("M8", [8, 128], F32, kind="ExternalInput")
    yo = kb.dram("yo", [4, Tp * 128], F32, kind="ExternalOutput")
    ST = kb.sb("ST", [128, 128], F32)
    b_STh = [Buf("ST0"), Buf("ST1")]
    m8 = kb.sb("m8", [8, 128], F32)
    b_m8 = Buf("m8")
    kb.dma("sp", lambda e: e.dma_start(out=m8[:, :], in_=M8[:, :]), writes=[b_m8], via=b_m8)
    kb.op("pool", lambda e: e.memset(ST[:, :], 0.0), writes=b_STh)
    l1 = [kb.sb(f"l1_{i}", [128, SBLK, 8], F32) for i in range(2)]
    lb = [kb.sb(f"lb_{i}", [4, SBLK, 128], F32) for i in range(2)]
    lk = [kb.sb(f"lk_{i}", [4, SBLK, 128], F32) for i in range(2)]
    vm = [kb.sb(f"vm_{i}", [4, SBLK, 128], F32) for i in range(2)]
    dc = [kb.sb(f"dc_{i}", [128, SBLK, 2], F32) for i in range(2)]
    r2 = [kb.sb(f"r2_{i}", [8, SBLK, 128], F32) for i in range(2)]
    b_in = [Buf(f"in{i}") for i in range(2)]
    b_r2 = [Buf(f"r2_{i}") for i in range(2)]
    p1 = kb.ps("p1", [8, 128], F32)
    pU = kb.ps("pU", [128, 128], F32)
    b_p1, b_pU = Buf("p1"), Buf("pU")
    for blk in range(NB):
        i = blk % 2
        t0 = blk * SBLK
        for (dst, srcd, w, q) in ((l1[i], L1, 8, "sp"), (lb[i], LB, 128, "act"), (lk[i], LK, 128, "sp"),
                                  (vm[i], VM, 128, "act"), (dc[i], DC, 2, "sp")):
            kb.dma(q, lambda e, dst=dst, w=w, s=srcd[:, t0 * w:(t0 + SBLK) * w]: e.dma_start(
                out=dst[:, :, :].rearrange("p a b -> p (a b)"), in_=s), writes=[b_in[i]], via=b_in[i])
        for s in range(SBLK):
            kb.op("pe", lambda e, i=i, s=s: e.matmul(p1[:, :], l1[i][:, s, :], ST[:, :], start=True, stop=True),
                  reads=[b_in[i], b_STh[0], b_STh[1]], writes=[b_p1])
            kb.op("dve", lambda e, i=i, s=s: e.tensor_tensor(r2[i][:, s, :], p1[:, :], m8[:, :], ALU.mult),
                  reads=[b_p1, b_m8], writes=[b_r2[i]])
            kb.op("pe", lambda e, i=i, s=s: e.matmul(pU[:, :], lk[i][:, s, :], vm[i][:, s, :], start=True, stop=False),
                  reads=[b_in[i]], writes=[b_pU])
            kb.op("pe", lambda e, i=i, s=s: e.matmul(pU[:, :], lb[i][:, s, :], r2[i][0:4, s, :], start=False, stop=True),
                  reads=[b_in[i], b_r2[i]], writes=[b_pU])
            for g in range(2):
                kb.op("dve", lambda e, i=i, s=s, g=g: e.scalar_tensor_tensor(
                    ST[:, g * 64:(g + 1) * 64], ST[:, g * 64:(g + 1) * 64], dc[i][:, s, g:g + 1],
                    pU[:, g * 64:(g + 1) * 64], ALU.mult, ALU.add),
                    reads=[b_STh[g], b_in[i], b_pU], writes=[b_STh[g]])
        kb.dma("pool", lambda e, i=i, d=yo[:, t0 * 128:(t0 + SBLK) * 128]: e.dma_start(
            out=d, in_=r2[i][4:8, :, :].rearrange("p a b -> p (a b)")), reads=[b_r2[i]], via=b_r2[i])
    return kb.emit()


def run_rwkv_scan(P):
    L = P["o_r"].shape[0]
    Tp = -(-(L + 1) // SBLK) * SBLK
    H = lambda a: a.reshape(L, 32, 64)
    r, kp, v, kap, b, dec = (H(P[n]) for n in ("o_r", "o_kp", "o_v", "o_kap", "o_b", "o_dec"))
    m8 = np.zeros((8, 128), np.float32)
    for j in range(4):
        g = j // 2
        m8[j, g * 64:(g + 1) * 64] = -1.0
        m8[4 + j, g * 64:(g + 1) * 64] = 1.0
    nc = build_rwkv_scan(Tp)
    maps = []
    for c in range(NCORES):
        L1 = np.zeros((128, Tp, 8), np.float32)
        LB = np.zeros((4, Tp, 128), np.float32)
        LK = np.zeros((4, Tp, 128), np.float32)
        VM = np.zeros((4, Tp, 128), np.float32)
        DC = np.ones((128, Tp, 2), np.float32)
        for j in range(4):
            g, h2 = j // 2, j % 2
            hd = 4 * c + j
            ps = slice(h2 * 64, (h2 + 1) * 64)
            L1[ps, 0:L, j] = kap[:, hd, :].T
            L1[ps, 1:L + 1, 4 + j] = r[:, hd, :].T
            LB[j, 0:L, ps] = b[:, hd, :]
            LK[j, 0:L, ps] = kp[:, hd, :]
            VM[j, 0:L, g * 64:(g + 1) * 64] = v[:, hd, :]
            DC[ps, 0:L, g] = dec[:, hd, :].T
        maps.append({"L1": L1.reshape(128, -1), "LB": LB.reshape(4, -1), "LK": LK.reshape(4, -1),
                     "VM": VM.reshape(4, -1), "DC": DC.reshape(128, -1), "M8": m8})
    res = run_bass_kernel_spmd(nc, maps, core_ids=list(range(NCORES)))
    y = np.zeros((L, 32, 64), np.float32)
    for c in range(NCORES):
        yo = res.results[c]["yo"].reshape(4, Tp, 128)
        for j in range(4):
            g = j // 2
            y[:, 4 * c + j, :] = yo[j, 1:L + 1, g * 64:(g + 1) * 64]
    return y.reshape(L, 2048)


A_GN_EPS = 64e-5


def build_rwkv_post(NT128):
    kb = KB()
    names = ("i_y", "i_r", "i_kp", "i_v", "i_g")
    ins = {n: kb.dram(n, [NT128 * 128, AW], F32, kind="ExternalInput") for n in names}
    cst = kb.dram("cst", [3, 128, AW], F32, kind="ExternalInput")
    ya = kb.dram("ya", [NT128 * 128, AW], F32, kind="ExternalOutput")
    cs = kb.sb("cs", [128, 3, AW], F32)
    b_c = Buf("c")
    for i in range(3):
        kb.dma("sp", lambda e, i=i: e.dma_start(out=cs[:, i, :], in_=cst[i, :, :]), writes=[b_c], via=b_c)
    T = {n: kb.sb("t_" + n, [128, AW], F32) for n in names}
    b_T = {n: Buf("t_" + n) for n in names}
    tmp = kb.sb("tmp", [128, AW], F32)
    b_tmp = Buf("tmp")
    hs = kb.sb("hs", [128, 96], F32)
    b_hs = Buf("hs")
    X = mybir.AxisListType.X
    v3 = lambda ap: ap.rearrange("p (h k) -> p h k", k=64)
    bc = lambda ap: ap.unsqueeze(2).to_broadcast([128, 32, 64])
    for j in range(NT128):
        rows = slice(j * 128, (j + 1) * 128)
        for qi, n in enumerate(names):
            kb.dma("sp" if qi % 2 == 0 else "act", lambda e, n=n, s=ins[n][rows, :]: e.dma_start(out=T[n][:, :], in_=s),
                   writes=[b_T[n]], via=b_T[n])
        y, r, kp, v, g = (T[n] for n in names)
        by, br, bkp, bv, bg = (b_T[n] for n in names)
        kb.op("dve", lambda e: e.reduce_sum(hs[:, 0:32], v3(y[:, :]), X), reads=[by], writes=[b_hs])
        kb.op("dve", lambda e: e.tensor_scalar(hs[:, 0:32], hs[:, 0:32], 1.0 / 64, None, ALU.mult), reads=[b_hs], writes=[b_hs])
        kb.op("dve", lambda e: e.tensor_tensor(v3(y[:, :]), v3(y[:, :]), bc(hs[:, 0:32]), ALU.subtract),
              reads=[by, b_hs], writes=[by])
        kb.op("pool", lambda e: e.tensor_tensor(tmp[:, :], y[:, :], y[:, :], ALU.mult), reads=[by], writes=[b_tmp])
        kb.op("dve", lambda e: e.reduce_sum(hs[:, 32:64], v3(tmp[:, :]), X), reads=[b_tmp], writes=[b_hs])
        kb.op("dve", lambda e: e.tensor_scalar(hs[:, 32:64], hs[:, 32:64], 1.0 / 64, A_GN_EPS, ALU.mult, ALU.add),
              reads=[b_hs], writes=[b_hs])
        kb.op("act", lambda e: e.activation(hs[:, 32:64], hs[:, 32:64], AF.Sqrt), reads=[b_hs], writes=[b_hs])
        kb.op("dve", lambda e: e.reciprocal(hs[:, 32:64], hs[:, 32:64]), reads=[b_hs], writes=[b_hs])
        kb.op("dve", lambda e: e.tensor_tensor(v3(y[:, :]), v3(y[:, :]), bc(hs[:, 32:64]), ALU.mult),
              reads=[by, b_hs], writes=[by])
        kb.op("pool", lambda e: e.tensor_tensor(y[:, :], y[:, :], cs[:, 0, :], ALU.mult), reads=[by, b_c], writes=[by])
        kb.op("pool", lambda e: e.tensor_tensor(y[:, :], y[:, :], cs[:, 1, :], ALU.add), reads=[by, b_c], writes=[by])
        kb.op("dve", lambda e: e.tensor_tensor(tmp[:, :], r[:, :], kp[:, :], ALU.mult), reads=[br, bkp, b_tmp], writes=[b_tmp])
        kb.op("dve", lambda e: e.tensor_tensor(tmp[:, :], tmp[:, :], cs[:, 2, :], ALU.mult), reads=[b_tmp, b_c], writes=[b_tmp])
        kb.op("dve", lambda e: e.reduce_sum(hs[:, 64:96], v3(tmp[:, :]), X), reads=[b_tmp], writes=[b_hs])
        kb.op("dve", lambda e: e.tensor_tensor(v3(v[:, :]), v3(v[:, :]), bc(hs[:, 64:96]), ALU.mult),
              reads=[bv, b_hs], writes=[bv])
        kb.op("pool", lambda e: e.tensor_tensor(y[:, :], y[:, :], v[:, :], ALU.add), reads=[by, bv], writes=[by])
        kb.op("pool", lambda e: e.tensor_tensor(y[:, :], y[:, :], g[:, :], ALU.mult), reads=[by, bg], writes=[by])
        kb.dma("pool", lambda e, d=ya[rows, :]: e.dma_start(out=d, in_=y[:, :]), reads=[by], via=by)
    return kb.emit()


def run_rwkv_post(y, P, ln_w, ln_b, r_k):
    L = y.shape[0]
    NT128 = -(-(-(-L // 128)) // NCORES)
    tot = NT128 * NCORES * 128

    def pad(a):
        o = np.zeros((tot, AW), np.float32)
        o[:L] = a
        return o
    arrs = {"i_y": pad(y), "i_r": pad(P["o_r"]), "i_kp": pad(P["o_kp"]), "i_v": pad(P["o_v"]), "i_g": pad(P["o_g"])}
    bc = lambda v: np.ascontiguousarray(np.broadcast_to(v.reshape(1, -1), (128, AW))).astype(np.float32)
    cst = np.stack([bc(ln_w), bc(ln_b), bc(r_k)])
    nc = build_rwkv_post(NT128)
    maps = []
    for c in range(NCORES):
        rr = slice(c * NT128 * 128, (c + 1) * NT128 * 128)
        maps.append(dict({n: a[rr] for n, a in arrs.items()}, cst=cst))
    res = run_bass_kernel_spmd(nc, maps, core_ids=list(range(NCORES)))
    return np.concatenate([res.results[c]["ya"] for c in range(NCORES)], axis=0)[:L]


def build_mixout(D, DA, NT128):
    kb = KB()
    KC, KA = D // 128, DA // 128
    hr = kb.dram("hr", [NT128 * 128, D], F32, kind="ExternalInput")
    yar = kb.dram("yar", [NT128 * 128, DA], F32, kind="ExternalInput")
    ybr = kb.dram("ybr", [NT128 * 128, DA], F32, kind="ExternalInput")
    gr = kb.dram("gr", [NT128 * 128, 2 * D], F32, kind="ExternalInput")
    wpa = kb.dram("wpa", [DA, D], F32, kind="ExternalInput")
    wpb = kb.dram("wpb", [DA, D], F32, kind="ExternalInput")
    wou = kb.dram("wou", [D, D], F32, kind="ExternalInput")
    ident_in = kb.dram("ident_in", [128, 128], F32, kind="ExternalInput")
    h1 = kb.dram("h1", [NT128 * 128, D], F32, kind="ExternalOutput")
    S = Shared(kb, D, ident_in, KC)
    sa, b_sa = emit_cast_tiled(S, wpa, DA, D, "s_wpa")
    sb_, b_sb_ = emit_cast_tiled(S, wpb, DA, D, "s_wpb")
    so, b_so = emit_cast_tiled(S, wou, D, D, "s_wou")
    h_t = kb.sb("h_t", [128, D], F32)
    b_h = Buf("h_t")
    g_t = kb.sb("g_t", [128, 2 * D], F32)
    b_g = Buf("g_t")
    m_t = kb.sb("m_t", [128, D], F32)
    b_m = Buf("m_t")
    ys = kb.sb("ys", [128, DA], F32)
    b_ys = Buf("ys")
    tmp = [kb.sb(f"tmp{i}", [128, 512], F32) for i in range(2)]
    b_tmp = [Buf(f"tmp{i}") for i in range(2)]
    yaT = kb.sb("yaT", [128, KA, 128], BF16)
    ybT = kb.sb("ybT", [128, KA, 128], BF16)
    mT = kb.sb("mT", [128, KC, 128], BF16)
    b_yaT, b_ybT, b_mT = Buf("yaT"), Buf("ybT"), Buf("mT")
    tc_ = [0]
    for j in range(NT128):
        rows = slice(j * 128, (j + 1) * 128)
        kb.dma("pool", lambda e, s=hr[rows, :]: e.dma_start(out=h_t[:, :], in_=s), writes=[b_h], via=b_h)
        kb.dma("pool", lambda e, s=gr[rows, :]: e.dma_start(out=g_t[:, :], in_=s), writes=[b_g], via=b_g)
        for (srcy, dT, b_dT) in ((yar, yaT, b_yaT), (ybr, ybT, b_ybT)):
            kb.dma("pool", lambda e, s=srcy[rows, :]: e.dma_start(out=ys[:, :], in_=s), writes=[b_ys], via=b_ys)
            kb.op("dve", lambda e: e.tensor_copy(S.ub[:, :DA], ys[:, :]), reads=[b_ys], writes=[S.b_ub])
            emit_transposeT(S, DA, dT, b_dT, 0)

        def evac_a(t, nb, ps, b_ps):
            kb.op("dve", lambda e: e.tensor_tensor(m_t[:, nb * 512:(nb + 1) * 512], ps, g_t[:, nb * 512:(nb + 1) * 512],
                                                   ALU.mult), reads=[b_ps, b_g], writes=[b_m])

        def evac_b(t, nb, ps, b_ps):
            i = tc_[0] % 2
            tc_[0] += 1
            kb.op("dve", lambda e: e.tensor_tensor(tmp[i][:, :], ps, g_t[:, D + nb * 512:D + (nb + 1) * 512], ALU.mult),
                  reads=[b_ps, b_g], writes=[b_tmp[i]])
            kb.op("pool", lambda e: e.tensor_tensor(m_t[:, nb * 512:(nb + 1) * 512], m_t[:, nb * 512:(nb + 1) * 512],
                                                    tmp[i][:, :], ALU.add), reads=[b_tmp[i], b_m], writes=[b_m])

        def evac_o(t, nb, ps, b_ps):
            kb.op("dve", lambda e: e.tensor_tensor(h_t[:, nb * 512:(nb + 1) * 512], ps, h_t[:, nb * 512:(nb + 1) * 512],
                                                   ALU.add), reads=[b_ps, b_h], writes=[b_h])
        emit_gemm_tok(S, yaT, b_yaT, KA, sa, b_sa, D // 512, 1, evac_a)
        emit_gemm_tok(S, ybT, b_ybT, KA, sb_, b_sb_, D // 512, 1, evac_b)
        kb.op("dve", lambda e: e.tensor_copy(S.ub[:, :], m_t[:, :]), reads=[b_m], writes=[S.b_ub])
        emit_transposeT(S, D, mT, b_mT, 0)
        emit_gemm_tok(S, mT, b_mT, KC, so, b_so, D // 512, 1, evac_o)
        kb.dma("pool", lambda e, d=h1[rows, :]: e.dma_start(out=d, in_=h_t[:, :]), reads=[b_h], via=b_h)
    return kb.emit()


def run_mixout(hfull, ya, yb, gates, w_proj_a, w_proj_b, w_out):
    L, D = hfull.shape
    DA = ya.shape[1]
    NT128 = -(-(-(-L // 128)) // NCORES)
    tot = NT128 * NCORES * 128

    def pad(a):
        o = np.zeros((tot, a.shape[1]), np.float32)
        o[:L] = a
        return o
    hp, yap, ybp, gp = pad(hfull), pad(ya), pad(yb), pad(gates)
    nc = build_mixout(D, DA, NT128)
    ident = np.eye(128, dtype=np.float32)
    maps = []
    for c in range(NCORES):
        r = slice(c * NT128 * 128, (c + 1) * NT128 * 128)
        maps.append({"hr": hp[r], "yar": yap[r], "ybr": ybp[r], "gr": gp[r],
                     "wpa": np.ascontiguousarray(w_proj_a), "wpb": np.ascontiguousarray(w_proj_b),
                     "wou": np.ascontiguousarray(w_out), "ident_in": ident})
    res = run_bass_kernel_spmd(nc, maps, core_ids=list(range(NCORES)))
    return np.concatenate([res.results[c]["h1"] for c in range(NCORES)], axis=0)[:L]


NEG_ADM = -1.0e30
NEG_SEL = -2.0e30
IDX_C0 = (32 ** -0.5) * (64 ** -0.5)
TOPK = 256


def build_dsa(Lp, NQ, klim=None):
    kb = KB()
    NKT = Lp // 128
    NBLK = Lp // 64
    ckv = kb.dram("ckv", [Lp, 512], F32, kind="ExternalInput")
    kidx = kb.dram("kidx", [Lp, 64], F32, kind="ExternalInput")
    qr = kb.dram("qr", [NQ * 128, 2048], F32, kind="ExternalInput")
    qir = kb.dram("qir", [NQ * 128, 2048], F32, kind="ExternalInput")
    wir = kb.dram("wir", [NQ * 128, 32], F32, kind="ExternalInput")
    qch = kb.dram("qch", [NQ * 128, 1], F32, kind="ExternalInput")
    wukT = kb.dram("wukT", [512, 2048], F32, kind="ExternalInput")
    wuvR = kb.dram("wuvR", [512, 2048], F32, kind="ExternalInput")
    kvw_b = kb.dram("kvw_b", [128, 512], F32, kind="ExternalInput")
    lnw_b = kb.dram("lnw_b", [128, 64], F32, kind="ExternalInput")
    lnb_b = kb.dram("lnb_b", [128, 64], F32, kind="ExternalInput")
    blk_in = kb.dram("blk_in", [128, NBLK], F32, kind="ExternalInput")
    ident_in = kb.dram("ident_in", [128, 128], F32, kind="ExternalInput")
    yb = kb.dram("yb", [NQ * 128, 2048], F32, kind="ExternalOutput")
    Kscr = kb.dram("Kscr", [4 * NKT * 128, 512], BF16)
    Vscr = kb.dram("Vscr", [4 * NKT * 128, 512], BF16)
    kIscr = kb.dram("kIscr", [128, Lp], BF16)
    b_Kscr, b_Vscr, b_kIscr = Buf("Kscr"), Buf("Vscr"), Buf("kIscr")

    ident_f = kb.sb("ident_f", [128, 128], F32)
    ident4 = kb.sb("ident4", [128, 512], BF16)
    ones_bf = kb.sb("ones_bf", [128, 1], BF16)
    kvw = kb.sb("kvw", [128, 512], F32)
    lnw = kb.sb("lnw", [128, 64], F32)
    lnb = kb.sb("lnb", [128, 64], F32)
    blk = kb.sb("blk", [128, NBLK], F32)
    b_c = Buf("consts")
    for dst, s in ((ident_f, ident_in), (kvw, kvw_b), (lnw, lnw_b), (lnb, lnb_b), (blk, blk_in)):
        kb.dma("sp", lambda e, dst=dst, s=s: e.dma_start(out=dst[:, :], in_=s[:, :]), writes=[b_c], via=b_c)
    b_id = Buf("ident4")
    for a in range(4):
        kb.op("dve", lambda e, a=a: e.tensor_copy(ident4[:, a * 128:(a + 1) * 128], ident_f[:, :]),
              reads=[b_c], writes=[b_id])
    kb.op("pool", lambda e: e.memset(ones_bf[:, :], 1.0), writes=[b_id])
    ident = ident4[:, 0:128]

    big8 = kb.sb("big8", [128, 2048], F32)
    b_big8 = Buf("big8")
    wuk_sb = kb.sb("wuk_sb", [128, 4, 2048], BF16)
    wuv_sb = kb.sb("wuv_sb", [128, 4, 2048], BF16)
    b_wuk, b_wuv = Buf("wuk"), Buf("wuv")
    for (srcw, dstw, bw) in ((wukT, wuk_sb, b_wuk), (wuvR, wuv_sb, b_wuv)):
        for rc in range(4):
            kb.dma("sp", lambda e, s=srcw[rc * 128:(rc + 1) * 128, :]: e.dma_start(out=big8[:, :], in_=s),
                   writes=[b_big8], via=b_big8)
            kb.op("dve", lambda e, dstw=dstw, rc=rc: e.tensor_copy(dstw[:, rc, :], big8[:, :]),
                  reads=[b_big8], writes=[bw])

    pT_ = kb.ps("pT", [128, 1024], BF16)
    b_pT = Buf("pT")
    pI = [kb.ps(f"pI{i}", [128, 1024], F32) for i in range(2)]
    b_pI = [[Buf(f"pI{i}_{h}") for h in range(2)] for i in range(2)]
    pL = [kb.ps(f"pL{i}", [128, 512], F32) for i in range(2)]
    b_pL = [Buf(f"pL{i}") for i in range(2)]
    pR = kb.ps("pR", [128, 512], F32)
    b_pR = Buf("pR")

    ss = kb.sb("ss", [128, 8], F32)
    b_ss = Buf("ss")
    sc = kb.sb("sc", [128, 512], F32)
    b_sc = Buf("sc")

    def rstd_of(src_ap, b_src, width, col, eps):
        kb.op("dve", lambda e: e.tensor_tensor(sc[:, :width], src_ap, src_ap, ALU.mult), reads=[b_src], writes=[b_sc])
        kb.op("dve", lambda e: e.reduce_sum(ss[:, col:col + 1], sc[:, :width], mybir.AxisListType.X),
              reads=[b_sc], writes=[b_ss])
        kb.op("dve", lambda e: e.tensor_scalar(ss[:, col:col + 1], ss[:, col:col + 1], 1.0 / width, eps,
                                               ALU.mult, ALU.add), reads=[b_ss], writes=[b_ss])
        kb.op("act", lambda e: e.activation(ss[:, col:col + 1], ss[:, col:col + 1], AF.Sqrt),
              reads=[b_ss], writes=[b_ss])
        kb.op("dve", lambda e: e.reciprocal(ss[:, col:col + 1], ss[:, col:col + 1]), reads=[b_ss], writes=[b_ss])

    ckt = [kb.sb(f"ckt{i}", [128, 512], F32) for i in range(2)]
    kxt = [kb.sb(f"kxt{i}", [128, 64], F32) for i in range(2)]
    b_ckt = [Buf(f"ckt{i}") for i in range(2)]
    b_kxt = [Buf(f"kxt{i}") for i in range(2)]
    cb = kb.sb("cb", [128, 512], BF16)
    b_cb = Buf("cb")
    cT = [kb.sb(f"cT{i}", [128, 4, 128], BF16) for i in range(2)]
    b_cT = [Buf(f"cT{i}") for i in range(2)]
    kst = [kb.sb(f"kst{i}", [128, 512], BF16) for i in range(3)]
    b_kst = [Buf(f"kst{i}") for i in range(3)]
    xc = kb.sb("xc", [128, 64], F32)
    b_xc = Buf("xc")
    kb2 = kb.sb("kb2", [128, 128], BF16)
    b_kb2 = Buf("kb2")
    kIt = [kb.sb(f"kIt{i}", [128, 128], BF16) for i in range(2)]
    b_kIt = [Buf(f"kIt{i}") for i in range(2)]
    kc = [0]
    lc = [0]
    for st in range(NKT):
        i = st % 2
        kb.dma("sp", lambda e, i=i, s=ckv[st * 128:(st + 1) * 128, :]: e.dma_start(out=ckt[i][:, :], in_=s),
               writes=[b_ckt[i]], via=b_ckt[i])
        kb.dma("act", lambda e, i=i, s=kidx[st * 128:(st + 1) * 128, :]: e.dma_start(out=kxt[i][:, :], in_=s),
               writes=[b_kxt[i]], via=b_kxt[i])
        rstd_of(ckt[i][:, :], b_ckt[i], 512, 0, EPS)
        kb.op("dve", lambda e, i=i: e.scalar_tensor_tensor(cb[:, :], ckt[i][:, :], ss[:, 0:1], kvw[:, :],
                                                            ALU.mult, ALU.mult),
              reads=[b_ckt[i], b_ss, b_c], writes=[b_cb])
        for rc in range(4):
            kb.op("pe", lambda e, rc=rc: e.transpose(pT_[:, rc * 128:(rc + 1) * 128], cb[:, rc * 128:(rc + 1) * 128],
                                                     ident), reads=[b_cb, b_id], writes=[b_pT])
        kb.op("act", lambda e, i=i: e.activation(cT[i][:, :, :].rearrange("p a b -> p (a b)"), pT_[:, 0:512], AF.Copy),
              reads=[b_pT], writes=[b_cT[i]])
        for hg in range(4):
            li = lc[0] % 2
            lc[0] += 1
            for h4 in range(4):
                h = hg * 4 + h4
                for rc in range(4):
                    kb.op("pe", lambda e, li=li, h4=h4, h=h, rc=rc, i=i: e.matmul(
                        pL[li][:, h4 * 128:(h4 + 1) * 128], wuk_sb[:, rc, h * 128:(h + 1) * 128], cT[i][:, rc, :],
                        start=(rc == 0), stop=(rc == 3)), reads=[b_wuk, b_cT[i]], writes=[b_pL[li]])
            ki = kc[0] % 3
            kc[0] += 1
            kb.op("act", lambda e, ki=ki, li=li: e.activation(kst[ki][:, :], pL[li][:, :], AF.Copy),
                  reads=[b_pL[li]], writes=[b_kst[ki]])
            r0 = (hg * NKT + st) * 128
            kb.dma("sp", lambda e, ki=ki, d=Kscr[r0:r0 + 128, :]: e.dma_start(out=d, in_=kst[ki][:, :]),
                   reads=[b_kst[ki]], writes=[b_Kscr], via=b_kst[ki])
        for hg in range(4):
            li = lc[0] % 2
            lc[0] += 1
            for rc in range(4):
                kb.op("pe", lambda e, li=li, hg=hg, rc=rc, i=i: e.matmul(
                    pL[li][:, :], cT[i][:, rc, :], wuv_sb[:, rc, hg * 512:(hg + 1) * 512],
                    start=(rc == 0), stop=(rc == 3)), reads=[b_wuv, b_cT[i]], writes=[b_pL[li]])
            ki = kc[0] % 3
            kc[0] += 1
            kb.op("dve", lambda e, ki=ki, li=li: e.tensor_copy(kst[ki][:, :], pL[li][:, :]),
                  reads=[b_pL[li]], writes=[b_kst[ki]])
            r0 = (hg * NKT + st) * 128
            kb.dma("act", lambda e, ki=ki, d=Vscr[r0:r0 + 128, :]: e.dma_start(out=d, in_=kst[ki][:, :]),
                   reads=[b_kst[ki]], writes=[b_Vscr], via=b_kst[ki])
        kb.op("dve", lambda e, i=i: e.reduce_sum(ss[:, 1:2], kxt[i][:, :], mybir.AxisListType.X),
              reads=[b_kxt[i]], writes=[b_ss])
        kb.op("dve", lambda e: e.tensor_scalar(ss[:, 1:2], ss[:, 1:2], 1.0 / 64, None, ALU.mult),
              reads=[b_ss], writes=[b_ss])
        kb.op("dve", lambda e, i=i: e.tensor_scalar(xc[:, :], kxt[i][:, :], ss[:, 1:2], None, ALU.subtract),
              reads=[b_kxt[i], b_ss], writes=[b_xc])
        rstd_of(xc[:, :], b_xc, 64, 2, 1e-6)
        kb.op("dve", lambda e: e.scalar_tensor_tensor(xc[:, :], xc[:, :], ss[:, 2:3], lnw[:, :], ALU.mult, ALU.mult),
              reads=[b_xc, b_ss, b_c], writes=[b_xc])
        for a in range(2):
            kb.op("dve", lambda e, a=a: e.tensor_tensor(kb2[:, a * 64:(a + 1) * 64], xc[:, :], lnb[:, :], ALU.add),
                  reads=[b_xc, b_c], writes=[b_kb2])
        kb.op("pe", lambda e: e.transpose(pT_[:, 512:640], kb2[:, :], ident), reads=[b_kb2, b_id], writes=[b_pT])
        kb.op("act", lambda e, i=i: e.activation(kIt[i][:, :], pT_[:, 512:640], AF.Copy),
              reads=[b_pT], writes=[b_kIt[i]])
        kb.dma("sp", lambda e, i=i, d=kIscr[:, st * 128:(st + 1) * 128]: e.dma_start(out=d, in_=kIt[i][:, :]),
               reads=[b_kIt[i]], writes=[b_kIscr], via=b_kIt[i])

    score = kb.sb("score", [128, Lp], F32)
    b_score = Buf("score")
    mb = kb.sb("mb", [128, Lp], BF16)
    b_mb = Buf("mb")
    qb16 = kb.sb("qb16", [128, 2048], BF16)
    b_qb16 = Buf("qb16")
    qT = kb.sb("qT", [128, 16, 128], BF16)
    qiT = kb.sb("qiT", [128, 16, 128], BF16)
    b_qT, b_qiT = Buf("qT"), Buf("qiT")
    wq = kb.sb("wq", [128, 32], F32)
    qc = kb.sb("qc", [128, 1], F32)
    b_wq = Buf("wq")
    rl = [kb.sb(f"rl{i}", [128, 1024], F32) for i in range(2)]
    b_rl = [Buf(f"rl{i}") for i in range(2)]
    kIb = [kb.sb(f"kIb{i}", [128, 1024], BF16) for i in range(2)]
    b_kIb = [Buf(f"kIb{i}") for i in range(2)]
    kt = [kb.sb(f"kt{i}", [128, 512], BF16) for i in range(3)]
    vt = [kb.sb(f"vt{i}", [128, 4, 132], BF16) for i in range(3)]
    b_kt = [Buf(f"kt{i}") for i in range(3)]
    b_vt = [Buf(f"vt{i}") for i in range(3)]
    for i in range(3):
        kb.op("pool", lambda e, i=i: e.memset(vt[i][:, :, 128:129], 1.0), writes=[b_vt[i]])
    pTs = [kb.sb(f"pTs{i}", [128, 512], BF16) for i in range(2)]
    b_pTs = [Buf(f"pTs{i}") for i in range(2)]
    mx = kb.sb("mx", [128, 8], F32)
    mx16 = kb.sb("mx16", [128, 16], F32)
    b_mx = Buf("mx")
    rinv = kb.sb("rinv", [128, 4], F32)
    b_rinv = Buf("rinv")
    po = [(pI[0], 0, b_pI[0][0]), (pI[0], 1, b_pI[0][1]), (pI[1], 0, b_pI[1][0]), (pI[1], 1, b_pI[1][1])]
    scale = 128 ** -0.5
    cnt = [0]
    for j in range(NQ):
        rows = slice(j * 128, (j + 1) * 128)
        Lj = Lp if klim is None else klim[j]
        NKTj = Lj // 128
        NBj = Lj // 64
        HA = Lj if Lj <= 16384 else 8192
        blkB = blk[:, :NBj].unsqueeze(2).to_broadcast([128, NBj, 64])
        mb3 = mb[:, :Lj].rearrange("p (a b) -> p a b", b=64)
        kb.dma("pool", lambda e, s=wir[rows, :]: e.dma_start(out=wq[:, :], in_=s), writes=[b_wq], via=b_wq)
        kb.dma("pool", lambda e, s=qch[rows, :]: e.dma_start(out=qc[:, :], in_=s), writes=[b_wq], via=b_wq)
        kb.op("dve", lambda e: e.tensor_scalar(wq[:, :], wq[:, :], IDX_C0, None, ALU.mult), reads=[b_wq], writes=[b_wq])
        for (srcq, dstT, b_dst) in ((qr, qT, b_qT), (qir, qiT, b_qiT)):
            kb.dma("pool", lambda e, s=srcq[rows, :]: e.dma_start(out=big8[:, :], in_=s), writes=[b_big8], via=b_big8)
            kb.op("dve", lambda e: e.tensor_copy(qb16[:, :], big8[:, :]), reads=[b_big8], writes=[b_qb16])
            for b8 in range(2):
                for k in range(8):
                    kb.op("pe", lambda e, b8=b8, k=k: e.transpose(
                        pT_[:, k * 128:(k + 1) * 128], qb16[:, (b8 * 8 + k) * 128:(b8 * 8 + k + 1) * 128], ident),
                        reads=[b_qb16, b_id], writes=[b_pT])
                kb.op("act", lambda e, b8=b8, dstT=dstT: e.activation(
                    dstT[:, b8 * 8:(b8 + 1) * 8, :].rearrange("p a b -> p (a b)"), pT_[:, :], AF.Copy),
                    reads=[b_pT], writes=[b_dst])
        kb.op("pool", lambda e, Lj=Lj, HA=HA, mb3=mb3, blkB=blkB: e.memset(score[:, :Lj], 0.0), writes=[b_score])
        for k0 in range(0, Lj, 1024):
            wd = min(1024, Lj - k0)
            gi_ = (k0 // 1024) % 2
            kb.dma("sp", lambda e, gi_=gi_, wd=wd, s=kIscr[:, k0:k0 + wd]: e.dma_start(out=kIb[gi_][:, :wd], in_=s),
                   reads=[b_kIscr], writes=[b_kIb[gi_]], via=b_kIb[gi_])
            for h in range(32):
                pi = h % 2
                half = (h % 2) * 64
                hp = h // 2
                for c0 in range(0, wd, 512):
                    cw_ = min(512, wd - c0)
                    kb.op("pe", lambda e, pi=pi, half=half, hp=hp, c0=c0, cw_=cw_, gi_=gi_: e.matmul(
                        pI[pi][:, c0:c0 + cw_], qiT[half:half + 64, hp, :], kIb[gi_][half:half + 64, c0:c0 + cw_],
                        start=True, stop=True), reads=[b_qiT, b_kIb[gi_]], writes=[b_pI[pi][c0 // 512]])
                kb.op("act", lambda e, pi=pi, wd=wd: e.activation(rl[pi][:, :wd], pI[pi][:, :wd], AF.Relu),
                      reads=b_pI[pi][:(wd + 511) // 512], writes=[b_rl[pi]])
                kb.op("dve", lambda e, pi=pi, wd=wd, k0=k0, h=h: e.scalar_tensor_tensor(
                    score[:, k0:k0 + wd], rl[pi][:, :wd], wq[:, h:h + 1], score[:, k0:k0 + wd], ALU.mult, ALU.add),
                    reads=[b_rl[pi], b_wq, b_score], writes=[b_score])
        kb.op("dve", lambda e, Lj=Lj, HA=HA, mb3=mb3, blkB=blkB: e.tensor_scalar(mb3, blkB, qc[:, 0:1], None, ALU.is_le), reads=[b_c, b_wq], writes=[b_mb])
        kb.op("pool", lambda e, Lj=Lj, HA=HA, mb3=mb3, blkB=blkB: e.memset(mb[:, 0:48], 0.0), writes=[b_mb])
        kb.op("dve", lambda e, Lj=Lj, HA=HA, mb3=mb3, blkB=blkB: e.tensor_tensor(score[:, :Lj], score[:, :Lj], mb[:, :Lj], ALU.mult),
              reads=[b_score, b_mb], writes=[b_score])
        kb.op("dve", lambda e, Lj=Lj, HA=HA, mb3=mb3, blkB=blkB: e.tensor_scalar(mb[:, :Lj], mb[:, :Lj], -NEG_ADM, NEG_ADM, ALU.mult, ALU.add),
              reads=[b_mb], writes=[b_mb])
        kb.op("dve", lambda e, Lj=Lj, HA=HA, mb3=mb3, blkB=blkB: e.tensor_tensor(score[:, :Lj], score[:, :Lj], mb[:, :Lj], ALU.add),
              reads=[b_score, b_mb], writes=[b_score])
        for r_ in range(TOPK // 8):
            if HA == Lj:
                kb.op("dve", lambda e, Lj=Lj, HA=HA, mb3=mb3, blkB=blkB: e.max(out=mx[:, :], in_=score[:, :Lj]), reads=[b_score], writes=[b_mx])
            else:
                kb.op("dve", lambda e, Lj=Lj, HA=HA, mb3=mb3, blkB=blkB: e.max(out=mx16[:, 0:8], in_=score[:, :HA]), reads=[b_score], writes=[b_mx])
                kb.op("dve", lambda e, Lj=Lj, HA=HA, mb3=mb3, blkB=blkB: e.max(out=mx16[:, 8:16], in_=score[:, HA:Lj]), reads=[b_score], writes=[b_mx])
                kb.op("dve", lambda e, Lj=Lj, HA=HA, mb3=mb3, blkB=blkB: e.max(out=mx[:, :], in_=mx16[:, :]), reads=[b_mx], writes=[b_mx])
            for (a0_, a1_) in ((0, HA), (HA, Lj)):
                if a1_ > a0_:
                    kb.op("dve", lambda e, a0_=a0_, a1_=a1_: e.match_replace(
                        out=score[:, a0_:a1_], in_to_replace=mx[:, :], in_values=score[:, a0_:a1_],
                        imm_value=NEG_SEL), reads=[b_mx, b_score], writes=[b_score])
        kb.op("dve", lambda e, Lj=Lj, HA=HA, mb3=mb3, blkB=blkB: e.tensor_scalar(mb[:, :Lj], score[:, :Lj], 1.5 * NEG_ADM, None, ALU.is_le),
              reads=[b_score], writes=[b_mb])
        kb.op("dve", lambda e, Lj=Lj, HA=HA, mb3=mb3, blkB=blkB: e.scalar_tensor_tensor(mb3, blkB, qc[:, 0:1], mb3, ALU.is_le, ALU.mult),
              reads=[b_c, b_wq, b_mb], writes=[b_mb])
        kb.op("pool", lambda e, Lj=Lj, HA=HA, mb3=mb3, blkB=blkB: e.memset(mb[:, 0:48], 0.0), writes=[b_mb])
        kb.op("dve", lambda e, Lj=Lj, HA=HA, mb3=mb3, blkB=blkB: e.tensor_scalar(mb[:, :Lj], mb[:, :Lj], 1.0, 30000.0, ALU.subtract, ALU.mult),
              reads=[b_mb], writes=[b_mb])
        for hg in range(4):
            for st in range(NKTj):
                ti = cnt[0] % 3
                li = cnt[0] % 2
                cnt[0] += 1
                r0 = (hg * NKT + st) * 128
                kb.dma("sp", lambda e, ti=ti, s=Kscr[r0:r0 + 128, :]: e.dma_start(out=kt[ti][:, :], in_=s),
                       reads=[b_Kscr], writes=[b_kt[ti]], via=b_kt[ti])
                kb.dma("pool", lambda e, ti=ti, s=Vscr[r0:r0 + 128, :]: e.dma_start(
                    out=vt[ti][:, :, 0:128], in_=s.rearrange("p (a b) -> p a b", a=4)),
                    reads=[b_Vscr], writes=[b_vt[ti]], via=b_vt[ti])
                kb.op("pe", lambda e, li=li, st=st: e.matmul(pL[li][:, :], mb[:, st * 128:(st + 1) * 128], ident4[:, :],
                                                             start=True, stop=False, skip_group_check=True),
                      reads=[b_mb, b_id], writes=[b_pL[li]])
                for h4 in range(4):
                    kb.op("pe", lambda e, li=li, h4=h4, ti=ti, hg=hg: e.matmul(
                        pL[li][:, h4 * 128:(h4 + 1) * 128], kt[ti][:, h4 * 128:(h4 + 1) * 128], qT[:, hg * 4 + h4, :],
                        start=False, stop=True, skip_group_check=True),
                        reads=[b_kt[ti], b_qT], writes=[b_pL[li]])
                kb.op("act", lambda e, li=li: e.activation(pTs[li][:, :], pL[li][:, :], AF.Exp, scale=scale),
                      reads=[b_pL[li]], writes=[b_pTs[li]])
                for h4 in range(4):
                    pt, hf, bpo = po[h4]
                    kb.op("pe", lambda e, li=li, h4=h4, ti=ti, pt=pt, hf=hf, st=st: e.matmul(
                        pt[:, hf * 512:hf * 512 + 129], pTs[li][:, h4 * 128:(h4 + 1) * 128],
                        vt[ti][:, h4, 0:129], start=(st == 0), stop=(st == NKTj - 1),
                        skip_group_check=True), reads=[b_pTs[li], b_vt[ti]], writes=[bpo])
            for h4 in range(4):
                pt, hf, bpo = po[h4]
                h = hg * 4 + h4
                kb.op("dve", lambda e, pt=pt, hf=hf, h4=h4: e.reciprocal(
                    rinv[:, h4:h4 + 1], pt[:, hf * 512 + 128:hf * 512 + 129]), reads=[bpo], writes=[b_rinv])
                kb.op("dve", lambda e, pt=pt, hf=hf, h4=h4, h=h: e.tensor_scalar(
                    big8[:, h * 128:(h + 1) * 128], pt[:, hf * 512:hf * 512 + 128], rinv[:, h4:h4 + 1], None, ALU.mult),
                    reads=[bpo, b_rinv], writes=[b_big8])
        kb.dma("pool", lambda e, d=yb[rows, :]: e.dma_start(out=d, in_=big8[:, :]), reads=[b_big8], via=b_big8)
    return kb.emit()


def run_dsa(zb, kv_norm_w, w_uk, w_uv, idx_ln_w, idx_ln_b):
    L = zb.shape[0]
    q, ckv, qi, ki, wi = np.split(zb, np.cumsum([2048, 512, 2048, 64])[:4], axis=1)
    Lp = -(-(48 + L) // 128) * 128
    NQT = -(-L // 128)
    NQ = -(-NQT // NCORES)
    ckv_p = np.zeros((Lp, 512), np.float32)
    ckv_p[48:48 + L] = ckv
    ki_p = np.zeros((Lp, 64), np.float32)
    ki_p[48:48 + L] = ki
    tot = NQ * NCORES * 128
    pos = np.arange(tot)
    chunk = np.where(pos < N_META, 0, 1 + (pos - N_META) // 64).astype(np.float32)

    def pad_rows(a):
        o = np.zeros((tot, a.shape[1]), np.float32)
        o[:L] = a
        return o
    qp, qip, wip = pad_rows(q), pad_rows(qi), pad_rows(wi)
    klim = []
    for j in range(NQ):
        pmax = min((8 * j + 8) * 128 - 1, L - 1)
        cmax = 0 if pmax < N_META else 1 + (pmax - N_META) // 64
        klim.append(min(Lp, -(-((cmax + 1) * 64) // 128) * 128))
    nc = build_dsa(Lp, NQ, klim)
    common = {
        "ckv": ckv_p, "kidx": ki_p,
        "wukT": np.ascontiguousarray(w_uk.transpose(1, 0, 2).reshape(512, 2048)),
        "wuvR": np.ascontiguousarray(w_uv.transpose(1, 0, 2).reshape(512, 2048)),
        "kvw_b": np.ascontiguousarray(np.broadcast_to(kv_norm_w[None, :], (128, 512))),
        "lnw_b": np.ascontiguousarray(np.broadcast_to(idx_ln_w[None, :], (128, 64))),
        "lnb_b": np.ascontiguousarray(np.broadcast_to(idx_ln_b[None, :], (128, 64))),
        "blk_in": np.ascontiguousarray(np.broadcast_to(np.arange(Lp // 64, dtype=np.float32)[None, :], (128, Lp // 64))),
        "ident_in": np.eye(128, dtype=np.float32),
    }
    maps = []
    for c in range(NCORES):
        idx = np.concatenate([np.arange((j * NCORES + c) * 128, (j * NCORES + c + 1) * 128) for j in range(NQ)])
        maps.append(dict(common, qr=qp[idx], qir=qip[idx], wir=wip[idx], qch=chunk[idx, None].copy()))
    res = run_bass_kernel_spmd(nc, maps, core_ids=list(range(NCORES)))
    out = np.zeros((tot, 2048), np.float32)
    for c in range(NCORES):
        for j in range(NQ):
            g = j * NCORES + c
            out[g * 128:(g + 1) * 128] = res.results[c]["yb"][j * 128:(j + 1) * 128]
    return out[:L]


def _host_layout(x, meta_tokens, norm_ffn_w, w_ffn_in, ffn_conv_w, ffn_conv_b, w_ffn_out, norm_final_w, NG, use_cc=True):
    D = x.shape[-1]
    DFF = w_ffn_out.shape[1]
    KC, FC = D // 128, DFF // 128
    hfull = x if meta_tokens is None else np.concatenate([meta_tokens.astype(np.float32), x[0]], axis=0)
    L = hfull.shape[0]
    maps = []
    cwc = np.zeros((128, 2 * FC * 4), np.float32)
    cw3 = ffn_conv_w[0]
    for half in range(2):
        for fc in range(FC):
            cols = slice(half * DFF + fc * 128, half * DFF + (fc + 1) * 128)
            c4 = (half * FC + fc) * 4
            cwc[:, c4 + 0] = cw3[0, cols]
            cwc[:, c4 + 1] = cw3[1, cols]
            cwc[:, c4 + 2] = cw3[2, cols]
            cwc[:, c4 + 3] = ffn_conv_b[0, cols]
    nfw_col = np.ascontiguousarray(norm_ffn_w[0].reshape(KC, 128).T)
    nfin_b = np.ascontiguousarray(np.broadcast_to(norm_final_w[None, :], (128, D)))
    ident = np.eye(128, dtype=np.float32)
    for c in range(NCORES):
        rows = np.zeros((NG * TB, D), np.float32)
        for j in range(NG):
            gidx = c * NG + j
            p0 = N_META + gidx * OWN - 2
            p1 = min(p0 + TB, L)
            if p1 > p0:
                rows[j * TB:j * TB + (p1 - p0)] = hfull[p0:p1]
        KR = D // NCORES
        if use_cc:
            wi_c = np.ascontiguousarray(w_ffn_in[0][c * KR:(c + 1) * KR, :])
            wo_c = np.ascontiguousarray(w_ffn_out[0][:, c * KR:(c + 1) * KR])
        else:
            wi_c = np.ascontiguousarray(w_ffn_in[0])
            wo_c = np.ascontiguousarray(np.concatenate(
                [w_ffn_out[0][:, r * KR:(r + 1) * KR] for r in range(NCORES)], axis=0))
        maps.append({
            "xt": rows,
            "wi_sh": wi_c,
            "wo_sh": wo_c,
            "nfw_col": nfw_col, "nfin_b": nfin_b, "cw_col": cwc, "ident_in": ident,
        })
    return maps


def run_ffn(x, meta_tokens, norm_ffn_w, w_ffn_in, ffn_conv_w, ffn_conv_b, w_ffn_out, norm_final_w, use_cc=True):
    if meta_tokens is None:
        S, D = x.shape[0] - N_META, x.shape[1]
    else:
        S, D = x.shape[1], x.shape[2]
    DFF = w_ffn_out.shape[1]
    n_groups = -(-S // OWN)
    NG = -(-n_groups // NCORES)
    nc = build_program(D, DFF, NG, use_cc)
    maps = _host_layout(x, meta_tokens, norm_ffn_w, w_ffn_in, ffn_conv_w, ffn_conv_b, w_ffn_out,
                        norm_final_w, NG, use_cc)
    res = run_bass_kernel_spmd(nc, maps, core_ids=list(range(NCORES)))
    y = np.concatenate([res.results[c]["out"] for c in range(NCORES)], axis=0)[:S]
    return y[None].astype(np.float32)


A_COLS = 6592


def kernel(x, meta_tokens, norm_mix_w, w_in, mu_shift, rwkv_w0, rwkv_w2, rwkv_a0, rwkv_a2,
           rwkv_g2, rwkv_k_k, rwkv_k_a, rwkv_r_k, rwkv_ln_w, rwkv_ln_b, kv_norm_w, w_uk, w_uv,
           idx_ln_w, idx_ln_b, w_proj_a, w_proj_b, w_gate, w_out, norm_ffn_w, w_ffn_in,
           ffn_conv_w, ffn_conv_b, w_ffn_out, norm_final_w):
    f = lambda a: np.asarray(a, np.float32)
    hfull = np.concatenate([f(meta_tokens), f(x)[0]], axis=0)
    z, gates = run_mixin(hfull, f(norm_mix_w)[0], f(w_in)[0], f(w_gate)[0])
    yb = run_dsa(np.ascontiguousarray(z[:, A_COLS:]), f(kv_norm_w)[0], f(w_uk)[0], f(w_uv)[0],
                 f(idx_ln_w)[0], f(idx_ln_b)[0])
    P = run_rwkv_prep(np.ascontiguousarray(z[:, :A_COLS]), f(mu_shift)[0], f(rwkv_w0)[0], f(rwkv_w2)[0],
                      f(rwkv_a0)[0], f(rwkv_a2)[0], f(rwkv_g2)[0], f(rwkv_k_k)[0], f(rwkv_k_a)[0])
    yscan = run_rwkv_scan(P)
    ya = run_rwkv_post(yscan, P, f(rwkv_ln_w)[0], f(rwkv_ln_b)[0], f(rwkv_r_k)[0])
    del P, yscan
    h1 = run_mixout(hfull, ya, yb, gates, f(w_proj_a)[0], f(w_proj_b)[0], f(w_out)[0])
    return run_ffn(h1, None, f(norm_ffn_w), f(w_ffn_in), f(ffn_conv_w), f(ffn_conv_b), f(w_ffn_out),
                   f(norm_final_w), use_cc=False)
```
